# Optimizing a Trainium2 kernel written in Bass

```python
import jax, jax.numpy as jnp
from jax import lax
import numpy as np

D_MODEL = 1024
BATCH = 8
SEQ = 4096
DEPTH = 4

N_MEM = 256
HEAD_DIM = 64
N_MIX_HEADS = 12
MIX_WIDTH = N_MIX_HEADS * HEAD_DIM
N_MEM_HEADS = 4
MEM_WIDTH = N_MEM_HEADS * HEAD_DIM
D_FF = 2816
CHUNK = 128
BLOCK_Q = 128
ROPE_THETA = 10000.0
DILATED_GROUPS = ((128, 1), (512, 4), (2048, 16))
HEADS_PER_DIL_GROUP = N_MIX_HEADS // len(DILATED_GROUPS)
DIL_OUT_WIDTH = HEADS_PER_DIL_GROUP * HEAD_DIM
RMS_EPS = 1e-6
LN_EPS = 1e-5
NEG_INF = -1e30
ATTN_SCALE = HEAD_DIM ** -0.5
MAX_POS_OFFSET = 1024
N_NORMS = 6

N_A = (DEPTH + 2) // 3
N_B = (DEPTH + 1) // 3
N_C = DEPTH // 3
A_IN = 2 * MIX_WIDTH + MEM_WIDTH
A_OUT = MIX_WIDTH + MEM_WIDTH
B_IN = 3 * MIX_WIDTH + MEM_WIDTH
B_OUT = DIL_OUT_WIDTH + MEM_WIDTH
C_IN = 3 * MIX_WIDTH + N_MIX_HEADS + MEM_WIDTH
C_OUT = MIX_WIDTH + MEM_WIDTH

kernel_name = 'hybrid_interleaved_gmlp_dilated_fox_macaron'


def rms_norm(x, g):
    xf = x.astype(jnp.float32)
    y = xf * lax.rsqrt(jnp.mean(xf * xf, axis=-1, keepdims=True) + RMS_EPS)
    return (y * g.astype(jnp.float32)).astype(x.dtype)


def layer_norm(x, g):
    xf = x.astype(jnp.float32)
    mu = jnp.mean(xf, axis=-1, keepdims=True)
    var = jnp.mean(jnp.square(xf - mu), axis=-1, keepdims=True)
    return ((xf - mu) * lax.rsqrt(var + LN_EPS) * g.astype(jnp.float32)).astype(x.dtype)


def swiglu_ffn(x, w_gate_up, w_down):
    gate, up = jnp.split(x @ w_gate_up, 2, axis=-1)
    return (jax.nn.silu(gate) * up) @ w_down


def rope_tables(positions):
    inv_freq = ROPE_THETA ** (-jnp.arange(0, HEAD_DIM, 2, dtype=jnp.float32) / HEAD_DIM)
    ang = positions.astype(jnp.float32)[..., None] * inv_freq
    return jnp.cos(ang), jnp.sin(ang)


def apply_rope(t, cos, sin):
    c = cos[:, :, None, :].astype(t.dtype)
    s = sin[:, :, None, :].astype(t.dtype)
    t1, t2 = jnp.split(t, 2, axis=-1)
    return jnp.concatenate([t1 * c - t2 * s, t2 * c + t1 * s], axis=-1)


def memory_attention(q_mem, mem_n, w_mem_kv):
    B, S, _ = q_mem.shape
    k, v = jnp.split(mem_n @ w_mem_kv, 2, axis=-1)
    q = q_mem.reshape(B, S, N_MEM_HEADS, HEAD_DIM)
    k = k.reshape(B, -1, N_MEM_HEADS, HEAD_DIM)
    v = v.reshape(B, -1, N_MEM_HEADS, HEAD_DIM)
    s = jnp.einsum('bqhd,bkhd->bhqk', q, k).astype(jnp.float32) * ATTN_SCALE
    p = jax.nn.softmax(s, axis=-1).astype(v.dtype)
    return jnp.einsum('bhqk,bkhd->bqhd', p, v).reshape(B, S, MEM_WIDTH)


def mixer_a(h, mem_n, w_mem_kv, w_in, spatial_w, spatial_b, v_norm_g, w_out):
    B, S, _ = h.shape
    proj = h @ w_in
    uv, q_mem = proj[..., :2 * MIX_WIDTH], proj[..., 2 * MIX_WIDTH:]
    u, v = jnp.split(jax.nn.gelu(uv), 2, axis=-1)
    v = layer_norm(v, v_norm_g).reshape(B, S // CHUNK, CHUNK, N_MIX_HEADS, HEAD_DIM)
    w_causal = spatial_w * jnp.tril(jnp.ones((CHUNK, CHUNK), spatial_w.dtype))
    mixed = jnp.einsum('gts,bcsgd->bctgd', w_causal, v) + spatial_b.T[None, None, :, :, None]
    gated = u * mixed.reshape(B, S, MIX_WIDTH)
    y = jnp.concatenate([gated, memory_attention(q_mem, mem_n, w_mem_kv)], axis=-1)
    return y @ w_out


def dilated_window_attention(q, k, v, window, dilation):
    B, S, H, Dh = q.shape
    span = window // dilation
    L = S // dilation
    nb = -(-L // span)
    Lp = nb * span

    def to_blocks(t):
        t = t.reshape(B, L, dilation, H, Dh).transpose(0, 2, 1, 3, 4)
        t = jnp.pad(t, ((0, 0), (0, 0), (0, Lp - L), (0, 0), (0, 0)))
        return t.reshape(B, dilation, nb, span, H, Dh)

    def with_prev(t):
        prev = jnp.pad(t, ((0, 0), (0, 0), (1, 0), (0, 0), (0, 0), (0, 0)))[:, :, :-1]
        return jnp.concatenate([prev, t], axis=3)

    qb = to_blocks(q)
    kk = with_prev(to_blocks(k))
    vv = with_prev(to_blocks(v))
    s = jnp.einsum('brnqhd,brnkhd->brnhqk', qb, kk).astype(jnp.float32) * ATTN_SCALE
    qi = jnp.arange(span)[:, None] + span
    ki = jnp.arange(2 * span)[None, :]
    dist = qi - ki
    band = (dist >= 0) & (dist <= span)
    first = jnp.arange(nb)[:, None, None] == 0
    valid = band[None] & ~(first & (ki < span)[None])
    s = jnp.where(valid[None, None, :, None], s, NEG_INF)
    m = jnp.max(s, axis=-1, keepdims=True)
    p = jnp.exp(s - m)
    denom = jnp.sum(p, axis=-1, keepdims=True)
    out = jnp.einsum('brnhqk,brnkhd->brnqhd', (p / denom).astype(v.dtype), vv)
    lse = (m + jnp.log(denom))[..., 0]
    out = out.reshape(B, dilation, Lp, H, Dh)[:, :, :L].transpose(0, 2, 1, 3, 4).reshape(B, S, H, Dh)
    lse = lse.transpose(0, 1, 2, 4, 3).reshape(B, dilation, Lp, H)[:, :, :L]
    lse = lse.transpose(0, 2, 1, 3).reshape(B, S, H)
    return out, lse


def mixer_b(h, mem_n, w_mem_kv, cos, sin, w_in, w_out):
    B, S, _ = h.shape
    q, k, v, q_mem = jnp.split(h @ w_in, [MIX_WIDTH, 2 * MIX_WIDTH, 3 * MIX_WIDTH], axis=-1)
    q = apply_rope(q.reshape(B, S, N_MIX_HEADS, HEAD_DIM), cos, sin)
    k = apply_rope(k.reshape(B, S, N_MIX_HEADS, HEAD_DIM), cos, sin)
    v = v.reshape(B, S, N_MIX_HEADS, HEAD_DIM)
    outs, lses = [], []
    for g, (window, dilation) in enumerate(DILATED_GROUPS):
        hs = slice(g * HEADS_PER_DIL_GROUP, (g + 1) * HEADS_PER_DIL_GROUP)
        o, l = dilated_window_attention(q[:, :, hs], k[:, :, hs], v[:, :, hs], window, dilation)
        outs.append(o)
        lses.append(l)
    wts = jax.nn.softmax(jnp.stack(lses, axis=0), axis=0)
    merged = jnp.sum(wts[..., None].astype(v.dtype) * jnp.stack(outs, axis=0), axis=0)
    merged = merged.reshape(B, S, DIL_OUT_WIDTH)
    y = jnp.concatenate([merged, memory_attention(q_mem, mem_n, w_mem_kv)], axis=-1)
    return y @ w_out


def forgetting_attention(q, k, v, log_f):
    B, S, H, Dh = q.shape
    c = jnp.cumsum(log_f.astype(jnp.float32), axis=1).transpose(0, 2, 1)
    q_idx = jnp.arange(BLOCK_Q)
    outs = []
    for i in range(S // BLOCK_Q):
        lo, hi = i * BLOCK_Q, (i + 1) * BLOCK_Q
        s = jnp.einsum('bqhd,bkhd->bhqk', q[:, lo:hi], k[:, :hi]).astype(jnp.float32) * ATTN_SCALE
        s = s + c[:, :, lo:hi, None] - c[:, :, None, :hi]
        causal = (lo + q_idx)[:, None] >= jnp.arange(hi)[None, :]
        p = jax.nn.softmax(jnp.where(causal, s, NEG_INF), axis=-1).astype(v.dtype)
        outs.append(jnp.einsum('bhqk,bkhd->bqhd', p, v[:, :hi]))
    return jnp.concatenate(outs, axis=1)


def mixer_c(h, mem_n, w_mem_kv, w_in, forget_bias, w_out):
    B, S, _ = h.shape
    q, k, v, f_logit, q_mem = jnp.split(
        h @ w_in, [MIX_WIDTH, 2 * MIX_WIDTH, 3 * MIX_WIDTH, 3 * MIX_WIDTH + N_MIX_HEADS], axis=-1)
    log_f = jax.nn.log_sigmoid(f_logit.astype(jnp.float32) + forget_bias.astype(jnp.float32))
    heads = lambda t: t.reshape(B, S, N_MIX_HEADS, HEAD_DIM)
    att = forgetting_attention(heads(q), heads(k), heads(v), log_f).reshape(B, S, MIX_WIDTH)
    y = jnp.concatenate([att, memory_attention(q_mem, mem_n, w_mem_kv)], axis=-1)
    return y @ w_out


def setup_inputs(seed: int = 0) -> dict:
    key = jax.random.key(seed)
    ks = jax.random.split(key, 20)
    f32 = jnp.float32

    def dense(k, shape, fan_in):
        return jax.random.normal(k, shape, f32) * fan_in ** -0.5

    def gain(k, shape):
        return 1.0 + 0.02 * jax.random.normal(k, shape, f32)

    x = jax.random.normal(ks[0], (BATCH, SEQ, D_MODEL), f32)
    mem = jax.random.normal(ks[1], (BATCH, N_MEM, D_MODEL), f32)
    offset = jax.random.randint(ks[2], (BATCH, 1), 0, MAX_POS_OFFSET, dtype=jnp.int32)
    positions = (offset + jnp.arange(SEQ, dtype=jnp.int32)[None, :]).astype(jnp.int32)
    return {
        'x': x,
        'mem': mem,
        'positions': positions,
        'norm_g': gain(ks[3], (DEPTH, N_NORMS, D_MODEL)),
        'mem_norm_g': gain(ks[4], (DEPTH, D_MODEL)),
        'w_mem_kv': dense(ks[5], (DEPTH, D_MODEL, 2 * MEM_WIDTH), D_MODEL),
        'ffn_w_gate_up': dense(ks[6], (DEPTH, 2, D_MODEL, 2 * D_FF), D_MODEL),
        'ffn_w_down': dense(ks[7], (DEPTH, 2, D_FF, D_MODEL), D_FF),
        'a_w_in': dense(ks[8], (N_A, D_MODEL, A_IN), D_MODEL),
        'a_spatial_w': dense(ks[9], (N_A, N_MIX_HEADS, CHUNK, CHUNK), CHUNK),
        'a_spatial_b': gain(ks[10], (N_A, N_MIX_HEADS, CHUNK)),
        'a_v_norm_g': gain(ks[11], (N_A, MIX_WIDTH)),
        'a_w_out': dense(ks[12], (N_A, A_OUT, D_MODEL), A_OUT),
        'b_w_in': dense(ks[13], (N_B, D_MODEL, B_IN), D_MODEL),
        'b_w_out': dense(ks[14], (N_B, B_OUT, D_MODEL), B_OUT),
        'c_w_in': dense(ks[15], (N_C, D_MODEL, C_IN), D_MODEL),
        'c_forget_bias': jax.random.uniform(ks[16], (N_C, N_MIX_HEADS), f32, 1.0, 5.0),
        'c_w_out': dense(ks[17], (N_C, C_OUT, D_MODEL), C_OUT),
    }


def reference(x, mem, positions, norm_g, mem_norm_g, w_mem_kv, ffn_w_gate_up, ffn_w_down,
              a_w_in, a_spatial_w, a_spatial_b, a_v_norm_g, a_w_out,
              b_w_in, b_w_out, c_w_in, c_forget_bias, c_w_out):
    cos, sin = rope_tables(positions)
    for i in range(DEPTH):
        kind, j = i % 3, i // 3
        g = norm_g[i]
        x = x + 0.5 * rms_norm(swiglu_ffn(rms_norm(x, g[0]), ffn_w_gate_up[i, 0], ffn_w_down[i, 0]), g[1])
        h = rms_norm(x, g[2])
        mem_n = rms_norm(mem, mem_norm_g[i])
        if kind == 0:
            y = mixer_a(h, mem_n, w_mem_kv[i], a_w_in[j], a_spatial_w[j], a_spatial_b[j],
                        a_v_norm_g[j], a_w_out[j])
        elif kind == 1:
            y = mixer_b(h, mem_n, w_mem_kv[i], cos, sin, b_w_in[j], b_w_out[j])
        else:
            y = mixer_c(h, mem_n, w_mem_kv[i], c_w_in[j], c_forget_bias[j], c_w_out[j])
        x = x + rms_norm(y, g[3])
        x = x + 0.5 * rms_norm(swiglu_ffn(rms_norm(x, g[4]), ffn_w_gate_up[i, 1], ffn_w_down[i, 1]), g[5])
    return x
```

```python
import numpy as np
from contextlib import ExitStack
import concourse.bass as bass
import concourse.mybir as mybir
from concourse.bass_utils import run_bass_kernel_spmd

F32 = mybir.dt.float32
BF16 = mybir.dt.bfloat16
I32 = mybir.dt.int32
AF = mybir.ActivationFunctionType
ALU = mybir.AluOpType
AP = bass.AP

D = 1024
SEQ = 4096
NT = SEQ // 128
DFF = 2816
NKF = DFF // 128
NMEM = 256
DEPTH = 4
EPS = 1e-6
LN_EPS = 1e-5
SCALE = 0.125
NEG = -1e30

C_ID, C_SEL, C_TRI, C_MB, C_MC, C_IF, C_NH, C_RM, NCONST = 0, 128, 192, 320, 576, 2624, 2625, 2628, 2756


def make_consts():
    c = np.zeros((128, NCONST), np.float32)
    c[:, C_ID:C_ID + 128] = np.eye(128, dtype=np.float32)
    c[64, C_SEL:C_SEL + 64] = 1.0
    s = np.arange(128)[:, None]
    t = np.arange(128)[None, :]
    c[:, C_TRI:C_TRI + 128] = (t >= s).astype(np.float32)
    mb = np.zeros((128, 256), np.float32)
    mb[:, :128] = np.where(t >= s, 0.0, NEG)
    mb[:, 128:] = np.where(s >= t, 0.0, NEG)
    c[:, C_MB:C_MB + 256] = mb
    tt = np.arange(512)[None, :]
    for j in range(4):
        c[:, C_MC + j * 512:C_MC + (j + 1) * 512] = np.where(tt >= j * 128 + s, 0.0, NEG)
    invf = (np.float32(10000.0) ** (-np.arange(0, 64, 2, dtype=np.float32) / np.float32(64))).astype(np.float32)
    c[:, C_IF] = invf[np.arange(128) % 32]
    c[:, C_NH] = -0.5
    for m in range(128):
        if m % 64 < 32:
            c[m + 32, C_RM + m] = -1.0
        else:
            c[m - 32, C_RM + m] = 1.0
    return c


class Buf:
    __slots__ = ("t", "w", "rs", "name", "dsem", "dcnt", "excl")

    def __init__(self, t=None, name="", excl=False):
        self.t = t
        self.w = None
        self.rs = []
        self.name = name
        self.dsem = None
        self.dcnt = 0
        self.excl = excl

    def __getitem__(self, idx):
        return self.t[idx]


class Sched:
    def __init__(self, nc):
        self.nc = nc
        self.e = dict(pe=nc.tensor, act=nc.scalar, dve=nc.vector, pool=nc.gpsimd, sp=nc.sync)
        self.sem = {k: nc.alloc_semaphore("prog_" + k) for k in self.e}
        self.cnt = {k: 0 for k in self.e}
        self.seen = {k: {} for k in self.e}
        self.pend = {k: False for k in self.e}
        self.dma_pending = {}
        self.pool_sems = []
        self.nsem = 0

    def _wait(self, eng, tok, raw=True):
        if tok is None:
            return
        sem, val = tok
        key = id(sem)
        if self.seen[eng].get(key, 0) >= val:
            return
        if sem is self.sem[eng]:
            if eng == "pe" or eng == "sp" or (not raw) or val > self.cnt[eng]:
                return
        self.e[eng].wait_ge(sem, val)
        self.seen[eng][key] = val

    def _deps(self, eng, reads, writes):
        for b in reads:
            self._wait(eng, b.w, True)
            if b.excl:
                for r in b.rs:
                    self._wait(eng, r, False)
        for b in writes:
            self._wait(eng, b.w, False)
            for r in b.rs:
                self._wait(eng, r, False)

    def _post(self, tok, reads, writes):
        for b in reads:
            if b.excl:
                b.w = tok
                b.rs = []
            else:
                b.rs.append(tok)
                if len(b.rs) > 16:
                    best = {}
                    for sem, val in b.rs:
                        k = id(sem)
                        if k not in best or best[k][1] < val:
                            best[k] = (sem, val)
                    b.rs = list(best.values())
        for b in writes:
            b.w = tok
            b.rs = []

    def op(self, eng, fn, reads=(), writes=(), inc=True):
        self._deps(eng, reads, writes)
        tok = (self.sem[eng], self.cnt[eng] + 1)
        ins = fn(self.e[eng])
        if inc:
            ins.then_inc(self.sem[eng], 1)
            self.cnt[eng] += 1
            self.pend[eng] = False
        else:
            self.pend[eng] = True
        self._post(tok, reads, writes)
        return tok

    def _slot_sem(self, slot):
        if slot.dsem is None:
            if self.pool_sems:
                slot.dsem, slot.dcnt = self.pool_sems.pop()
            else:
                slot.dsem = self.nc.alloc_semaphore("dm%d" % self.nsem)
                slot.dcnt = 0
                self.nsem += 1

    def dma(self, eng, out_ap, in_ap, reads=(), writes=(), slot=None, **kw):
        self._slot_sem(slot)
        saved = []
        for b in writes:
            if b.w is not None and b.w[0] is slot.dsem and not b.rs:
                saved.append((b, b.w))
                b.w = None
        self._deps(eng, reads, writes)
        for b, w_ in saved:
            b.w = w_
        slot.dcnt += 16
        tok = (slot.dsem, slot.dcnt)
        self.e[eng].dma_start(out=out_ap, in_=in_ap, **kw).then_inc(slot.dsem, 16)
        self.dma_pending[id(slot.dsem)] = tok
        self._post(tok, reads, writes)
        return tok

    def release(self, slot):
        if slot.dsem is not None:
            self.pool_sems.append((slot.dsem, slot.dcnt))
            slot.dsem = None

    def barrier(self):
        for tok in self.dma_pending.values():
            self._wait("sp", tok)
        self.dma_pending = {}
        for k in self.e:
            if k != "sp" and self.cnt[k] > 0:
                assert not self.pend[k], k
                self._wait("sp", (self.sem[k], self.cnt[k]))
        ins = self.e["sp"].nop()
        ins.then_inc(self.sem["sp"], 1)
        self.cnt["sp"] += 1
        tok = (self.sem["sp"], self.cnt["sp"])
        for k in self.e:
            if k != "sp":
                self._wait(k, tok)
        return tok


class Phase:
    def __init__(self, K, name):
        self.K = K
        self.name = name
        self.es = ExitStack()
        self.bufs = []

    def __enter__(self):
        return self

    def sb(self, shape, dtype, tag="b"):
        K = self.K
        K.uid += 1
        t = self.es.enter_context(K.nc.sbuf_tensor("%s_%s_%d" % (self.name, tag, K.uid), list(shape), dtype))
        b = Buf(t, tag)
        self.bufs.append(b)
        return b

    def ps(self, shape, dtype, tag="p"):
        K = self.K
        K.uid += 1
        t = self.es.enter_context(K.nc.psum_tensor("%s_%s_%d" % (self.name, tag, K.uid), list(shape), dtype))
        b = Buf(t, tag, excl=True)
        self.bufs.append(b)
        return b

    def __exit__(self, *a):
        self.K.S.barrier()
        for b in self.bufs:
            self.K.S.release(b)
        self.es.close()
        return False


class K:
    pass


def dram_in(nc, name, shape, dtype=F32):
    return nc.dram_tensor(name, list(shape), dtype, kind="ExternalInput").ap()


DBG_OUT = set()


def dram_tmp(nc, name, shape, dtype):
    kind = "ExternalOutput" if name in DBG_OUT else "Internal"
    return nc.dram_tensor(name, list(shape), dtype, kind=kind).ap()


def bc_rows(ap2d_row, nparts=128):
    n = ap2d_row.shape[-1]
    return AP(ap2d_row.tensor, ap2d_row.offset, [[0, nparts], [1, n]])


def build(n_layers=DEPTH, dbg=None):
    nc = bass.Bass("TRN2", target_bir_lowering=False)
    k = K()
    k.nc = nc
    k.uid = 0
    I = {}
    I["x"] = dram_in(nc, "x", [SEQ, D])
    I["mem"] = dram_in(nc, "mem", [NMEM, D])
    I["positions"] = dram_in(nc, "positions", [1, SEQ], I32)
    I["norm_g"] = dram_in(nc, "norm_g", [DEPTH, 6, D])
    I["mem_norm_g"] = dram_in(nc, "mem_norm_g", [DEPTH, D])
    I["w_mem_kv"] = dram_in(nc, "w_mem_kv", [DEPTH, D, 512])
    I["ffn_w_gate_up"] = dram_in(nc, "ffn_w_gate_up", [DEPTH, 2, D, 2 * DFF])
    I["ffn_w_down"] = dram_in(nc, "ffn_w_down", [DEPTH, 2, DFF, D])
    I["a_w_in"] = dram_in(nc, "a_w_in", [2, D, 1792])
    I["a_spatial_w"] = dram_in(nc, "a_spatial_w", [2, 12, 128, 128])
    I["a_spatial_b"] = dram_in(nc, "a_spatial_b", [2, 12, 128])
    I["a_v_norm_g"] = dram_in(nc, "a_v_norm_g", [2, 768])
    I["a_w_out"] = dram_in(nc, "a_w_out", [2, 1024, D])
    I["b_w_in"] = dram_in(nc, "b_w_in", [1, D, 2560])
    I["b_w_out"] = dram_in(nc, "b_w_out", [1, 512, D])
    I["c_w_in"] = dram_in(nc, "c_w_in", [1, D, 2572])
    I["c_forget_bias"] = dram_in(nc, "c_forget_bias", [1, 12])
    I["c_w_out"] = dram_in(nc, "c_w_out", [1, 1024, D])
    I["consts"] = dram_in(nc, "consts", [128, NCONST])
    out_d = nc.dram_tensor("out", [SEQ, D], F32, kind="ExternalOutput").ap()
    k.I = I

    XS = dram_tmp(nc, "XS", [SEQ, D], F32)
    QT = dram_tmp(nc, "QT", [768, SEQ], BF16)
    KT = dram_tmp(nc, "KT", [768, SEQ], BF16)
    VV = dram_tmp(nc, "VV", [SEQ, 768], BF16)
    QM = dram_tmp(nc, "QM", [256, SEQ], BF16)
    ATT = dram_tmp(nc, "ATT", [1024, SEQ], BF16)
    CT = dram_tmp(nc, "CT", [12, SEQ], F32)
    k.XS, k.QT, k.KT, k.VV, k.QM, k.ATT, k.CT = XS, QT, KT, VV, QM, ATT, CT
    k.xs_t = [Buf(None, "xs%d" % i) for i in range(NT)]
    k.dQT, k.dKT, k.dVV, k.dQM, k.dATT, k.dCT = (Buf(None, n) for n in ("dQT", "dKT", "dVV", "dQM", "dATT", "dCT"))
    k.dIN = Buf(None, "din")

    ctx = nc.cleanup_on_exit()
    ctx.__enter__()
    S = Sched(nc)
    k.S = S

    kinds = [0, 1, 2, 0]
    k.W = []
    k.conv_jobs = []

    def add_weight(name, src_ap, rows, cols, li):
        dst = dram_tmp(nc, name, [rows, cols], BF16)
        b = Buf(None, name)
        r0 = 0
        while r0 < rows:
            r1 = min(rows, r0 + 64)
            k.conv_jobs.append((li, dst[r0:r1, :], src_ap[r0:r1, :], b))
            r0 = r1
        return (dst, b)

    for li in range(n_layers):
        kind, j = kinds[li], li // 3
        w = {}
        w["wgu0"] = add_weight("wgu%d0" % li, I["ffn_w_gate_up"][li, 0], D, 2 * DFF, li)
        w["wd0"] = add_weight("wd%d0" % li, I["ffn_w_down"][li, 0], DFF, D, li)
        w["wkv"] = add_weight("wkv%d" % li, I["w_mem_kv"][li], D, 512, li)
        if kind == 0:
            w["win"] = add_weight("win%d" % li, I["a_w_in"][j], D, 1792, li)
            w["wout"] = add_weight("wout%d" % li, I["a_w_out"][j], 1024, D, li)
        elif kind == 1:
            w["win"] = add_weight("win%d" % li, I["b_w_in"][0], D, 2560, li)
            w["wout"] = add_weight("wout%d" % li, I["b_w_out"][0], 512, D, li)
        else:
            w["win"] = add_weight("win%d" % li, I["c_w_in"][0], D, 2572, li)
            w["wout"] = add_weight("wout%d" % li, I["c_w_out"][0], 1024, D, li)
        w["wgu1"] = add_weight("wgu%d1" % li, I["ffn_w_gate_up"][li, 1], D, 2 * DFF, li)
        w["wd1"] = add_weight("wd%d1" % li, I["ffn_w_down"][li, 1], DFF, D, li)
        k.W.append(w)
    k.conv_pos = 0

    def conv_pump(n=1, upto_layer=None):
        while n > 0 and k.conv_pos < len(k.conv_jobs):
            li, dst, src, b = k.conv_jobs[k.conv_pos]
            if upto_layer is not None and li > upto_layer:
                return
            S.dma("pool", dst, src, reads=[k.dIN], writes=[b], slot=b, max_dma_last_dim=4096)
            k.conv_pos += 1
            n -= 1

    def conv_flush(layer):
        while k.conv_pos < len(k.conv_jobs) and k.conv_jobs[k.conv_pos][0] <= layer:
            conv_pump(1)

    k.conv_pump = conv_pump

    gst = ExitStack()
    cst = Buf(gst.enter_context(nc.sbuf_tensor("cst", [128, NCONST], F32)), "cst")
    idb = Buf(gst.enter_context(nc.sbuf_tensor("idb", [128, 128], BF16)), "idb")
    S.dma("sp", cst[:], I["consts"], reads=[k.dIN], writes=[cst], slot=cst)
    S.op("dve", lambda e: e.tensor_copy(out=idb[:], in_=cst[:, C_ID:C_ID + 128]), reads=[cst], writes=[idb])
    wrm = Buf(gst.enter_context(nc.sbuf_tensor("wrm", [128, 512], BF16)), "wrm")
    S.op("pool", lambda e: e.memset(wrm[:], 1.0), writes=[wrm])
    k.cst, k.idb = cst, idb
    S.barrier()

    def warm(bank, n=16):
        for i in range(n):
            S.op("pe", lambda e: e.matmul(bank[:, 0:512], lhsT=idb[:], rhs=wrm[:], start=True, stop=True),
                 reads=[idb, wrm], writes=[bank], inc=(i == n - 1))

    conv_flush(0)

    def load_gain(ph, row_ap, tag):
        g = ph.sb([128, D], F32, tag)
        S.dma("sp", g[:], bc_rows(row_ap), reads=[k.dIN], writes=[g], slot=g)
        return g

    def rms_stats(ss_ap, ms_ap, r_ap, st, n, eps):
        S.op("dve", lambda e: e.tensor_scalar(out=ms_ap, in0=ss_ap, scalar1=1.0 / n, scalar2=eps,
                                              op0=ALU.mult, op1=ALU.add), reads=[st], writes=[st])
        S.op("pool", lambda e: e.tensor_tensor(out=r_ap, in0=ms_ap, in1=cst[:, C_NH:C_NH + 1], op=ALU.pow),
             reads=[st, cst], writes=[st])

    def prep_a(x_ap, x_db, xt, g, hb, junk, st):
        if x_ap is not None:
            S.dma("sp", xt[:], x_ap, reads=[x_db], writes=[xt], slot=xt)
        S.op("act", lambda e: e.activation(out=junk[:], in_=xt[:], func=AF.Square, accum_out=st[:, 0:1]),
             reads=[xt], writes=[junk, st])
        rms_stats(st[:, 0:1], st[:, 1:2], st[:, 2:3], st, D, EPS)
        S.op("dve", lambda e: e.scalar_tensor_tensor(out=hb[:], in0=xt[:], scalar=st[:, 2:3], in1=g[:],
                                                     op0=ALU.mult, op1=ALU.mult), reads=[xt, st, g], writes=[hb])

    def prep_b(hb, trb, hT, dst_ap, evac):
        for kk in range(8):
            S.op("pe", lambda e: e.transpose(out=trb[:, kk, :], in_=hb[:, kk * 128:(kk + 1) * 128], identity=idb[:]),
                 reads=[hb, idb], writes=[trb], inc=(kk == 7))
        if evac == "act":
            S.op("act", lambda e: e.copy(out=dst_ap, in_=trb[:]), reads=[trb], writes=[hT])
        else:
            S.op("dve", lambda e: e.tensor_copy(out=dst_ap, in_=trb[:]), reads=[trb], writes=[hT])

    def load_w(ph, wt, rows, cols, tag):
        dst, b = wt
        nk = rows // 128
        t = ph.sb([128, nk, cols], BF16, tag)
        S.dma("sp", t[:], dst.rearrange("(k p) n -> p k n", p=128), reads=[b], writes=[t], slot=t)
        return t

    class Epi:
        def __init__(self, ph, gpost, factor, src_ap, src_bufs, dst_ap, dst_bufs, nslot=2, pump=0):
            self.ph = ph
            self.pump = pump
            self.ns = nslot
            self.g = gpost
            if factor != 1.0:
                S.op("pool", lambda e: e.tensor_scalar(out=gpost[:], in0=gpost[:], scalar1=float(factor), scalar2=None,
                                                       op0=ALU.mult), reads=[gpost], writes=[gpost])
            self.xe = [ph.sb([128, D], F32, "xe") for _ in range(nslot)]
            self.ysb = [ph.sb([128, D], F32, "ysb") for _ in range(nslot)]
            self.st = [ph.sb([128, 8], F32, "est") for _ in range(nslot)]
            self.junk = ph.sb([128, 512], BF16, "ejunk")
            self.src_ap, self.src_bufs, self.dst_ap, self.dst_bufs = src_ap, src_bufs, dst_ap, dst_bufs
            self.loaded = 0

        def prefetch(self, upto):
            while self.loaded <= min(upto, NT - 1):
                T = self.loaded
                xe = self.xe[T % self.ns]
                S.dma("sp", xe[:], self.src_ap[T * 128:(T + 1) * 128, :], reads=[self.src_bufs[T]], writes=[xe], slot=xe)
                self.loaded += 1

        def half(self, T, hf, bank):
            st, ysb = self.st[T % self.ns], self.ysb[T % self.ns]
            S.op("act", lambda e: e.activation(out=self.junk[:], in_=bank[:], func=AF.Square, accum_out=st[:, hf:hf + 1]),
                 reads=[bank], writes=[self.junk, st])
            S.op("dve", lambda e: e.tensor_copy(out=ysb[:, hf * 512:(hf + 1) * 512], in_=bank[:]), reads=[bank], writes=[ysb])

        def finish(self, T):
            st = self.st[T % self.ns]
            S.op("dve", lambda e: e.tensor_tensor(out=st[:, 2:3], in0=st[:, 0:1], in1=st[:, 1:2], op=ALU.add),
                 reads=[st], writes=[st])
            rms_stats(st[:, 2:3], st[:, 3:4], st[:, 4:5], st, D, EPS)
            if T >= 1:
                self.apply(T - 1)
            self.prefetch(T - 1 + self.ns)

        def apply(self, T):
            st, ysb, xe = self.st[T % self.ns], self.ysb[T % self.ns], self.xe[T % self.ns]
            S.op("dve", lambda e: e.scalar_tensor_tensor(out=ysb[:], in0=ysb[:], scalar=st[:, 4:5], in1=self.g[:],
                                                         op0=ALU.mult, op1=ALU.mult), reads=[ysb, st, self.g], writes=[ysb])
            S.op("pool", lambda e: e.tensor_tensor(out=xe[:], in0=ysb[:], in1=xe[:], op=ALU.add),
                 reads=[ysb, xe], writes=[xe])
            S.dma("pool", self.dst_ap[T * 128:(T + 1) * 128, :], xe[:], reads=[xe], writes=[self.dst_bufs[T]], slot=xe)
            if self.pump:
                conv_pump(self.pump)

        def flush(self):
            self.apply(NT - 1)

    def ffn_phase(li, fi, src_ap, src_bufs, dst_ap, dst_bufs):
        w = k.W[li]
        wgu_d, wgu_b = w["wgu%d" % fi]
        with Phase(k, "f%d%d" % (li, fi)) as ph:
            gi = 0 if fi == 0 else 4
            gpre = load_gain(ph, I["norm_g"][li, gi:gi + 1, :], "gpre")
            gpost = load_gain(ph, I["norm_g"][li, gi + 1:gi + 2, :], "gpost")
            epi = Epi(ph, gpost, 0.5, src_ap, src_bufs, dst_ap, dst_bufs, pump=3)
            hT = [ph.sb([128, 8, 1024], BF16, "hT") for _ in range(2)]
            act = ph.sb([128, NKF, 1024], BF16, "act")
            wd = ph.sb([128, NKF, 1024], BF16, "wd")
            wg = [ph.sb([128, 8, 512], BF16, "wg") for _ in range(3)]
            xt = [ph.sb([128, D], F32, "xt") for _ in range(3)]
            hb = [ph.sb([128, D], BF16, "hb") for _ in range(2)]
            junk = ph.sb([128, D], BF16, "junk")
            sg = [ph.sb([128, 512], F32, "sg") for _ in range(2)]
            xl = [0]

            def ensure_x(upto):
                while xl[0] <= min(upto, NT - 1):
                    T = xl[0]
                    S.dma("sp", xt[T % 3][:], src_ap[T * 128:(T + 1) * 128, :], reads=[src_bufs[T]], writes=[xt[T % 3]], slot=xt[T % 3])
                    xl[0] += 1

            def prep_c(T):
                ensure_x(T)
                prep_a(None, None, xt[T % 3], gpre, hb[T % 2], junk, st[T % 4])
            st = [ph.sb([128, 4], F32, "st") for _ in range(4)]
            pg = [ph.ps([128, 512], F32, "pg") for _ in range(2)]
            pu = [ph.ps([128, 512], F32, "pu") for _ in range(2)]
            pd = [ph.ps([128, 512], F32, "pd") for _ in range(2)]
            ptr = [ph.ps([128, 8, 128], BF16, "ptr") for _ in range(2)]
            NG = 4
            NJG = 11
            wsrc = wgu_d.rearrange("(k p) n -> p k n", p=128)
            nload = [0]

            def wg_load(upto):
                while nload[0] <= min(upto, NG * NJG - 1):
                    idx = nload[0]
                    jg = idx % NJG
                    b = wg[idx % 3]
                    S.dma("sp", b[:, :, 0:256], wsrc[:, :, jg * 256:(jg + 1) * 256], reads=[wgu_b], writes=[b], slot=b)
                    S.dma("sp", b[:, :, 256:512], wsrc[:, :, DFF + jg * 256:DFF + (jg + 1) * 256], reads=[wgu_b], writes=[b], slot=b)
                    nload[0] += 1

            def prepA(G):
                for t in range(8):
                    T = G * 8 + t
                    ensure_x(T + 1)
                    prep_c(T)
                    if G == 0:
                        prepB_tile(G, t)

            def prepB_tile(G, t):
                T = G * 8 + t
                prep_b(hb[T % 2], ptr[T % 2], hT[G % 2], hT[G % 2][:, :, t * 128:(t + 1) * 128], "act" if t % 2 == 0 else "dve")

            wg_load(2)
            prepA(0)
            cnt = [0]
            for G in range(NG):
                wdsrc = w["wd%d" % fi][0].rearrange("(k p) n -> p k n", p=128)
                for jg in range(NJG):
                    idx = G * NJG + jg
                    if G + 1 < NG and jg < 8:
                        ensure_x((G + 1) * 8 + jg + 1)
                    wg_load(idx + 2)
                    S.dma("sp", wd[:, 2 * jg:2 * jg + 2, :], wdsrc[:, 2 * jg:2 * jg + 2, :], reads=[w["wd%d" % fi][1]],
                          writes=[wd], slot=wd)
                    wb = wg[idx % 3]
                    if G + 1 < NG and jg < 8:
                        prep_c((G + 1) * 8 + jg)
                    for jj in range(2):
                        j = jg * 2 + jj
                        for hf in range(2):
                            s = cnt[0] % 2
                            cnt[0] += 1
                            for kk in range(8):
                                S.op("pe", lambda e: e.matmul(pg[s][:], lhsT=wb[:, kk, jj * 128:(jj + 1) * 128],
                                                              rhs=hT[G % 2][:, kk, hf * 512:(hf + 1) * 512],
                                                              start=(kk == 0), stop=(kk == 7)),
                                     reads=[wb, hT[G % 2]], writes=[pg[s]], inc=(kk == 7))
                            for kk in range(8):
                                S.op("pe", lambda e: e.matmul(pu[s][:], lhsT=wb[:, kk, 256 + jj * 128:256 + (jj + 1) * 128],
                                                              rhs=hT[G % 2][:, kk, hf * 512:(hf + 1) * 512],
                                                              start=(kk == 0), stop=(kk == 7)),
                                     reads=[wb, hT[G % 2]], writes=[pu[s]], inc=(kk == 7))
                            S.op("act", lambda e: e.activation(out=sg[s][:], in_=pg[s][:], func=AF.Silu),
                                 reads=[pg[s]], writes=[sg[s]])
                            S.op("dve", lambda e: e.tensor_tensor(out=act[:, j, hf * 512:(hf + 1) * 512], in0=sg[s][:],
                                                                  in1=pu[s][:], op=ALU.mult),
                                 reads=[sg[s], pu[s]], writes=[act])
                    if G + 1 < NG and jg < 8:
                        prepB_tile(G + 1, jg)
                if G == 0:
                    epi.prefetch(1)
                for t in range(8):
                    T = G * 8 + t
                    for hf in range(2):
                        for kk in range(NKF):
                            S.op("pe", lambda e: e.matmul(pd[hf][:], lhsT=act[:, kk, t * 128:(t + 1) * 128],
                                                          rhs=wd[:, kk, hf * 512:(hf + 1) * 512],
                                                          start=(kk == 0), stop=(kk == NKF - 1)),
                                 reads=[act, wd], writes=[pd[hf]], inc=(kk == NKF - 1))
                        epi.half(T, hf, pd[hf])
                    epi.finish(T)
            epi.flush()

    def mixout_phase(li, nch):
        w = k.W[li]
        with Phase(k, "o%d" % li) as ph:
            g3 = load_gain(ph, I["norm_g"][li, 3:4, :], "g3")
            epi = Epi(ph, g3, 1.0, XS, k.xs_t, XS, k.xs_t, nslot=4)
            wo = load_w(ph, w["wout"], nch * 128, D, "wo")
            at = [ph.sb([128, nch, 512], BF16, "at") for _ in range(2)]
            pd = [ph.ps([128, 512], F32, "pd") for _ in range(4)]
            asrc = ATT.rearrange("(c p) t -> p c t", p=128)
            epi.prefetch(3)
            warm(pd[0])
            def at_load(G):
                S.dma("sp", at[G % 2][:], asrc[:, 0:nch, G * 512:(G + 1) * 512], reads=[k.dATT], writes=[at[G % 2]], slot=at[G % 2])

            at_load(0)
            for G in range(8):
                a = at[G % 2]
                if G + 1 < 8:
                    at_load(G + 1)
                for t in range(4):
                    T = G * 4 + t
                    for hf in range(2):
                        bank = pd[(T % 2) * 2 + hf]
                        for c in range(nch):
                            S.op("pe", lambda e: e.matmul(bank[:], lhsT=a[:, c, t * 128:(t + 1) * 128],
                                                          rhs=wo[:, c, hf * 512:(hf + 1) * 512],
                                                          start=(c == 0), stop=(c == nch - 1)),
                                 reads=[a, wo], writes=[bank], inc=(c == nch - 1))
                        epi.half(T, hf, bank)
                    epi.finish(T)
            epi.flush()

    def mem_prep(ph, li):
        w = k.W[li]
        gm = load_gain(ph, I["mem_norm_g"][li:li + 1, :], "gm")
        wkv = load_w(ph, w["wkv"], D, 512, "wkv")
        memT = ph.sb([128, 8, 256], BF16, "memT")
        KmT = ph.sb([128, 2, 256], BF16, "KmT")
        Vm = ph.sb([128, 2, 4, 128], BF16, "Vm")
        xt = [ph.sb([128, D], F32, "mxt") for _ in range(2)]
        hb = [ph.sb([128, D], BF16, "mhb") for _ in range(2)]
        junk = ph.sb([128, D], BF16, "mjunk")
        st = [ph.sb([128, 4], F32, "mst") for _ in range(2)]
        ptr = ph.ps([128, 8, 128], BF16, "mptr")
        pk = ph.ps([128, 512], F32, "mpk")
        for t in range(2):
            prep_a(I["mem"][t * 128:(t + 1) * 128, :], k.dIN, xt[t], gm, hb[t], junk, st[t])
            prep_b(hb[t], ptr, memT, memT[:, :, t * 128:(t + 1) * 128], "act")
        S.op("pool", lambda e: e.memset(Vm[:], 1.0), writes=[Vm])
        for c in range(2):
            for kk in range(8):
                S.op("pe", lambda e: e.matmul(pk[:, 0:256], lhsT=wkv[:, kk, c * 128:(c + 1) * 128], rhs=memT[:, kk, :],
                                              start=(kk == 0), stop=(kk == 7)), reads=[wkv, memT], writes=[pk], inc=(kk == 7))
            S.op("act", lambda e: e.copy(out=KmT[:, c, :], in_=pk[:, 0:256]), reads=[pk], writes=[KmT])
        for kc in range(2):
            for kk in range(8):
                S.op("pe", lambda e: e.matmul(pk[:, 0:256], lhsT=memT[:, kk, kc * 128:(kc + 1) * 128], rhs=wkv[:, kk, 256:512],
                                              start=(kk == 0), stop=(kk == 7)), reads=[wkv, memT], writes=[pk], inc=(kk == 7))
            S.op("dve", lambda e: e.tensor_copy(out=Vm[:, kc, :, 0:64], in_=pk[:, 0:256].rearrange("p (h d) -> p h d", h=4)),
                 reads=[pk], writes=[Vm])
        return KmT, Vm, pk

    class Normalizer:
        def __init__(self, ph, n=512, with_pz=False):
            self.rec = [ph.sb([128, n], F32, "rec") for _ in range(2)]
            self.stg = [ph.sb([64, n], BF16, "stg") for _ in range(2)]
            self.pz = ph.ps([64, n], F32, "pz") if with_pz else None
            self.i = 0
            self.n = n

        def _fin(self, num_ap, den_ap, reads, row0, tok0, n):
            i = self.i
            self.i += 1
            rec, stg = self.rec[i % 2], self.stg[i % 2]
            S.op("act", lambda e: e.activation(out=rec[64:128, 0:n], in_=den_ap, func=AF.Ln), reads=reads, writes=[rec])
            S.op("act", lambda e: e.activation(out=rec[64:128, 0:n], in_=rec[64:128, 0:n], func=AF.Exp, scale=-1.0),
                 reads=[rec], writes=[rec])
            return rec, stg

        def from_psum(self, po, row0, tok0, n=None):
            n = n or self.n
            rec, stg = self._fin(None, po[64:128, 0:n], [po], row0, tok0, n)
            S.op("dve", lambda e: e.tensor_tensor(out=stg[:, 0:n], in0=po[0:64, 0:n], in1=rec[64:128, 0:n], op=ALU.mult),
                 reads=[po, rec], writes=[stg])
            S.dma("pool", ATT[row0:row0 + 64, tok0:tok0 + n], stg[:, 0:n], reads=[stg], writes=[k.dATT], slot=stg)

        def from_sbuf(self, ob, c0, row0, tok0, n=None):
            n = n or self.n
            rec, stg = self._fin(None, ob[64:128, c0:c0 + n], [ob], row0, tok0, n)
            S.op("dve", lambda e: e.tensor_copy(out=self.pz[:, 0:n], in_=ob[0:64, c0:c0 + n]), reads=[ob], writes=[self.pz])
            S.op("dve", lambda e: e.tensor_tensor(out=stg[:, 0:n], in0=self.pz[:, 0:n], in1=rec[64:128, 0:n], op=ALU.mult),
                 reads=[self.pz, rec], writes=[stg])
            S.dma("pool", ATT[row0:row0 + 64, tok0:tok0 + n], stg[:, 0:n], reads=[stg], writes=[k.dATT], slot=stg)

    def memattn_phase(li, row_off):
        with Phase(k, "m%d" % li) as ph:
            KmT, Vm, pk = mem_prep(ph, li)
            nrm = Normalizer(ph)
            qz = [ph.sb([128, 4, 512], BF16, "qz") for _ in range(2)]
            for b_ in qz:
                S.op("pool", lambda e: e.memset(b_[:], 0.0), writes=[b_])
            pT = [ph.sb([128, 512], BF16, "pT") for _ in range(6)]
            pst = [ph.ps([128, 512], F32, "pst") for _ in range(3)] + [pk]
            po = [ph.ps([128, 512], F32, "po") for _ in range(2)]
            warm(pst[0])
            items = [(G, h) for G in range(8) for h in range(4)]
            n = len(items)

            def s1(i):
                G, h = items[i]
                q = qz[G % 2]
                if h == 0:
                    for hh_ in range(4):
                        r0 = (hh_ % 2) * 64
                        S.dma("sp", q[r0:r0 + 64, hh_, :], QM[hh_ * 64:(hh_ + 1) * 64, G * 512:(G + 1) * 512],
                              reads=[k.dQM], writes=[q], slot=q)
                c = h // 2
                for kc in range(2):
                    sb_ = pst[(2 * i + kc) % 4]
                    p = pT[(2 * i + kc) % 6]
                    S.op("pe", lambda e: e.matmul(sb_[:], lhsT=KmT[:, c, kc * 128:(kc + 1) * 128], rhs=q[:, h, :], start=True, stop=True),
                         reads=[KmT, q], writes=[sb_])
                    S.op("act", lambda e: e.activation(out=p[:], in_=sb_[:], func=AF.Exp, scale=SCALE), reads=[sb_], writes=[p])

            def s2(i):
                G, h = items[i]
                o = po[i % 2]
                for kc in range(2):
                    p = pT[(2 * i + kc) % 6]
                    S.op("pe", lambda e: e.matmul(o[:], lhsT=Vm[:, kc, h, :], rhs=p[:], start=(kc == 0), stop=(kc == 1)),
                         reads=[Vm, p], writes=[o], inc=(kc == 1))
                nrm.from_psum(o, row_off + h * 64, G * 512)

            for i in range(n + 1):
                if i < n:
                    s1(i)
                if 0 <= i - 1 < n:
                    s2(i - 1)

    def mixA_phase(li, j):
        w = k.W[li]
        with Phase(k, "a%d" % li) as ph:
            g2 = load_gain(ph, I["norm_g"][li, 2:3, :], "g2")
            gv = ph.sb([128, 768], F32, "gv")
            S.dma("sp", gv[:], bc_rows(I["a_v_norm_g"][j:j + 1, :]), reads=[k.dIN], writes=[gv], slot=gv)
            win = load_w(ph, w["win"], D, 1792, "win")
            WcT = ph.sb([128, 12, 128], BF16, "WcT")
            Bt = ph.sb([128, 12, 64], F32, "Bt")
            sbT = ph.sb([128, 12], F32, "sbT")
            pu = [ph.ps([128, 512], F32, "pu") for _ in range(3)]
            pm = [ph.ps([128, 512], F32, "pm") for _ in range(2)]
            ptr = ph.ps([128, 8, 128], BF16, "ptr")
            ptg = ph.ps([128, 8, 128], BF16, "ptg")
            pq = ph.ps([128, 512], F32, "pq")
            with ExitStack() as es2:
                k.uid += 1
                swt = Buf(es2.enter_context(nc.sbuf_tensor("swt%d" % k.uid, [128, 12, 128], F32)), "swt")
                S.dma("sp", swt[:], I["a_spatial_w"][j].rearrange("g t s -> t g s"), reads=[k.dIN], writes=[swt], slot=swt)
                for g in range(12):
                    pb = pu[g % 3]
                    S.op("pe", lambda e: e.transpose(out=pb[:, 0:128], in_=swt[:, g, :], identity=cst[:, C_ID:C_ID + 128]),
                         reads=[swt, cst], writes=[pb])
                    S.op("dve", lambda e: e.tensor_tensor(out=WcT[:, g, :], in0=pb[:, 0:128], in1=cst[:, C_TRI:C_TRI + 128],
                                                          op=ALU.mult), reads=[pb, cst], writes=[WcT])
                with nc.allow_non_contiguous_dma(reason="tiny bias transpose"):
                    S.dma("sp", sbT[:], I["a_spatial_b"][j].rearrange("g t -> t g"), reads=[k.dIN], writes=[sbT], slot=sbT)
                S.op("dve", lambda e: e.tensor_copy(out=Bt[:], in_=AP(sbT.t[:].tensor, sbT.t[:].offset,
                                                                     [list(sbT.t[:].ap[0]), [1, 12], [0, 64]])),
                     reads=[sbT], writes=[Bt])
                S.barrier()
                S.release(swt)
            hT = [ph.sb([128, 8, 512], BF16, "hT") for _ in range(2)]
            xt = [ph.sb([128, D], F32, "xt") for _ in range(4)]
            hb = [ph.sb([128, D], BF16, "hb") for _ in range(4)]
            junk = ph.sb([128, D], BF16, "junk")
            st = [ph.sb([128, 4], F32, "st") for _ in range(4)]
            ug = [ph.sb([128, 768], F32, "ug") for _ in range(3)]
            vg = [ph.sb([128, 768], F32, "vg") for _ in range(3)]
            vj = ph.sb([128, 768], F32, "vj")
            vn = [ph.sb([128, 768], BF16, "vn") for _ in range(3)]
            ls = [ph.sb([128, 12], F32, "ls") for _ in range(3)]
            gt = [ph.sb([128, 768], F32, "gt") for _ in range(3)]
            gb = [ph.sb([128, 768], BF16, "gb") for _ in range(3)]
            gT = [ph.sb([128, 6, 512], BF16, "gT") for _ in range(2)]
            qs = [ph.sb([128, 512], BF16, "qs") for _ in range(2)]
            adst = ATT.rearrange("(c p) t -> p c t", p=128)
            Btf = Bt.t[:].rearrange("p g d -> p (g d)")
            warm(pm[0])

            def pA(T):
                prep_a(XS[T * 128:(T + 1) * 128, :], k.xs_t[T], xt[T % 4], g2, hb[T % 4], junk, st[T % 4])

            def pB(T):
                G_, t_ = divmod(T, 4)
                prep_b(hb[T % 4], ptr, hT[G_ % 2], hT[G_ % 2][:, :, t_ * 128:(t_ + 1) * 128], "act")

            for T_ in range(4):
                pA(T_)
                pB(T_)

            def sA(T):
                G, t = divmod(T, 4)
                if t == 0 and G + 1 < 8:
                    for t_ in range(4):
                        pA((G + 1) * 4 + t_)
                if G + 1 < 8:
                    pB((G + 1) * 4 + t)
                h = hT[G % 2]
                u, v, l, vnb = ug[T % 3], vg[T % 3], ls[T % 3], vn[T % 3]
                for c in range(3):
                    for kk in range(8):
                        S.op("pe", lambda e: e.matmul(pu[c][:], lhsT=h[:, kk, t * 128:(t + 1) * 128],
                                                      rhs=win[:, kk, c * 512:(c + 1) * 512], start=(kk == 0), stop=(kk == 7)),
                             reads=[h, win], writes=[pu[c]], inc=(kk == 7))
                S.op("act", lambda e: e.activation(out=u[:, 0:512], in_=pu[0][:], func=AF.Gelu_apprx_tanh), reads=[pu[0]], writes=[u])
                S.op("act", lambda e: e.activation(out=u[:, 512:768], in_=pu[1][:, 0:256], func=AF.Gelu_apprx_tanh),
                     reads=[pu[1]], writes=[u])
                S.op("act", lambda e: e.activation(out=v[:, 0:256], in_=pu[1][:, 256:512], func=AF.Gelu_apprx_tanh,
                                                   accum_out=l[:, 0:1]), reads=[pu[1]], writes=[v, l])
                S.op("act", lambda e: e.activation(out=v[:, 256:768], in_=pu[2][:], func=AF.Gelu_apprx_tanh,
                                                   accum_out=l[:, 1:2]), reads=[pu[2]], writes=[v, l])
                S.op("dve", lambda e: e.scalar_tensor_tensor(out=vj[:], in0=v[:], scalar=1.0, in1=v[:], op0=ALU.mult,
                                                             op1=ALU.mult, accum_out=l[:, 2:3]), reads=[v], writes=[vj, l])
                S.op("dve", lambda e: e.tensor_tensor(out=l[:, 3:4], in0=l[:, 0:1], in1=l[:, 1:2], op=ALU.add), reads=[l], writes=[l])
                S.op("dve", lambda e: e.tensor_scalar(out=l[:, 4:5], in0=l[:, 3:4], scalar1=1.0 / 768, scalar2=None, op0=ALU.mult),
                     reads=[l], writes=[l])
                S.op("dve", lambda e: e.tensor_tensor(out=l[:, 5:6], in0=l[:, 4:5], in1=l[:, 4:5], op=ALU.mult), reads=[l], writes=[l])
                S.op("dve", lambda e: e.scalar_tensor_tensor(out=l[:, 6:7], in0=l[:, 2:3], scalar=1.0 / 768, in1=l[:, 5:6],
                                                             op0=ALU.mult, op1=ALU.subtract), reads=[l], writes=[l])
                S.op("dve", lambda e: e.tensor_scalar(out=l[:, 7:8], in0=l[:, 6:7], scalar1=LN_EPS, scalar2=None, op0=ALU.add),
                     reads=[l], writes=[l])
                S.op("pool", lambda e: e.tensor_tensor(out=l[:, 8:9], in0=l[:, 7:8], in1=cst[:, C_NH:C_NH + 1], op=ALU.pow),
                     reads=[l, cst], writes=[l])
                S.op("dve", lambda e: e.tensor_scalar(out=vj[:], in0=v[:], scalar1=l[:, 4:5], scalar2=l[:, 8:9],
                                                      op0=ALU.subtract, op1=ALU.mult), reads=[v, l], writes=[vj])
                S.op("pool", lambda e: e.tensor_tensor(out=vnb[:], in0=vj[:], in1=gv[:], op=ALU.mult), reads=[vj, gv], writes=[vnb])
                if t == 3:
                    for c in range(2):
                        for kk in range(8):
                            S.op("pe", lambda e: e.matmul(pq[:], lhsT=win[:, kk, 1536 + c * 128:1536 + (c + 1) * 128], rhs=h[:, kk, :],
                                                          start=(kk == 0), stop=(kk == 7)), reads=[win, h], writes=[pq], inc=(kk == 7))
                        q = qs[c]
                        S.op("dve", lambda e: e.tensor_copy(out=q[:], in_=pq[:]), reads=[pq], writes=[q])
                        S.dma("pool", QM[c * 128:(c + 1) * 128, G * 512:(G + 1) * 512], q[:], reads=[q], writes=[k.dQM], slot=q)

            def sB(T):
                G, t = divmod(T, 4)
                h = hT[G % 2]
                u, vnb = ug[T % 3], vn[T % 3]
                for g in range(12):
                    bank = pm[0] if g < 8 else pm[1]
                    col = (g % 8) * 64
                    S.op("pe", lambda e: e.matmul(bank[:, col:col + 64], lhsT=WcT[:, g, :], rhs=vnb[:, g * 64:(g + 1) * 64],
                                                  start=True, stop=True), reads=[WcT, vnb], writes=[bank], inc=(g == 7 or g == 11))
                gtt, gbb = gt[T % 3], gb[T % 3]
                S.op("dve", lambda e: e.tensor_tensor(out=gtt[:, 0:512], in0=pm[0][:], in1=Btf[:, 0:512], op=ALU.add),
                     reads=[pm[0], Bt], writes=[gtt])
                S.op("dve", lambda e: e.tensor_tensor(out=gtt[:, 512:768], in0=pm[1][:, 0:256], in1=Btf[:, 512:768], op=ALU.add),
                     reads=[pm[1], Bt], writes=[gtt])
                S.op("pool", lambda e: e.tensor_tensor(out=gbb[:], in0=gtt[:], in1=u[:], op=ALU.mult), reads=[gtt, u], writes=[gbb])

            def sC(T):
                G, t = divmod(T, 4)
                h = hT[G % 2]
                gbb = gb[T % 3]
                for c in range(6):
                    S.op("pe", lambda e: e.transpose(out=ptg[:, c, :], in_=gbb[:, c * 128:(c + 1) * 128], identity=idb[:]),
                         reads=[gbb, idb], writes=[ptg], inc=(c == 5))
                S.op("act", lambda e: e.copy(out=gT[G % 2][:, :, t * 128:(t + 1) * 128], in_=ptg[:, 0:6, :]),
                     reads=[ptg], writes=[gT[G % 2]])
                if t == 3:
                    S.dma("pool", adst[:, 0:6, G * 512:(G + 1) * 512], gT[G % 2][:], reads=[gT[G % 2]], writes=[k.dATT], slot=gT[G % 2])

            for T in range(NT + 2):
                if T < NT:
                    sA(T)
                if 0 <= T - 1 < NT:
                    sB(T - 1)
                if 0 <= T - 2 < NT:
                    sC(T - 2)

    def proj_phase(li, kind):
        w = k.W[li]
        nin = 2560 if kind == 1 else 2572
        qoff = 2304 if kind == 1 else 2316
        with Phase(k, "p%d" % li) as ph:
            g2 = load_gain(ph, I["norm_g"][li, 2:3, :], "g2")
            win = load_w(ph, w["win"], D, nin, "win")
            hT = [ph.sb([128, 8, 512], BF16, "hT") for _ in range(2)]
            xt = [ph.sb([128, D], F32, "xt") for _ in range(4)]
            hb = [ph.sb([128, D], BF16, "hb") for _ in range(4)]
            junk = ph.sb([128, D], BF16, "junk")
            st = [ph.sb([128, 4], F32, "st") for _ in range(4)]
            stg = [ph.sb([128, 512], BF16, "stg") for _ in range(3)]
            vst = [ph.sb([128, 768], BF16, "vst") for _ in range(2)]
            ptr = ph.ps([128, 8, 128], BF16, "ptr")
            pf = [ph.ps([128, 512], F32, "pf") for _ in range(3)]
            pf2 = [ph.ps([128, 512], F32, "pf2") for _ in range(2)]
            pv = [ph.ps([128, 512], F32, "pv") for _ in range(2)]
            if kind == 1:
                rmb = ph.sb([128, 128], BF16, "rmb")
                S.op("dve", lambda e: e.tensor_copy(out=rmb[:], in_=cst[:, C_RM:C_RM + 128]), reads=[cst], writes=[rmb])
                pbs = [ph.sb([128, 512], BF16, "pbs") for _ in range(2)]
                cosT = ph.sb([128, SEQ], F32, "cosT")
                sinT = ph.sb([128, SEQ], F32, "sinT")
                CW = 1024
                posi = ph.sb([128, CW], I32, "posi")
                ang = ph.sb([128, CW], F32, "ang")
                nn = ph.sb([128, CW], F32, "nn")
                ni = ph.sb([128, CW], I32, "ni")
                TWO_PI = 2.0 * np.pi
                HI = float(np.float32(TWO_PI))
                LO = float(TWO_PI - float(np.float32(TWO_PI)))

                def reduce_into(dstb, c0, shift):
                    dst = dstb.t[:, c0:c0 + CW]
                    S.op("dve", lambda e: e.tensor_scalar(out=nn[:], in0=ang[:], scalar1=float(shift), scalar2=1.0 / TWO_PI,
                                                          op0=ALU.add, op1=ALU.mult), reads=[ang], writes=[nn])
                    S.op("dve", lambda e: e.tensor_copy(out=ni[:], in_=nn[:]), reads=[nn], writes=[ni])
                    S.op("dve", lambda e: e.tensor_copy(out=nn[:], in_=ni[:]), reads=[ni], writes=[nn])
                    S.op("dve", lambda e: e.scalar_tensor_tensor(out=dst, in0=nn[:], scalar=-HI, in1=ang[:], op0=ALU.mult,
                                                                 op1=ALU.add), reads=[nn, ang], writes=[dstb])
                    S.op("dve", lambda e: e.scalar_tensor_tensor(out=dst, in0=nn[:], scalar=-LO, in1=dst, op0=ALU.mult,
                                                                 op1=ALU.add), reads=[nn, dstb], writes=[dstb])
                    if shift != 0.0:
                        S.op("dve", lambda e: e.tensor_scalar(out=dst, in0=dst, scalar1=float(shift), scalar2=None, op0=ALU.add),
                             reads=[dstb], writes=[dstb])
                    for sgn in (1.0, -1.0):
                        cmp = ALU.is_gt if sgn > 0 else ALU.is_lt
                        S.op("dve", lambda e: e.tensor_scalar(out=nn[:], in0=dst, scalar1=sgn * np.pi, scalar2=-sgn * TWO_PI,
                                                              op0=cmp, op1=ALU.mult), reads=[dstb], writes=[nn])
                        S.op("dve", lambda e: e.tensor_tensor(out=dst, in0=dst, in1=nn[:], op=ALU.add), reads=[dstb, nn], writes=[dstb])
                    S.op("dve", lambda e: e.tensor_scalar(out=dst, in0=dst, scalar1=3.1415925, scalar2=-3.1415925,
                                                          op0=ALU.min, op1=ALU.max), reads=[dstb], writes=[dstb])
                    S.op("act", lambda e: e.activation(out=dst, in_=dst, func=AF.Sin), reads=[dstb], writes=[dstb])

                for c0 in range(0, SEQ, CW):
                    S.dma("sp", posi[:], bc_rows(I["positions"][:, c0:c0 + CW]), reads=[k.dIN], writes=[posi], slot=posi)
                    S.op("dve", lambda e: e.tensor_copy(out=ang[:], in_=posi[:]), reads=[posi], writes=[ang])
                    S.op("dve", lambda e: e.tensor_scalar(out=ang[:], in0=ang[:], scalar1=cst[:, C_IF:C_IF + 1], scalar2=None,
                                                          op0=ALU.mult), reads=[ang, cst], writes=[ang])
                    reduce_into(sinT, c0, 0.0)
                    reduce_into(cosT, c0, np.pi / 2)
                rt = [ph.sb([128, 512], F32, "rt") for _ in range(2)]
            if kind == 2:
                fT = ph.sb([12, SEQ], F32, "fT")
                fb = ph.sb([12, 2], F32, "fb")
                with nc.allow_non_contiguous_dma(reason="tiny bias"):
                    S.dma("sp", fb[:, 0:1], I["c_forget_bias"].rearrange("o h -> h o"), reads=[k.dIN], writes=[fb], slot=fb)
                S.op("dve", lambda e: e.tensor_scalar(out=fb[:, 1:2], in0=fb[:, 0:1], scalar1=-1.0, scalar2=None, op0=ALU.mult),
                     reads=[fb], writes=[fb])
            nfm = 0
            warm(pf[0])
            def pA(T):
                prep_a(XS[T * 128:(T + 1) * 128, :], k.xs_t[T], xt[T % 4], g2, hb[T % 4], junk, st[T % 4])

            def pB(T):
                G_, t_ = divmod(T, 4)
                prep_b(hb[T % 4], ptr, hT[G_ % 2], hT[G_ % 2][:, :, t_ * 128:(t_ + 1) * 128], "act" if t_ % 2 else "dve")

            for T in range(4):
                pA(T)
                pB(T)
            for G in range(8):
                h = hT[G % 2]
                if G + 1 < 8:
                    for t in range(4):
                        pA((G + 1) * 4 + t)
                blocks = [("q", c, c * 128) for c in range(6)] + [("k", c, 768 + c * 128) for c in range(6)] + \
                         [("m", c, qoff + c * 128) for c in range(2)]
                for bi, (what, c, col) in enumerate(blocks):
                    if G + 1 < 8 and 2 <= bi < 6:
                        pB((G + 1) * 4 + bi - 2)
                    p = pf[nfm % 3]
                    sg_ = stg[nfm % 3]
                    for kk in range(8):
                        S.op("pe", lambda e: e.matmul(p[:], lhsT=win[:, kk, col:col + 128], rhs=h[:, kk, :], start=(kk == 0), stop=(kk == 7)),
                             reads=[win, h], writes=[p], inc=(kk == 7))
                    if kind == 1 and what in ("q", "k"):
                        p2 = pf2[nfm % 2]
                        r = rt[nfm % 2]
                        pb_ = pbs[nfm % 2]
                        S.op("act", lambda e: e.copy(out=pb_[:], in_=p[:]), reads=[p], writes=[pb_])
                        S.op("pe", lambda e: e.matmul(p2[:], lhsT=rmb[:], rhs=pb_[:], start=True, stop=True), reads=[rmb, pb_], writes=[p2])
                        S.op("dve", lambda e: e.tensor_tensor(out=r[:], in0=p[:], in1=cosT[:, G * 512:(G + 1) * 512], op=ALU.mult),
                             reads=[p, cosT], writes=[r])
                        S.op("dve", lambda e: e.tensor_tensor(out=p2[:], in0=p2[:], in1=sinT[:, G * 512:(G + 1) * 512], op=ALU.mult),
                             reads=[p2, sinT], writes=[p2])
                        S.op("dve", lambda e: e.tensor_tensor(out=sg_[:], in0=p2[:], in1=r[:], op=ALU.add), reads=[p2, r], writes=[sg_])
                    else:
                        if nfm % 2 == 0:
                            S.op("act", lambda e: e.copy(out=sg_[:], in_=p[:]), reads=[p], writes=[sg_])
                        else:
                            S.op("dve", lambda e: e.tensor_copy(out=sg_[:], in_=p[:]), reads=[p], writes=[sg_])
                    dst, db = {"q": (QT, k.dQT), "k": (KT, k.dKT), "m": (QM, k.dQM)}[what]
                    S.dma("pool", dst[c * 128:(c + 1) * 128, G * 512:(G + 1) * 512], sg_[:], reads=[sg_], writes=[db], slot=sg_)
                    nfm += 1
                if kind == 2:
                    p = pf[nfm % 3]
                    nfm += 1
                    for kk in range(8):
                        S.op("pe", lambda e: e.matmul(p[0:12, :], lhsT=win[:, kk, 2304:2316], rhs=h[:, kk, :], start=(kk == 0), stop=(kk == 7)),
                             reads=[win, h], writes=[p], inc=(kk == 7))
                    S.op("dve", lambda e: e.tensor_copy(out=fT[:, G * 512:(G + 1) * 512], in_=p[0:12, :]), reads=[p], writes=[fT])
                for t in range(4):
                    T = G * 4 + t
                    vs = vst[T % 2]
                    for c, (c0, c1) in enumerate(((0, 512), (512, 768))):
                        p = pv[c]
                        for kk in range(8):
                            S.op("pe", lambda e: e.matmul(p[:, 0:c1 - c0], lhsT=h[:, kk, t * 128:(t + 1) * 128],
                                                          rhs=win[:, kk, 1536 + c0:1536 + c1], start=(kk == 0), stop=(kk == 7)),
                                 reads=[win, h], writes=[p], inc=(kk == 7))
                        if c == 0:
                            S.op("act", lambda e: e.copy(out=vs[:, c0:c1], in_=p[:, 0:c1 - c0]), reads=[p], writes=[vs])
                        else:
                            S.op("dve", lambda e: e.tensor_copy(out=vs[:, c0:c1], in_=p[:, 0:c1 - c0]), reads=[p], writes=[vs])
                    S.dma("pool", VV[T * 128:(T + 1) * 128, :], vs[:], reads=[vs], writes=[k.dVV], slot=vs)
            if kind == 2:
                ones = ph.sb([12, SEQ], F32, "ones")
                S.op("pool", lambda e: e.memset(ones[:], 1.0), writes=[ones])
                S.op("act", lambda e: e.activation(out=fT[:], in_=fT[:], func=AF.Exp, bias=fb[:, 1:2], scale=-1.0), reads=[fT, fb], writes=[fT])
                S.op("act", lambda e: e.activation(out=fT[:], in_=fT[:], func=AF.Ln, bias=1.0, scale=1.0), reads=[fT], writes=[fT])
                csum = ph.sb([12, SEQ], F32, "csum")
                S.op("dve", lambda e: e.tensor_tensor_scan(out=csum[:], data0=ones[:], data1=fT[:], initial=0.0, op0=ALU.mult, op1=ALU.add),
                     reads=[ones, fT], writes=[csum])
                S.dma("pool", CT, csum[:], reads=[csum], writes=[k.dCT], slot=csum)

    def fox_phase(li):
        with Phase(k, "c%d" % li) as ph:
            nrm = Normalizer(ph)
            q0 = [ph.sb([128, SEQ], BF16, "q0") for _ in range(2)]
            kTp = [ph.sb([128, SEQ], BF16, "kTp") for _ in range(2)]
            va = [ph.sb([128, NT, 128], BF16, "va") for _ in range(2)]
            csb = [ph.sb([128, SEQ], F32, "csb") for _ in range(2)]
            cska = ph.sb([128, 12 * NT], F32, "cska")
            ctr = ph.sb([128, 3, 128], F32, "ctr")
            arg = [ph.sb([128, 512], F32, "arg") for _ in range(4)]
            pT = [ph.sb([128, 512], BF16, "pT") for _ in range(4)]
            cm = [ph.sb([128, 512], F32, "cm") for _ in range(12)]
            pst = [ph.ps([128, 512], F32, "pst") for _ in range(4)]
            po = [ph.ps([128, 512], F32, "po") for _ in range(2)]
            for b_ in va:
                S.op("pool", lambda e: e.memset(b_[:], 1.0), writes=[b_])
            for b_ in q0:
                S.op("pool", lambda e: e.memset(b_[:], 0.0), writes=[b_])
            vsrc = VV.rearrange("(kb p) f -> p kb f", p=128)
            S.dma("sp", ctr[:], CT.rearrange("h (kb p) -> (h kb) p", p=128).rearrange("(c q) p -> q c p", q=128),
                  reads=[k.dCT], writes=[ctr], slot=ctr)
            for c in range(3):
                S.op("pe", lambda e: e.transpose(out=pst[c][:, 0:128], in_=ctr[:, c, :], identity=cst[:, C_ID:C_ID + 128]),
                     reads=[ctr, cst], writes=[pst[c]])
                S.op("dve", lambda e: e.tensor_copy(out=cska[:, c * 128:(c + 1) * 128], in_=pst[c][:, 0:128]), reads=[pst[c]], writes=[cska])

            def load_head(h):
                i = h % 2
                r0 = i * 64
                S.dma("sp", q0[i][r0:r0 + 64, :], QT[h * 64:(h + 1) * 64, :], reads=[k.dQT], writes=[q0[i]], slot=q0[i])
                if h % 2 == 0:
                    c = h // 2
                    S.dma("sp", kTp[c % 2][:], KT[c * 128:(c + 1) * 128, :], reads=[k.dKT], writes=[kTp[c % 2]], slot=kTp[c % 2])
                S.dma("sp", va[i][:, :, 0:64], vsrc[:, :, h * 64:(h + 1) * 64], reads=[k.dVV], writes=[va[i]], slot=va[i])
                S.dma("sp", csb[i][:], bc_rows(CT[h:h + 1, :]), reads=[k.dCT], writes=[csb[i]], slot=csb[i])

            items = []
            for h in range(12):
                for qg in range(8):
                    nkb = 4 * qg + 4
                    for kb in range(nkb):
                        items.append((h, qg, kb, nkb))
            n = len(items)

            def stA(i):
                h, qg, kb, nkb = items[i]
                kt = kTp[(h // 2) % 2]
                s_ = pst[i % 4]
                c0 = max(0, kb - 4 * qg) * 128
                S.op("pe", lambda e: e.matmul(s_[:, c0:512], lhsT=kt[:, kb * 128:(kb + 1) * 128], rhs=q0[h % 2][:, qg * 512 + c0:(qg + 1) * 512],
                                              start=True, stop=True), reads=[kt, q0[h % 2]], writes=[s_])

            def stM(i):
                h, qg, kb, nkb = items[i]
                jd = kb - 4 * qg
                if jd < 0:
                    return
                c0 = jd * 128
                m_ = cm[diag_idx[i] % 12]
                S.op("pool", lambda e: e.tensor_tensor(out=m_[:, c0:512], in0=csb[h % 2][:, qg * 512 + c0:(qg + 1) * 512],
                                                       in1=cst[:, C_MC + jd * 512 + c0:C_MC + (jd + 1) * 512], op=ALU.subtract),
                     reads=[csb[h % 2], cst], writes=[m_])

            def stB(i):
                h, qg, kb, nkb = items[i]
                hb_ = h % 2
                s_, a_, p = pst[i % 4], arg[i % 4], pT[i % 4]
                jd = kb - 4 * qg
                c0 = max(0, jd) * 128
                if jd >= 0:
                    m_ = cm[diag_idx[i] % 12]
                    S.op("dve", lambda e: e.scalar_tensor_tensor(out=a_[:, c0:512], in0=s_[:, c0:512], scalar=SCALE, in1=m_[:, c0:512],
                                                                 op0=ALU.mult, op1=ALU.subtract), reads=[s_, m_], writes=[a_])
                else:
                    S.op("dve", lambda e: e.scalar_tensor_tensor(out=a_[:], in0=s_[:], scalar=SCALE, in1=csb[hb_][:, qg * 512:(qg + 1) * 512],
                                                                 op0=ALU.mult, op1=ALU.subtract), reads=[s_, csb[hb_]], writes=[a_])
                S.op("act", lambda e: e.activation(out=p[:, c0:512], in_=a_[:, c0:512], func=AF.Exp, bias=cska[:, h * NT + kb:h * NT + kb + 1], scale=1.0),
                     reads=[a_, cska], writes=[p])

            def stC(i):
                h, qg, kb, nkb = items[i]
                hb_ = h % 2
                o = po[(h * 8 + qg) % 2]
                c0 = max(0, kb - 4 * qg) * 128
                S.op("pe", lambda e: e.matmul(o[:, c0:512], lhsT=va[hb_][:, kb, :], rhs=pT[i % 4][:, c0:512], start=(kb == 0), stop=(kb == nkb - 1)),
                     reads=[va[hb_], pT[i % 4]], writes=[o], inc=(kb == nkb - 1))
                if kb == nkb - 1:
                    nrm.from_psum(o, h * 64, qg * 512)
                    if qg == 7 and h + 2 < 12:
                        load_head(h + 2)

            diag_idx = {}
            for i_, it_ in enumerate(items):
                if it_[2] - 4 * it_[1] >= 0:
                    diag_idx[i_] = len(diag_idx)
            load_head(0)
            load_head(1)
            warm(pst[2])
            LEAD = 5
            for i in range(min(LEAD, n)):
                stM(i)
            for i in range(n + 3):
                if i < n:
                    stA(i)
                if 0 <= i - 1 < n:
                    stB(i - 1)
                if 0 <= i - 3 < n:
                    stC(i - 3)
                if i + LEAD < n:
                    stM(i + LEAD)

    def dil_phase(li):
        with Phase(k, "b%d" % li) as ph:
            nrm = Normalizer(ph, with_pz=True)
            q0 = [ph.sb([128, SEQ], BF16, "q0") for _ in range(2)]
            kTp = [ph.sb([128, SEQ], BF16, "kTp") for _ in range(2)]
            va = [ph.sb([128, NT, 128], BF16, "va") for _ in range(2)]
            oacc = [ph.sb([128, SEQ], F32, "oacc") for _ in range(2)]
            arg = [ph.sb([128, 256], F32, "arg") for _ in range(4)]
            pT = [ph.sb([128, 256], BF16, "pT") for _ in range(6)]
            pst = [ph.ps([128, 512], F32, "pst") for _ in range(4)]
            po = [ph.ps([128, 512], F32, "po") for _ in range(2)]
            for b_ in va:
                S.op("pool", lambda e: e.memset(b_[:], 1.0), writes=[b_])
            otmp = [ph.sb([128, 512], F32, "otmp") for _ in range(2)]
            acc_n = [0]
            dils = [1, 4, 16]
            order = [(j, g) for j in range(4) for g in range(3)]

            def load_head(ih):
                j, g = order[ih]
                h = 4 * g + j
                d = dils[g]
                i = ih % 2
                r0 = (h % 2) * 64
                z0 = 64 - r0
                S.op("pool", lambda e: e.memset(q0[i][z0:z0 + 64, :], 0.0), writes=[q0[i]])
                S.dma("sp", q0[i][r0:r0 + 64, :], QT[h * 64:(h + 1) * 64, :], reads=[k.dQT], writes=[q0[i]], slot=q0[i])
                c = h // 2
                S.dma("sp", kTp[i][:], KT[c * 128:(c + 1) * 128, :], reads=[k.dKT], writes=[kTp[i]], slot=kTp[i])
                nb = NT // d
                for r in range(d):
                    src = AP(VV.tensor, VV.offset + r * 768 + h * 64, [[d * 768, 128], [128 * d * 768, nb], [1, 64]])
                    S.dma("sp", va[i][:, r * nb:(r + 1) * nb, 0:64], src, reads=[k.dVV], writes=[va[i]], slot=va[i])

            load_head(0)
            warm(pst[3])
            for ih, (j, g) in enumerate(order):
                if ih + 1 < len(order):
                    load_head(ih + 1)
                d = dils[g]
                nb = NT // d
                i2 = ih % 2
                oa = oacc[j % 2]
                q, kk_, v = q0[i2], kTp[i2], va[i2]
                items = [(r, m) for r in range(d) for m in range(nb)]
                n = len(items)

                def cls(r, m0, cnt):
                    start = r + 128 * m0 * d
                    return slice(start, start + (cnt - 1) * d + 1, d)

                def stA(i):
                    r, m = items[i]
                    nq = 256 if m + 1 < nb else 128
                    s_ = pst[i % 4]
                    S.op("pe", lambda e: e.matmul(s_[:, 0:nq], lhsT=kk_[:, cls(r, m, 128)], rhs=q[:, cls(r, m, nq)], start=True, stop=True),
                         reads=[kk_, q], writes=[s_])

                def stB(i):
                    r, m = items[i]
                    nq = 256 if m + 1 < nb else 128
                    s_, a_, p = pst[i % 4], arg[i % 4], pT[i % 6]
                    S.op("dve", lambda e: e.scalar_tensor_tensor(out=a_[:, 0:nq], in0=s_[:, 0:nq], scalar=SCALE, in1=cst[:, C_MB:C_MB + nq],
                                                                 op0=ALU.mult, op1=ALU.add), reads=[s_, cst], writes=[a_])
                    S.op("act", lambda e: e.activation(out=p[:, 0:nq], in_=a_[:, 0:nq], func=AF.Exp), reads=[a_], writes=[p])

                def stC(i):
                    r, m = items[i]
                    o = po[(m // 4) % 2] if d < 16 else po[r % 2]
                    slot = (m % 4) * 128
                    if m > 0:
                        S.op("pe", lambda e: e.matmul(o[:, slot:slot + 128], lhsT=v[:, r * nb + m - 1, :], rhs=pT[(i - 1) % 6][:, 128:256],
                                                      start=True, stop=False), reads=[v, pT[(i - 1) % 6]], writes=[o], inc=False)
                    S.op("pe", lambda e: e.matmul(o[:, slot:slot + 128], lhsT=v[:, r * nb + m, :], rhs=pT[i % 6][:, 0:128],
                                                  start=(m == 0), stop=True), reads=[v, pT[i % 6]], writes=[o])
                    if m % 4 == 3 or m == nb - 1:
                        m0 = (m // 4) * 4
                        cntq = (m - m0 + 1) * 128
                        dst = oa[:, cls(r, m0, cntq)]
                        if g == 0:
                            S.op("act", lambda e: e.copy(out=dst, in_=o[:, 0:cntq]), reads=[o], writes=[oa])
                        else:
                            tb = otmp[acc_n[0] % 2]
                            acc_n[0] += 1
                            S.op("act", lambda e: e.copy(out=tb[:, 0:cntq], in_=o[:, 0:cntq]), reads=[o], writes=[tb])
                            S.op("pool", lambda e: e.tensor_tensor(out=dst, in0=tb[:, 0:cntq], in1=dst, op=ALU.add), reads=[tb, oa], writes=[oa])

                for i in range(n + 3):
                    if i < n:
                        stA(i)
                    if 0 <= i - 1 < n:
                        stB(i - 1)
                    if 0 <= i - 3 < n:
                        stC(i - 3)
                if g == 2:
                    for c in range(8):
                        nrm.from_sbuf(oa, c * 512, j * 64, c * 512)

    src_ap, src_bufs = I["x"], [k.dIN] * NT
    for li in range(n_layers):
        kind, j = kinds[li], li // 3
        ffn_phase(li, 0, src_ap, src_bufs, XS, k.xs_t)
        if kind == 0:
            mixA_phase(li, j)
            memattn_phase(li, 768)
            mixout_phase(li, 8)
        elif kind == 1:
            proj_phase(li, 1)
            dil_phase(li)
            memattn_phase(li, 256)
            mixout_phase(li, 4)
        else:
            proj_phase(li, 2)
            fox_phase(li)
            memattn_phase(li, 768)
            mixout_phase(li, 8)
        last = (li == n_layers - 1)
        ffn_phase(li, 1, XS, k.xs_t, out_d if last else XS, [Buf(None, "o%d" % i) for i in range(NT)] if last else k.xs_t)
        src_ap, src_bufs = XS, k.xs_t
        conv_flush(li + 1)

    S.barrier()
    gst.close()
    ctx.__exit__(None, None, None)
    return nc


_CACHE = {}


def kernel(**inputs):
    x = np.ascontiguousarray(np.asarray(inputs["x"], dtype=np.float32))
    mem = np.ascontiguousarray(np.asarray(inputs["mem"], dtype=np.float32))
    pos = np.ascontiguousarray(np.asarray(inputs["positions"], dtype=np.int32))
    B = x.shape[0]
    if "nc" not in _CACHE:
        _CACHE["nc"] = build()
    nc = _CACHE["nc"]
    consts = make_consts()
    shared = {}
    for name in ("norm_g", "mem_norm_g", "w_mem_kv", "ffn_w_gate_up", "ffn_w_down", "a_w_in", "a_spatial_w", "a_spatial_b",
                 "a_v_norm_g", "a_w_out", "b_w_in", "b_w_out", "c_w_in", "c_forget_bias", "c_w_out"):
        shared[name] = np.ascontiguousarray(np.asarray(inputs[name], dtype=np.float32))
    in_maps = []
    for b in range(B):
        m = dict(shared)
        m["x"] = x[b]
        m["mem"] = mem[b]
        m["positions"] = pos[b:b + 1]
        m["consts"] = consts
        in_maps.append(m)
    res = run_bass_kernel_spmd(nc, in_maps, core_ids=list(range(B)))
    return np.stack([np.asarray(r["out"], dtype=np.float32) for r in res.results], axis=0)
```

```python
import numpy as np
from contextlib import ExitStack
import concourse.bass as bass
import concourse.mybir as mybir
from concourse.bass_utils import run_bass_kernel_spmd

F32 = mybir.dt.float32
BF16 = mybir.dt.bfloat16
I32 = mybir.dt.int32
AF = mybir.ActivationFunctionType
ALU = mybir.AluOpType
AP = bass.AP

D = 1024
SEQ = 4096
NT = SEQ // 128
DFF = 2816
NKF = DFF // 128
NMEM = 256
DEPTH = 4
EPS = 1e-6
LN_EPS = 1e-5
SCALE = 0.125
NEG = -1e30

C_ID, C_SEL, C_TRI, C_MB, C_MC, C_IF, C_NH, C_RM, NCONST = 0, 128, 192, 320, 576, 2624, 2625, 2628, 2756


def make_consts():
    c = np.zeros((128, NCONST), np.float32)
    c[:, C_ID:C_ID + 128] = np.eye(128, dtype=np.float32)
    c[64, C_SEL:C_SEL + 64] = 1.0
    s = np.arange(128)[:, None]
    t = np.arange(128)[None, :]
    c[:, C_TRI:C_TRI + 128] = (t >= s).astype(np.float32)
    mb = np.zeros((128, 256), np.float32)
    mb[:, :128] = np.where(t >= s, 0.0, NEG)
    mb[:, 128:] = np.where(s >= t, 0.0, NEG)
    c[:, C_MB:C_MB + 256] = mb
    tt = np.arange(512)[None, :]
    for j in range(4):
        c[:, C_MC + j * 512:C_MC + (j + 1) * 512] = np.where(tt >= j * 128 + s, 0.0, NEG)
    invf = (np.float32(10000.0) ** (-np.arange(0, 64, 2, dtype=np.float32) / np.float32(64))).astype(np.float32)
    c[:, C_IF] = invf[np.arange(128) % 32]
    c[:, C_NH] = -0.5
    for m in range(128):
        if m % 64 < 32:
            c[m + 32, C_RM + m] = -1.0
        else:
            c[m - 32, C_RM + m] = 1.0
    return c


class Buf:
    __slots__ = ("t", "w", "rs", "name", "dsem", "dcnt", "excl")

    def __init__(self, t=None, name="", excl=False):
        self.t = t
        self.w = None
        self.rs = []
        self.name = name
        self.dsem = None
        self.dcnt = 0
        self.excl = excl

    def __getitem__(self, idx):
        return self.t[idx]


class Sched:
    def __init__(self, nc):
        self.nc = nc
        self.e = dict(pe=nc.tensor, act=nc.scalar, dve=nc.vector, pool=nc.gpsimd, sp=nc.sync)
        self.sem = {k: nc.alloc_semaphore("prog_" + k) for k in self.e}
        self.cnt = {k: 0 for k in self.e}
        self.seen = {k: {} for k in self.e}
        self.pend = {k: False for k in self.e}
        self.dma_pending = {}
        self.pool_sems = []
        self.nsem = 0

    def _wait(self, eng, tok, raw=True):
        if tok is None:
            return
        sem, val = tok
        key = id(sem)
        if self.seen[eng].get(key, 0) >= val:
            return
        if sem is self.sem[eng]:
            if eng == "pe" or eng == "sp" or (not raw) or val > self.cnt[eng]:
                return
        self.e[eng].wait_ge(sem, val)
        self.seen[eng][key] = val

    def _deps(self, eng, reads, writes):
        for b in reads:
            self._wait(eng, b.w, True)
            if b.excl:
                for r in b.rs:
                    self._wait(eng, r, False)
        for b in writes:
            self._wait(eng, b.w, False)
            for r in b.rs:
                self._wait(eng, r, False)

    def _post(self, tok, reads, writes):
        for b in reads:
            if b.excl:
                b.w = tok
                b.rs = []
            else:
                b.rs.append(tok)
                if len(b.rs) > 16:
                    best = {}
                    for sem, val in b.rs:
                        k = id(sem)
                        if k not in best or best[k][1] < val:
                            best[k] = (sem, val)
                    b.rs = list(best.values())
        for b in writes:
            b.w = tok
            b.rs = []

    def op(self, eng, fn, reads=(), writes=(), inc=True):
        self._deps(eng, reads, writes)
        tok = (self.sem[eng], self.cnt[eng] + 1)
        ins = fn(self.e[eng])
        if inc:
            ins.then_inc(self.sem[eng], 1)
            self.cnt[eng] += 1
            self.pend[eng] = False
        else:
            self.pend[eng] = True
        self._post(tok, reads, writes)
        return tok

    def _slot_sem(self, slot):
        if slot.dsem is None:
            if self.pool_sems:
                slot.dsem, slot.dcnt = self.pool_sems.pop()
            else:
                slot.dsem = self.nc.alloc_semaphore("dm%d" % self.nsem)
                slot.dcnt = 0
                self.nsem += 1

    def dma(self, eng, out_ap, in_ap, reads=(), writes=(), slot=None, **kw):
        self._slot_sem(slot)
        saved = []
        for b in writes:
            if b.w is not None and b.w[0] is slot.dsem and not b.rs:
                saved.append((b, b.w))
                b.w = None
        self._deps(eng, reads, writes)
        for b, w_ in saved:
            b.w = w_
        slot.dcnt += 16
        tok = (slot.dsem, slot.dcnt)
        self.e[eng].dma_start(out=out_ap, in_=in_ap, **kw).then_inc(slot.dsem, 16)
        self.dma_pending[id(slot.dsem)] = tok
        self._post(tok, reads, writes)
        return tok

    def release(self, slot):
        if slot.dsem is not None:
            self.pool_sems.append((slot.dsem, slot.dcnt))
            slot.dsem = None

    def barrier(self):
        for tok in self.dma_pending.values():
            self._wait("sp", tok)
        self.dma_pending = {}
        for k in self.e:
            if k != "sp" and self.cnt[k] > 0:
                assert not self.pend[k], k
                self._wait("sp", (self.sem[k], self.cnt[k]))
        ins = self.e["sp"].nop()
        ins.then_inc(self.sem["sp"], 1)
        self.cnt["sp"] += 1
        tok = (self.sem["sp"], self.cnt["sp"])
        for k in self.e:
            if k != "sp":
                self._wait(k, tok)
        return tok


class Phase:
    def __init__(self, K, name):
        self.K = K
        self.name = name
        self.es = ExitStack()
        self.bufs = []

    def __enter__(self):
        return self

    def sb(self, shape, dtype, tag="b"):
        K = self.K
        K.uid += 1
        t = self.es.enter_context(K.nc.sbuf_tensor("%s_%s_%d" % (self.name, tag, K.uid), list(shape), dtype))
        b = Buf(t, tag)
        self.bufs.append(b)
        return b

    def ps(self, shape, dtype, tag="p"):
        K = self.K
        K.uid += 1
        t = self.es.enter_context(K.nc.psum_tensor("%s_%s_%d" % (self.name, tag, K.uid), list(shape), dtype))
        b = Buf(t, tag, excl=True)
        self.bufs.append(b)
        return b

    def __exit__(self, *a):
        self.K.S.barrier()
        for b in self.bufs:
            self.K.S.release(b)
        self.es.close()
        return False


class K:
    pass


def dram_in(nc, name, shape, dtype=F32):
    return nc.dram_tensor(name, list(shape), dtype, kind="ExternalInput").ap()


DBG_OUT = set()


def dram_tmp(nc, name, shape, dtype):
    kind = "ExternalOutput" if name in DBG_OUT else "Internal"
    return nc.dram_tensor(name, list(shape), dtype, kind=kind).ap()


def bc_rows(ap2d_row, nparts=128):
    n = ap2d_row.shape[-1]
    return AP(ap2d_row.tensor, ap2d_row.offset, [[0, nparts], [1, n]])


def build(n_layers=DEPTH, dbg=None):
    nc = bass.Bass("TRN2", target_bir_lowering=False)
    k = K()
    k.nc = nc
    k.uid = 0
    I = {}
    I["x"] = dram_in(nc, "x", [SEQ, D])
    I["mem"] = dram_in(nc, "mem", [NMEM, D])
    I["positions"] = dram_in(nc, "positions", [1, SEQ], I32)
    I["norm_g"] = dram_in(nc, "norm_g", [DEPTH, 6, D])
    I["mem_norm_g"] = dram_in(nc, "mem_norm_g", [DEPTH, D])
    I["w_mem_kv"] = dram_in(nc, "w_mem_kv", [DEPTH, D, 512])
    I["ffn_w_gate_up"] = dram_in(nc, "ffn_w_gate_up", [DEPTH, 2, D, 2 * DFF])
    I["ffn_w_down"] = dram_in(nc, "ffn_w_down", [DEPTH, 2, DFF, D])
    I["a_w_in"] = dram_in(nc, "a_w_in", [2, D, 1792])
    I["a_spatial_w"] = dram_in(nc, "a_spatial_w", [2, 12, 128, 128])
    I["a_spatial_b"] = dram_in(nc, "a_spatial_b", [2, 12, 128])
    I["a_v_norm_g"] = dram_in(nc, "a_v_norm_g", [2, 768])
    I["a_w_out"] = dram_in(nc, "a_w_out", [2, 1024, D])
    I["b_w_in"] = dram_in(nc, "b_w_in", [1, D, 2560])
    I["b_w_out"] = dram_in(nc, "b_w_out", [1, 512, D])
    I["c_w_in"] = dram_in(nc, "c_w_in", [1, D, 2572])
    I["c_forget_bias"] = dram_in(nc, "c_forget_bias", [1, 12])
    I["c_w_out"] = dram_in(nc, "c_w_out", [1, 1024, D])
    I["consts"] = dram_in(nc, "consts", [128, NCONST])
    out_d = nc.dram_tensor("out", [SEQ, D], F32, kind="ExternalOutput").ap()
    k.I = I

    XS = dram_tmp(nc, "XS", [SEQ, D], F32)
    QT = dram_tmp(nc, "QT", [768, SEQ], BF16)
    KT = dram_tmp(nc, "KT", [768, SEQ], BF16)
    VV = dram_tmp(nc, "VV", [SEQ, 768], BF16)
    QM = dram_tmp(nc, "QM", [256, SEQ], BF16)
    ATT = dram_tmp(nc, "ATT", [1024, SEQ], BF16)
    CT = dram_tmp(nc, "CT", [12, SEQ], F32)
    k.XS, k.QT, k.KT, k.VV, k.QM, k.ATT, k.CT = XS, QT, KT, VV, QM, ATT, CT
    k.xs_t = [Buf(None, "xs%d" % i) for i in range(NT)]
    k.dQT, k.dKT, k.dVV, k.dQM, k.dATT, k.dCT = (Buf(None, n) for n in ("dQT", "dKT", "dVV", "dQM", "dATT", "dCT"))
    k.dIN = Buf(None, "din")

    ctx = nc.cleanup_on_exit()
    ctx.__enter__()
    S = Sched(nc)
    k.S = S

    kinds = [0, 1, 2, 0]
    k.W = []
    k.conv_jobs = []

    def add_weight(name, src_ap, rows, cols, li):
        dst = dram_tmp(nc, name, [rows, cols], BF16)
        b = Buf(None, name)
        r0 = 0
        while r0 < rows:
            r1 = min(rows, r0 + 64)
            k.conv_jobs.append((li, dst[r0:r1, :], src_ap[r0:r1, :], b))
            r0 = r1
        return (dst, b)

    for li in range(n_layers):
        kind, j = kinds[li], li // 3
        w = {}
        w["wgu0"] = add_weight("wgu%d0" % li, I["ffn_w_gate_up"][li, 0], D, 2 * DFF, li)
        w["wd0"] = add_weight("wd%d0" % li, I["ffn_w_down"][li, 0], DFF, D, li)
        w["wkv"] = add_weight("wkv%d" % li, I["w_mem_kv"][li], D, 512, li)
        if kind == 0:
            w["win"] = add_weight("win%d" % li, I["a_w_in"][j], D, 1792, li)
            w["wout"] = add_weight("wout%d" % li, I["a_w_out"][j], 1024, D, li)
        elif kind == 1:
            w["win"] = add_weight("win%d" % li, I["b_w_in"][0], D, 2560, li)
            w["wout"] = add_weight("wout%d" % li, I["b_w_out"][0], 512, D, li)
        else:
            w["win"] = add_weight("win%d" % li, I["c_w_in"][0], D, 2572, li)
            w["wout"] = add_weight("wout%d" % li, I["c_w_out"][0], 1024, D, li)
        w["wgu1"] = add_weight("wgu%d1" % li, I["ffn_w_gate_up"][li, 1], D, 2 * DFF, li)
        w["wd1"] = add_weight("wd%d1" % li, I["ffn_w_down"][li, 1], DFF, D, li)
        k.W.append(w)
    k.conv_pos = 0

    def conv_pump(n=1, upto_layer=None):
        while n > 0 and k.conv_pos < len(k.conv_jobs):
            li, dst, src, b = k.conv_jobs[k.conv_pos]
            if upto_layer is not None and li > upto_layer:
                return
            S.dma("pool", dst, src, reads=[k.dIN], writes=[b], slot=b, max_dma_last_dim=4096)
            k.conv_pos += 1
            n -= 1

    def conv_flush(layer):
        while k.conv_pos < len(k.conv_jobs) and k.conv_jobs[k.conv_pos][0] <= layer:
            conv_pump(1)

    k.conv_pump = conv_pump

    gst = ExitStack()
    cst = Buf(gst.enter_context(nc.sbuf_tensor("cst", [128, NCONST], F32)), "cst")
    idb = Buf(gst.enter_context(nc.sbuf_tensor("idb", [128, 128], BF16)), "idb")
    S.dma("sp", cst[:], I["consts"], reads=[k.dIN], writes=[cst], slot=cst)
    S.op("dve", lambda e: e.tensor_copy(out=idb[:], in_=cst[:, C_ID:C_ID + 128]), reads=[cst], writes=[idb])
    wrm = Buf(gst.enter_context(nc.sbuf_tensor("wrm", [128, 512], BF16)), "wrm")
    S.op("pool", lambda e: e.memset(wrm[:], 1.0), writes=[wrm])
    k.cst, k.idb = cst, idb
    S.barrier()

    def warm(bank, n=8):
        for i in range(n):
            S.op("pe", lambda e: e.matmul(bank[:, 0:512], lhsT=idb[:], rhs=wrm[:], start=True, stop=True),
                 reads=[idb, wrm], writes=[bank], inc=(i == n - 1))

    conv_flush(0)

    def load_gain(ph, row_ap, tag):
        g = ph.sb([128, D], F32, tag)
        S.dma("sp", g[:], bc_rows(row_ap), reads=[k.dIN], writes=[g], slot=g)
        return g

    def rms_stats(ss_ap, ms_ap, r_ap, st, n, eps):
        S.op("dve", lambda e: e.tensor_scalar(out=ms_ap, in0=ss_ap, scalar1=1.0 / n, scalar2=eps,
                                              op0=ALU.mult, op1=ALU.add), reads=[st], writes=[st])
        S.op("pool", lambda e: e.tensor_tensor(out=r_ap, in0=ms_ap, in1=cst[:, C_NH:C_NH + 1], op=ALU.pow),
             reads=[st, cst], writes=[st])

    def prep_a(x_ap, x_db, xt, g, hb, junk, st):
        if x_ap is not None:
            S.dma("sp", xt[:], x_ap, reads=[x_db], writes=[xt], slot=xt)
        S.op("act", lambda e: e.activation(out=junk[:], in_=xt[:], func=AF.Square, accum_out=st[:, 0:1]),
             reads=[xt], writes=[junk, st])
        rms_stats(st[:, 0:1], st[:, 1:2], st[:, 2:3], st, D, EPS)
        S.op("dve", lambda e: e.scalar_tensor_tensor(out=hb[:], in0=xt[:], scalar=st[:, 2:3], in1=g[:],
                                                     op0=ALU.mult, op1=ALU.mult), reads=[xt, st, g], writes=[hb])

    def prep_b(hb, trb, hT, dst_ap, evac):
        for kk in range(8):
            S.op("pe", lambda e: e.transpose(out=trb[:, kk, :], in_=hb[:, kk * 128:(kk + 1) * 128], identity=idb[:]),
                 reads=[hb, idb], writes=[trb], inc=(kk == 7))
        if evac == "act":
            S.op("act", lambda e: e.copy(out=dst_ap, in_=trb[:]), reads=[trb], writes=[hT])
        else:
            S.op("dve", lambda e: e.tensor_copy(out=dst_ap, in_=trb[:]), reads=[trb], writes=[hT])

    def load_w(ph, wt, rows, cols, tag):
        dst, b = wt
        nk = rows // 128
        t = ph.sb([128, nk, cols], BF16, tag)
        S.dma("sp", t[:], dst.rearrange("(k p) n -> p k n", p=128), reads=[b], writes=[t], slot=t)
        return t

    class Epi:
        def __init__(self, ph, gpost, factor, src_ap, src_bufs, dst_ap, dst_bufs, nslot=2, pump=0):
            self.ph = ph
            self.pump = pump
            self.ns = nslot
            self.g = gpost
            if factor != 1.0:
                S.op("act", lambda e: e.activation(out=gpost[:], in_=gpost[:], func=AF.Copy, scale=float(factor)),
                     reads=[gpost], writes=[gpost])
            self.xe = [ph.sb([128, D], F32, "xe") for _ in range(nslot)]
            self.ysb = [ph.sb([128, D], F32, "ysb") for _ in range(nslot)]
            self.st = [ph.sb([128, 8], F32, "est") for _ in range(nslot)]
            self.junk = ph.sb([128, 512], BF16, "ejunk")
            self.src_ap, self.src_bufs, self.dst_ap, self.dst_bufs = src_ap, src_bufs, dst_ap, dst_bufs
            self.loaded = 0

        def prefetch(self, upto):
            while self.loaded <= min(upto, NT - 1):
                T = self.loaded
                xe = self.xe[T % self.ns]
                S.dma("sp", xe[:], self.src_ap[T * 128:(T + 1) * 128, :], reads=[self.src_bufs[T]], writes=[xe], slot=xe)
                self.loaded += 1

        def half(self, T, hf, bank):
            st, ysb = self.st[T % self.ns], self.ysb[T % self.ns]
            S.op("act", lambda e: e.activation(out=self.junk[:], in_=bank[:], func=AF.Square, accum_out=st[:, hf:hf + 1]),
                 reads=[bank], writes=[self.junk, st])
            S.op("dve", lambda e: e.tensor_copy(out=ysb[:, hf * 512:(hf + 1) * 512], in_=bank[:]), reads=[bank], writes=[ysb])

        def finish(self, T):
            st = self.st[T % self.ns]
            S.op("dve", lambda e: e.tensor_tensor(out=st[:, 2:3], in0=st[:, 0:1], in1=st[:, 1:2], op=ALU.add),
                 reads=[st], writes=[st])
            rms_stats(st[:, 2:3], st[:, 3:4], st[:, 4:5], st, D, EPS)
            if T >= 1:
                self.apply(T - 1)
            self.prefetch(T - 1 + self.ns)

        def apply(self, T):
            st, ysb, xe = self.st[T % self.ns], self.ysb[T % self.ns], self.xe[T % self.ns]
            S.op("dve", lambda e: e.scalar_tensor_tensor(out=ysb[:], in0=ysb[:], scalar=st[:, 4:5], in1=self.g[:],
                                                         op0=ALU.mult, op1=ALU.mult), reads=[ysb, st, self.g], writes=[ysb])
            S.op("pool", lambda e: e.tensor_tensor(out=xe[:], in0=ysb[:], in1=xe[:], op=ALU.add),
                 reads=[ysb, xe], writes=[xe])
            S.dma("pool", self.dst_ap[T * 128:(T + 1) * 128, :], xe[:], reads=[xe], writes=[self.dst_bufs[T]], slot=xe)
            if self.pump:
                conv_pump(self.pump)

        def flush(self):
            self.apply(NT - 1)

    def ffn_phase(li, fi, src_ap, src_bufs, dst_ap, dst_bufs):
        w = k.W[li]
        wgu_d, wgu_b = w["wgu%d" % fi]
        with Phase(k, "f%d%d" % (li, fi)) as ph:
            gi = 0 if fi == 0 else 4
            gpre = load_gain(ph, I["norm_g"][li, gi:gi + 1, :], "gpre")
            gpost = load_gain(ph, I["norm_g"][li, gi + 1:gi + 2, :], "gpost")
            epi = Epi(ph, gpost, 0.5, src_ap, src_bufs, dst_ap, dst_bufs, pump=3)
            hT = [ph.sb([128, 8, 1024], BF16, "hT") for _ in range(2)]
            act = ph.sb([128, NKF, 1024], BF16, "act")
            wd = ph.sb([128, NKF, 1024], BF16, "wd")
            wg = [ph.sb([128, 8, 512], BF16, "wg") for _ in range(3)]
            xt = [ph.sb([128, D], F32, "xt") for _ in range(3)]
            hb = [ph.sb([128, D], BF16, "hb") for _ in range(4)]
            junk = ph.sb([128, D], BF16, "junk")
            sg = [ph.sb([128, 512], F32, "sg") for _ in range(2)]
            xl = [0]

            def ensure_x(upto):
                while xl[0] <= min(upto, NT - 1):
                    T = xl[0]
                    S.dma("sp", xt[T % 3][:], src_ap[T * 128:(T + 1) * 128, :], reads=[src_bufs[T]], writes=[xt[T % 3]], slot=xt[T % 3])
                    xl[0] += 1

            def prep_c(T):
                ensure_x(T)
                prep_a(None, None, xt[T % 3], gpre, hb[T % 4], junk, st[T % 4])
            st = [ph.sb([128, 4], F32, "st") for _ in range(4)]
            pg = [ph.ps([128, 512], F32, "pg") for _ in range(2)]
            pu = [ph.ps([128, 512], F32, "pu") for _ in range(2)]
            pd = [ph.ps([128, 512], F32, "pd") for _ in range(2)]
            ptr = [ph.ps([128, 8, 128], BF16, "ptr") for _ in range(2)]
            NG = 4
            NJG = 11
            wsrc = wgu_d.rearrange("(k p) n -> p k n", p=128)
            nload = [0]

            def wg_load(upto):
                while nload[0] <= min(upto, NG * NJG - 1):
                    idx = nload[0]
                    jg = idx % NJG
                    b = wg[idx % 3]
                    S.dma("sp", b[:, :, 0:256], wsrc[:, :, jg * 256:(jg + 1) * 256], reads=[wgu_b], writes=[b], slot=b)
                    S.dma("sp", b[:, :, 256:512], wsrc[:, :, DFF + jg * 256:DFF + (jg + 1) * 256], reads=[wgu_b], writes=[b], slot=b)
                    nload[0] += 1

            def prepA(G):
                for t in range(8 + 2):
                    if t < 8:
                        ensure_x(G * 8 + t + 1)
                        prep_c(G * 8 + t)
                    if t >= 2:
                        prepB_tile(G, t - 2)

            def prepB_tile(G, t):
                T = G * 8 + t
                prep_b(hb[T % 4], ptr[T % 2], hT[G % 2], hT[G % 2][:, :, t * 128:(t + 1) * 128], "act" if t % 2 == 0 else "dve")

            ensure_x(2)
            wg_load(2)
            prepA(0)
            cnt = [0]
            for G in range(NG):
                wdsrc = w["wd%d" % fi][0].rearrange("(k p) n -> p k n", p=128)
                for jg in range(NJG):
                    idx = G * NJG + jg
                    if G + 1 < NG and jg < 8:
                        ensure_x((G + 1) * 8 + jg + 1)
                    wg_load(idx + 2)
                    S.dma("sp", wd[:, 2 * jg:2 * jg + 2, :], wdsrc[:, 2 * jg:2 * jg + 2, :], reads=[w["wd%d" % fi][1]],
                          writes=[wd], slot=wd)
                    wb = wg[idx % 3]
                    if G + 1 < NG and jg < 8:
                        prep_c((G + 1) * 8 + jg)
                    for jj in range(2):
                        j = jg * 2 + jj
                        for hf in range(2):
                            s = cnt[0] % 2
                            cnt[0] += 1
                            for kk in range(8):
                                S.op("pe", lambda e: e.matmul(pg[s][:], lhsT=wb[:, kk, jj * 128:(jj + 1) * 128],
                                                              rhs=hT[G % 2][:, kk, hf * 512:(hf + 1) * 512],
                                                              start=(kk == 0), stop=(kk == 7)),
                                     reads=[wb, hT[G % 2]], writes=[pg[s]], inc=(kk == 7))
                            for kk in range(8):
                                S.op("pe", lambda e: e.matmul(pu[s][:], lhsT=wb[:, kk, 256 + jj * 128:256 + (jj + 1) * 128],
                                                              rhs=hT[G % 2][:, kk, hf * 512:(hf + 1) * 512],
                                                              start=(kk == 0), stop=(kk == 7)),
                                     reads=[wb, hT[G % 2]], writes=[pu[s]], inc=(kk == 7))
                            S.op("act", lambda e: e.activation(out=sg[s][:], in_=pg[s][:], func=AF.Silu),
                                 reads=[pg[s]], writes=[sg[s]])
                            S.op("dve", lambda e: e.tensor_tensor(out=act[:, j, hf * 512:(hf + 1) * 512], in0=sg[s][:],
                                                                  in1=pu[s][:], op=ALU.mult),
                                 reads=[sg[s], pu[s]], writes=[act])
                    if G + 1 < NG and jg < 8:
                        prepB_tile(G + 1, jg)
                if G == 0:
                    epi.prefetch(1)
                for t in range(8):
                    T = G * 8 + t
                    for hf in range(2):
                        for kk in range(NKF):
                            S.op("pe", lambda e: e.matmul(pd[hf][:], lhsT=act[:, kk, t * 128:(t + 1) * 128],
                                                          rhs=wd[:, kk, hf * 512:(hf + 1) * 512],
                                                          start=(kk == 0), stop=(kk == NKF - 1)),
                                 reads=[act, wd], writes=[pd[hf]], inc=(kk == NKF - 1))
                        epi.half(T, hf, pd[hf])
                    epi.finish(T)
            epi.flush()

    def mixout_phase(li, nch):
        w = k.W[li]
        with Phase(k, "o%d" % li) as ph:
            g3 = load_gain(ph, I["norm_g"][li, 3:4, :], "g3")
            epi = Epi(ph, g3, 1.0, XS, k.xs_t, XS, k.xs_t, nslot=4)
            wo = load_w(ph, w["wout"], nch * 128, D, "wo")
            at = [ph.sb([128, nch, 512], BF16, "at") for _ in range(2)]
            pd = [ph.ps([128, 512], F32, "pd") for _ in range(4)]
            asrc = ATT.rearrange("(c p) t -> p c t", p=128)
            epi.prefetch(3)
            warm(pd[0])
            def at_load(G):
                S.dma("sp", at[G % 2][:], asrc[:, 0:nch, G * 512:(G + 1) * 512], reads=[k.dATT], writes=[at[G % 2]], slot=at[G % 2])

            at_load(0)
            for G in range(8):
                a = at[G % 2]
                if G + 1 < 8:
                    at_load(G + 1)
                for t in range(4):
                    T = G * 4 + t
                    for hf in range(2):
                        bank = pd[(T % 2) * 2 + hf]
                        for c in range(nch):
                            S.op("pe", lambda e: e.matmul(bank[:], lhsT=a[:, c, t * 128:(t + 1) * 128],
                                                          rhs=wo[:, c, hf * 512:(hf + 1) * 512],
                                                          start=(c == 0), stop=(c == nch - 1)),
                                 reads=[a, wo], writes=[bank], inc=(c == nch - 1))
                        epi.half(T, hf, bank)
                    epi.finish(T)
            epi.flush()

    def mem_prep(ph, li):
        w = k.W[li]
        gm = load_gain(ph, I["mem_norm_g"][li:li + 1, :], "gm")
        wkv = load_w(ph, w["wkv"], D, 512, "wkv")
        memT = ph.sb([128, 8, 256], BF16, "memT")
        KmT = ph.sb([128, 2, 256], BF16, "KmT")
        Vm = ph.sb([128, 2, 4, 128], BF16, "Vm")
        xt = [ph.sb([128, D], F32, "mxt") for _ in range(2)]
        hb = [ph.sb([128, D], BF16, "mhb") for _ in range(2)]
        junk = ph.sb([128, D], BF16, "mjunk")
        st = [ph.sb([128, 4], F32, "mst") for _ in range(2)]
        ptr = ph.ps([128, 8, 128], BF16, "mptr")
        pk = ph.ps([128, 512], F32, "mpk")
        for t in range(2):
            prep_a(I["mem"][t * 128:(t + 1) * 128, :], k.dIN, xt[t], gm, hb[t], junk, st[t])
            prep_b(hb[t], ptr, memT, memT[:, :, t * 128:(t + 1) * 128], "act")
        S.op("pool", lambda e: e.memset(Vm[:], 1.0), writes=[Vm])
        for c in range(2):
            for kk in range(8):
                S.op("pe", lambda e: e.matmul(pk[:, 0:256], lhsT=wkv[:, kk, c * 128:(c + 1) * 128], rhs=memT[:, kk, :],
                                              start=(kk == 0), stop=(kk == 7)), reads=[wkv, memT], writes=[pk], inc=(kk == 7))
            S.op("act", lambda e: e.copy(out=KmT[:, c, :], in_=pk[:, 0:256]), reads=[pk], writes=[KmT])
        for kc in range(2):
            for kk in range(8):
                S.op("pe", lambda e: e.matmul(pk[:, 0:256], lhsT=memT[:, kk, kc * 128:(kc + 1) * 128], rhs=wkv[:, kk, 256:512],
                                              start=(kk == 0), stop=(kk == 7)), reads=[wkv, memT], writes=[pk], inc=(kk == 7))
            S.op("dve", lambda e: e.tensor_copy(out=Vm[:, kc, :, 0:64], in_=pk[:, 0:256].rearrange("p (h d) -> p h d", h=4)),
                 reads=[pk], writes=[Vm])
        return KmT, Vm, pk

    class Normalizer:
        def __init__(self, ph, n=512, with_pz=False):
            self.rec = [ph.sb([128, n], F32, "rec") for _ in range(2)]
            self.stg = [ph.sb([64, n], BF16, "stg") for _ in range(2)]
            self.pz = ph.ps([64, n], F32, "pz") if with_pz else None
            self.i = 0
            self.n = n

        def _fin(self, num_ap, den_ap, reads, row0, tok0, n):
            i = self.i
            self.i += 1
            rec, stg = self.rec[i % 2], self.stg[i % 2]
            S.op("act", lambda e: e.activation(out=rec[64:128, 0:n], in_=den_ap, func=AF.Ln), reads=reads, writes=[rec])
            S.op("act", lambda e: e.activation(out=rec[64:128, 0:n], in_=rec[64:128, 0:n], func=AF.Exp, scale=-1.0),
                 reads=[rec], writes=[rec])
            return rec, stg

        def from_psum(self, po, row0, tok0, n=None):
            n = n or self.n
            rec, stg = self._fin(None, po[64:128, 0:n], [po], row0, tok0, n)
            S.op("dve", lambda e: e.tensor_tensor(out=stg[:, 0:n], in0=po[0:64, 0:n], in1=rec[64:128, 0:n], op=ALU.mult),
                 reads=[po, rec], writes=[stg])
            S.dma("pool", ATT[row0:row0 + 64, tok0:tok0 + n], stg[:, 0:n], reads=[stg], writes=[k.dATT], slot=stg)

        def from_sbuf(self, ob, c0, row0, tok0, n=None):
            n = n or self.n
            rec, stg = self._fin(None, ob[64:128, c0:c0 + n], [ob], row0, tok0, n)
            S.op("dve", lambda e: e.tensor_copy(out=self.pz[:, 0:n], in_=ob[0:64, c0:c0 + n]), reads=[ob], writes=[self.pz])
            S.op("dve", lambda e: e.tensor_tensor(out=stg[:, 0:n], in0=self.pz[:, 0:n], in1=rec[64:128, 0:n], op=ALU.mult),
                 reads=[self.pz, rec], writes=[stg])
            S.dma("pool", ATT[row0:row0 + 64, tok0:tok0 + n], stg[:, 0:n], reads=[stg], writes=[k.dATT], slot=stg)

    def memattn_phase(li, row_off):
        with Phase(k, "m%d" % li) as ph:
            KmT, Vm, pk = mem_prep(ph, li)
            nrm = Normalizer(ph)
            qz = [ph.sb([128, 4, 512], BF16, "qz") for _ in range(2)]
            for b_ in qz:
                S.op("pool", lambda e: e.memset(b_[:], 0.0), writes=[b_])
            pT = [ph.sb([128, 512], BF16, "pT") for _ in range(6)]
            pst = [ph.ps([128, 512], F32, "pst") for _ in range(3)] + [pk]
            po = [ph.ps([128, 512], F32, "po") for _ in range(2)]
            warm(pst[0])
            items = [(G, h) for G in range(8) for h in range(4)]
            n = len(items)

            def s1(i):
                G, h = items[i]
                q = qz[G % 2]
                if h == 0:
                    for hh_ in range(4):
                        r0 = (hh_ % 2) * 64
                        S.dma("sp", q[r0:r0 + 64, hh_, :], QM[hh_ * 64:(hh_ + 1) * 64, G * 512:(G + 1) * 512],
                              reads=[k.dQM], writes=[q], slot=q)
                c = h // 2
                for kc in range(2):
                    sb_ = pst[(2 * i + kc) % 4]
                    p = pT[(2 * i + kc) % 6]
                    S.op("pe", lambda e: e.matmul(sb_[:], lhsT=KmT[:, c, kc * 128:(kc + 1) * 128], rhs=q[:, h, :], start=True, stop=True),
                         reads=[KmT, q], writes=[sb_])
                    S.op("act", lambda e: e.activation(out=p[:], in_=sb_[:], func=AF.Exp, scale=SCALE), reads=[sb_], writes=[p])

            def s2(i):
                G, h = items[i]
                o = po[i % 2]
                for kc in range(2):
                    p = pT[(2 * i + kc) % 6]
                    S.op("pe", lambda e: e.matmul(o[:], lhsT=Vm[:, kc, h, :], rhs=p[:], start=(kc == 0), stop=(kc == 1)),
                         reads=[Vm, p], writes=[o], inc=(kc == 1))
                nrm.from_psum(o, row_off + h * 64, G * 512)

            for i in range(n + 1):
                if i < n:
                    s1(i)
                if 0 <= i - 1 < n:
                    s2(i - 1)

    def mixA_phase(li, j):
        w = k.W[li]
        with Phase(k, "a%d" % li) as ph:
            g2 = load_gain(ph, I["norm_g"][li, 2:3, :], "g2")
            gv = ph.sb([128, 768], F32, "gv")
            S.dma("sp", gv[:], bc_rows(I["a_v_norm_g"][j:j + 1, :]), reads=[k.dIN], writes=[gv], slot=gv)
            win = load_w(ph, w["win"], D, 1792, "win")
            WcT = ph.sb([128, 12, 128], BF16, "WcT")
            Bt = ph.sb([128, 12, 64], F32, "Bt")
            sbT = ph.sb([128, 12], F32, "sbT")
            pu = [ph.ps([128, 512], F32, "pu") for _ in range(3)]
            pm = [ph.ps([128, 512], F32, "pm") for _ in range(2)]
            ptr = ph.ps([128, 8, 128], BF16, "ptr")
            ptg = ph.ps([128, 8, 128], BF16, "ptg")
            pq = ph.ps([128, 512], F32, "pq")
            with ExitStack() as es2:
                k.uid += 1
                swt = Buf(es2.enter_context(nc.sbuf_tensor("swt%d" % k.uid, [128, 12, 128], F32)), "swt")
                S.dma("sp", swt[:], I["a_spatial_w"][j].rearrange("g t s -> t g s"), reads=[k.dIN], writes=[swt], slot=swt)
                for g in range(12):
                    pb = pu[g % 3]
                    S.op("pe", lambda e: e.transpose(out=pb[:, 0:128], in_=swt[:, g, :], identity=cst[:, C_ID:C_ID + 128]),
                         reads=[swt, cst], writes=[pb])
                    S.op("dve", lambda e: e.tensor_tensor(out=WcT[:, g, :], in0=pb[:, 0:128], in1=cst[:, C_TRI:C_TRI + 128],
                                                          op=ALU.mult), reads=[pb, cst], writes=[WcT])
                with nc.allow_non_contiguous_dma(reason="tiny bias transpose"):
                    S.dma("sp", sbT[:], I["a_spatial_b"][j].rearrange("g t -> t g"), reads=[k.dIN], writes=[sbT], slot=sbT)
                S.op("dve", lambda e: e.tensor_copy(out=Bt[:], in_=AP(sbT.t[:].tensor, sbT.t[:].offset,
                                                                     [list(sbT.t[:].ap[0]), [1, 12], [0, 64]])),
                     reads=[sbT], writes=[Bt])
                S.barrier()
                S.release(swt)
            hT = [ph.sb([128, 8, 512], BF16, "hT") for _ in range(2)]
            xt = [ph.sb([128, D], F32, "xt") for _ in range(4)]
            hb = [ph.sb([128, D], BF16, "hb") for _ in range(4)]
            junk = ph.sb([128, D], BF16, "junk")
            st = [ph.sb([128, 4], F32, "st") for _ in range(4)]
            ug = [ph.sb([128, 768], F32, "ug") for _ in range(3)]
            vg = [ph.sb([128, 768], F32, "vg") for _ in range(3)]
            vj = ph.sb([128, 768], F32, "vj")
            vn = [ph.sb([128, 768], BF16, "vn") for _ in range(3)]
            ls = [ph.sb([128, 12], F32, "ls") for _ in range(3)]
            gt = [ph.sb([128, 768], F32, "gt") for _ in range(3)]
            gb = [ph.sb([128, 768], BF16, "gb") for _ in range(3)]
            gT = [ph.sb([128, 6, 512], BF16, "gT") for _ in range(2)]
            qs = [ph.sb([128, 512], BF16, "qs") for _ in range(2)]
            adst = ATT.rearrange("(c p) t -> p c t", p=128)
            Btf = Bt.t[:].rearrange("p g d -> p (g d)")
            warm(pm[0])

            def pA(T):
                prep_a(XS[T * 128:(T + 1) * 128, :], k.xs_t[T], xt[T % 4], g2, hb[T % 4], junk, st[T % 4])

            def pB(T):
                G_, t_ = divmod(T, 4)
                prep_b(hb[T % 4], ptr, hT[G_ % 2], hT[G_ % 2][:, :, t_ * 128:(t_ + 1) * 128], "act")

            for T_ in range(4):
                pA(T_)
            for T_ in range(4):
                pB(T_)

            def sA(T):
                G, t = divmod(T, 4)
                if t == 0 and G + 1 < 8:
                    for t_ in range(4):
                        pA((G + 1) * 4 + t_)
                if G + 1 < 8:
                    pB((G + 1) * 4 + t)
                h = hT[G % 2]
                u, v, l, vnb = ug[T % 3], vg[T % 3], ls[T % 3], vn[T % 3]
                for c in range(3):
                    for kk in range(8):
                        S.op("pe", lambda e: e.matmul(pu[c][:], lhsT=h[:, kk, t * 128:(t + 1) * 128],
                                                      rhs=win[:, kk, c * 512:(c + 1) * 512], start=(kk == 0), stop=(kk == 7)),
                             reads=[h, win], writes=[pu[c]], inc=(kk == 7))
                S.op("act", lambda e: e.activation(out=u[:, 0:512], in_=pu[0][:], func=AF.Gelu_apprx_tanh), reads=[pu[0]], writes=[u])
                S.op("act", lambda e: e.activation(out=u[:, 512:768], in_=pu[1][:, 0:256], func=AF.Gelu_apprx_tanh),
                     reads=[pu[1]], writes=[u])
                S.op("act", lambda e: e.activation(out=v[:, 0:256], in_=pu[1][:, 256:512], func=AF.Gelu_apprx_tanh,
                                                   accum_out=l[:, 0:1]), reads=[pu[1]], writes=[v, l])
                S.op("act", lambda e: e.activation(out=v[:, 256:768], in_=pu[2][:], func=AF.Gelu_apprx_tanh,
                                                   accum_out=l[:, 1:2]), reads=[pu[2]], writes=[v, l])
                S.op("dve", lambda e: e.scalar_tensor_tensor(out=vj[:], in0=v[:], scalar=1.0, in1=v[:], op0=ALU.mult,
                                                             op1=ALU.mult, accum_out=l[:, 2:3]), reads=[v], writes=[vj, l])
                S.op("dve", lambda e: e.tensor_tensor(out=l[:, 3:4], in0=l[:, 0:1], in1=l[:, 1:2], op=ALU.add), reads=[l], writes=[l])
                S.op("dve", lambda e: e.tensor_scalar(out=l[:, 4:5], in0=l[:, 3:4], scalar1=1.0 / 768, scalar2=None, op0=ALU.mult),
                     reads=[l], writes=[l])
                S.op("dve", lambda e: e.tensor_tensor(out=l[:, 5:6], in0=l[:, 4:5], in1=l[:, 4:5], op=ALU.mult), reads=[l], writes=[l])
                S.op("dve", lambda e: e.scalar_tensor_tensor(out=l[:, 6:7], in0=l[:, 2:3], scalar=1.0 / 768, in1=l[:, 5:6],
                                                             op0=ALU.mult, op1=ALU.subtract), reads=[l], writes=[l])
                S.op("dve", lambda e: e.tensor_scalar(out=l[:, 7:8], in0=l[:, 6:7], scalar1=LN_EPS, scalar2=None, op0=ALU.add),
                     reads=[l], writes=[l])
                S.op("pool", lambda e: e.tensor_tensor(out=l[:, 8:9], in0=l[:, 7:8], in1=cst[:, C_NH:C_NH + 1], op=ALU.pow),
                     reads=[l, cst], writes=[l])
                S.op("dve", lambda e: e.tensor_scalar(out=vj[:], in0=v[:], scalar1=l[:, 4:5], scalar2=l[:, 8:9],
                                                      op0=ALU.subtract, op1=ALU.mult), reads=[v, l], writes=[vj])
                S.op("pool", lambda e: e.tensor_tensor(out=vnb[:], in0=vj[:], in1=gv[:], op=ALU.mult), reads=[vj, gv], writes=[vnb])
                if t == 3:
                    for c in range(2):
                        for kk in range(8):
                            S.op("pe", lambda e: e.matmul(pq[:], lhsT=win[:, kk, 1536 + c * 128:1536 + (c + 1) * 128], rhs=h[:, kk, :],
                                                          start=(kk == 0), stop=(kk == 7)), reads=[win, h], writes=[pq], inc=(kk == 7))
                        q = qs[c]
                        S.op("dve", lambda e: e.tensor_copy(out=q[:], in_=pq[:]), reads=[pq], writes=[q])
                        S.dma("pool", QM[c * 128:(c + 1) * 128, G * 512:(G + 1) * 512], q[:], reads=[q], writes=[k.dQM], slot=q)

            def sB(T):
                G, t = divmod(T, 4)
                h = hT[G % 2]
                u, vnb = ug[T % 3], vn[T % 3]
                for g in range(12):
                    bank = pm[0] if g < 8 else pm[1]
                    col = (g % 8) * 64
                    S.op("pe", lambda e: e.matmul(bank[:, col:col + 64], lhsT=WcT[:, g, :], rhs=vnb[:, g * 64:(g + 1) * 64],
                                                  start=True, stop=True), reads=[WcT, vnb], writes=[bank], inc=(g == 7 or g == 11))
                gtt, gbb = gt[T % 3], gb[T % 3]
                S.op("dve", lambda e: e.tensor_tensor(out=gtt[:, 0:512], in0=pm[0][:], in1=Btf[:, 0:512], op=ALU.add),
                     reads=[pm[0], Bt], writes=[gtt])
                S.op("dve", lambda e: e.tensor_tensor(out=gtt[:, 512:768], in0=pm[1][:, 0:256], in1=Btf[:, 512:768], op=ALU.add),
                     reads=[pm[1], Bt], writes=[gtt])
                S.op("pool", lambda e: e.tensor_tensor(out=gbb[:], in0=gtt[:], in1=u[:], op=ALU.mult), reads=[gtt, u], writes=[gbb])

            def sC(T):
                G, t = divmod(T, 4)
                h = hT[G % 2]
                gbb = gb[T % 3]
                for c in range(6):
                    S.op("pe", lambda e: e.transpose(out=ptg[:, c, :], in_=gbb[:, c * 128:(c + 1) * 128], identity=idb[:]),
                         reads=[gbb, idb], writes=[ptg], inc=(c == 5))
                S.op("act", lambda e: e.copy(out=gT[G % 2][:, :, t * 128:(t + 1) * 128], in_=ptg[:, 0:6, :]),
                     reads=[ptg], writes=[gT[G % 2]])
                if t == 3:
                    S.dma("pool", adst[:, 0:6, G * 512:(G + 1) * 512], gT[G % 2][:], reads=[gT[G % 2]], writes=[k.dATT], slot=gT[G % 2])

            for T in range(NT + 2):
                if T < NT:
                    sA(T)
                if 0 <= T - 1 < NT:
                    sB(T - 1)
                if 0 <= T - 2 < NT:
                    sC(T - 2)

    def proj_phase(li, kind):
        w = k.W[li]
        nin = 2560 if kind == 1 else 2572
        qoff = 2304 if kind == 1 else 2316
        with Phase(k, "p%d" % li) as ph:
            g2 = load_gain(ph, I["norm_g"][li, 2:3, :], "g2")
            win = load_w(ph, w["win"], D, nin, "win")
            hT = [ph.sb([128, 8, 512], BF16, "hT") for _ in range(2)]
            xt = [ph.sb([128, D], F32, "xt") for _ in range(4)]
            hb = [ph.sb([128, D], BF16, "hb") for _ in range(4)]
            junk = ph.sb([128, D], BF16, "junk")
            st = [ph.sb([128, 4], F32, "st") for _ in range(4)]
            stg = [ph.sb([128, 512], BF16, "stg") for _ in range(3)]
            vst = [ph.sb([128, 768], BF16, "vst") for _ in range(2)]
            ptr = ph.ps([128, 8, 128], BF16, "ptr")
            pf = [ph.ps([128, 512], F32, "pf") for _ in range(3)]
            pf2 = [ph.ps([128, 512], F32, "pf2") for _ in range(2)]
            pv = [ph.ps([128, 512], F32, "pv") for _ in range(2)]
            if kind == 1:
                rmb = ph.sb([128, 128], BF16, "rmb")
                S.op("dve", lambda e: e.tensor_copy(out=rmb[:], in_=cst[:, C_RM:C_RM + 128]), reads=[cst], writes=[rmb])
                pbs = [ph.sb([128, 512], BF16, "pbs") for _ in range(2)]
                cosT = ph.sb([128, SEQ], F32, "cosT")
                sinT = ph.sb([128, SEQ], F32, "sinT")
                CW = 1024
                posi = ph.sb([128, CW], I32, "posi")
                ang = ph.sb([128, CW], F32, "ang")
                nn = ph.sb([128, CW], F32, "nn")
                ni = ph.sb([128, CW], I32, "ni")
                TWO_PI = 2.0 * np.pi
                HI = float(np.float32(TWO_PI))
                LO = float(TWO_PI - float(np.float32(TWO_PI)))

                def reduce_into(dstb, c0, shift):
                    dst = dstb.t[:, c0:c0 + CW]
                    S.op("dve", lambda e: e.tensor_scalar(out=nn[:], in0=ang[:], scalar1=float(shift), scalar2=1.0 / TWO_PI,
                                                          op0=ALU.add, op1=ALU.mult), reads=[ang], writes=[nn])
                    S.op("dve", lambda e: e.tensor_copy(out=ni[:], in_=nn[:]), reads=[nn], writes=[ni])
                    S.op("dve", lambda e: e.tensor_copy(out=nn[:], in_=ni[:]), reads=[ni], writes=[nn])
                    S.op("dve", lambda e: e.scalar_tensor_tensor(out=dst, in0=nn[:], scalar=-HI, in1=ang[:], op0=ALU.mult,
                                                                 op1=ALU.add), reads=[nn, ang], writes=[dstb])
                    S.op("dve", lambda e: e.scalar_tensor_tensor(out=dst, in0=nn[:], scalar=-LO, in1=dst, op0=ALU.mult,
                                                                 op1=ALU.add), reads=[nn, dstb], writes=[dstb])
                    if shift != 0.0:
                        S.op("dve", lambda e: e.tensor_scalar(out=dst, in0=dst, scalar1=float(shift), scalar2=None, op0=ALU.add),
                             reads=[dstb], writes=[dstb])
                    for sgn in (1.0, -1.0):
                        cmp = ALU.is_gt if sgn > 0 else ALU.is_lt
                        S.op("dve", lambda e: e.tensor_scalar(out=nn[:], in0=dst, scalar1=sgn * np.pi, scalar2=-sgn * TWO_PI,
                                                              op0=cmp, op1=ALU.mult), reads=[dstb], writes=[nn])
                        S.op("dve", lambda e: e.tensor_tensor(out=dst, in0=dst, in1=nn[:], op=ALU.add), reads=[dstb, nn], writes=[dstb])
                    S.op("dve", lambda e: e.tensor_scalar(out=dst, in0=dst, scalar1=3.1415925, scalar2=-3.1415925,
                                                          op0=ALU.min, op1=ALU.max), reads=[dstb], writes=[dstb])
                    S.op("act", lambda e: e.activation(out=dst, in_=dst, func=AF.Sin), reads=[dstb], writes=[dstb])

                for c0 in range(0, SEQ, CW):
                    S.dma("sp", posi[:], bc_rows(I["positions"][:, c0:c0 + CW]), reads=[k.dIN], writes=[posi], slot=posi)
                    S.op("dve", lambda e: e.tensor_copy(out=ang[:], in_=posi[:]), reads=[posi], writes=[ang])
                    S.op("dve", lambda e: e.tensor_scalar(out=ang[:], in0=ang[:], scalar1=cst[:, C_IF:C_IF + 1], scalar2=None,
                                                          op0=ALU.mult), reads=[ang, cst], writes=[ang])
                    reduce_into(sinT, c0, 0.0)
                    reduce_into(cosT, c0, np.pi / 2)
                rt = [ph.sb([128, 512], F32, "rt") for _ in range(2)]
            if kind == 2:
                fT = ph.sb([12, SEQ], F32, "fT")
                fb = ph.sb([12, 2], F32, "fb")
                with nc.allow_non_contiguous_dma(reason="tiny bias"):
                    S.dma("sp", fb[:, 0:1], I["c_forget_bias"].rearrange("o h -> h o"), reads=[k.dIN], writes=[fb], slot=fb)
                S.op("dve", lambda e: e.tensor_scalar(out=fb[:, 1:2], in0=fb[:, 0:1], scalar1=-1.0, scalar2=None, op0=ALU.mult),
                     reads=[fb], writes=[fb])
            nfm = 0
            warm(pf[0])
            def pA(T):
                prep_a(XS[T * 128:(T + 1) * 128, :], k.xs_t[T], xt[T % 4], g2, hb[T % 4], junk, st[T % 4])

            def pB(T):
                G_, t_ = divmod(T, 4)
                prep_b(hb[T % 4], ptr, hT[G_ % 2], hT[G_ % 2][:, :, t_ * 128:(t_ + 1) * 128], "act" if t_ % 2 else "dve")

            for T in range(4):
                pA(T)
            for T in range(4):
                pB(T)
            for G in range(8):
                h = hT[G % 2]
                if G + 1 < 8:
                    for t in range(4):
                        pA((G + 1) * 4 + t)
                blocks = [("q", c, c * 128) for c in range(6)] + [("k", c, 768 + c * 128) for c in range(6)] + \
                         [("m", c, qoff + c * 128) for c in range(2)]
                for bi, (what, c, col) in enumerate(blocks):
                    if G + 1 < 8 and 2 <= bi < 6:
                        pB((G + 1) * 4 + bi - 2)
                    p = pf[nfm % 3]
                    sg_ = stg[nfm % 3]
                    for kk in range(8):
                        S.op("pe", lambda e: e.matmul(p[:], lhsT=win[:, kk, col:col + 128], rhs=h[:, kk, :], start=(kk == 0), stop=(kk == 7)),
                             reads=[win, h], writes=[p], inc=(kk == 7))
                    if kind == 1 and what in ("q", "k"):
                        p2 = pf2[nfm % 2]
                        r = rt[nfm % 2]
                        pb_ = pbs[nfm % 2]
                        S.op("act", lambda e: e.copy(out=pb_[:], in_=p[:]), reads=[p], writes=[pb_])
                        S.op("pe", lambda e: e.matmul(p2[:], lhsT=rmb[:], rhs=pb_[:], start=True, stop=True), reads=[rmb, pb_], writes=[p2])
                        S.op("pool", lambda e: e.tensor_tensor(out=r[:], in0=pb_[:], in1=cosT[:, G * 512:(G + 1) * 512], op=ALU.mult),
                             reads=[pb_, cosT], writes=[r])
                        S.op("dve", lambda e: e.tensor_tensor(out=p2[:], in0=p2[:], in1=sinT[:, G * 512:(G + 1) * 512], op=ALU.mult),
                             reads=[p2, sinT], writes=[p2])
                        S.op("dve", lambda e: e.tensor_tensor(out=sg_[:], in0=p2[:], in1=r[:], op=ALU.add), reads=[p2, r], writes=[sg_])
                    else:
                        if nfm % 2 == 0:
                            S.op("act", lambda e: e.copy(out=sg_[:], in_=p[:]), reads=[p], writes=[sg_])
                        else:
                            S.op("dve", lambda e: e.tensor_copy(out=sg_[:], in_=p[:]), reads=[p], writes=[sg_])
                    dst, db = {"q": (QT, k.dQT), "k": (KT, k.dKT), "m": (QM, k.dQM)}[what]
                    S.dma("pool", dst[c * 128:(c + 1) * 128, G * 512:(G + 1) * 512], sg_[:], reads=[sg_], writes=[db], slot=sg_)
                    nfm += 1
                if kind == 2:
                    p = pf[nfm % 3]
                    nfm += 1
                    for kk in range(8):
                        S.op("pe", lambda e: e.matmul(p[0:12, :], lhsT=win[:, kk, 2304:2316], rhs=h[:, kk, :], start=(kk == 0), stop=(kk == 7)),
                             reads=[win, h], writes=[p], inc=(kk == 7))
                    S.op("dve", lambda e: e.tensor_copy(out=fT[:, G * 512:(G + 1) * 512], in_=p[0:12, :]), reads=[p], writes=[fT])
                for t in range(4):
                    T = G * 4 + t
                    vs = vst[T % 2]
                    for c, (c0, c1) in enumerate(((0, 512), (512, 768))):
                        p = pv[c]
                        for kk in range(8):
                            S.op("pe", lambda e: e.matmul(p[:, 0:c1 - c0], lhsT=h[:, kk, t * 128:(t + 1) * 128],
                                                          rhs=win[:, kk, 1536 + c0:1536 + c1], start=(kk == 0), stop=(kk == 7)),
                                 reads=[win, h], writes=[p], inc=(kk == 7))
                        if c == 0:
                            S.op("act", lambda e: e.copy(out=vs[:, c0:c1], in_=p[:, 0:c1 - c0]), reads=[p], writes=[vs])
                        else:
                            S.op("dve", lambda e: e.tensor_copy(out=vs[:, c0:c1], in_=p[:, 0:c1 - c0]), reads=[p], writes=[vs])
                    S.dma("pool", VV[T * 128:(T + 1) * 128, :], vs[:], reads=[vs], writes=[k.dVV], slot=vs)
            if kind == 2:
                ones = ph.sb([12, SEQ], F32, "ones")
                S.op("pool", lambda e: e.memset(ones[:], 1.0), writes=[ones])
                S.op("act", lambda e: e.activation(out=fT[:], in_=fT[:], func=AF.Exp, bias=fb[:, 1:2], scale=-1.0), reads=[fT, fb], writes=[fT])
                S.op("act", lambda e: e.activation(out=fT[:], in_=fT[:], func=AF.Ln, bias=1.0, scale=1.0), reads=[fT], writes=[fT])
                csum = ph.sb([12, SEQ], F32, "csum")
                S.op("dve", lambda e: e.tensor_tensor_scan(out=csum[:], data0=ones[:], data1=fT[:], initial=0.0, op0=ALU.mult, op1=ALU.add),
                     reads=[ones, fT], writes=[csum])
                S.dma("pool", CT, csum[:], reads=[csum], writes=[k.dCT], slot=csum)

    def fox_phase(li):
        with Phase(k, "c%d" % li) as ph:
            nrm = Normalizer(ph)
            q0 = [ph.sb([128, SEQ], BF16, "q0") for _ in range(2)]
            kTp = [ph.sb([128, SEQ], BF16, "kTp") for _ in range(2)]
            va = [ph.sb([128, NT, 128], BF16, "va") for _ in range(2)]
            csb = [ph.sb([128, SEQ], F32, "csb") for _ in range(2)]
            cska = ph.sb([128, 12 * NT], F32, "cska")
            ctr = ph.sb([128, 3, 128], F32, "ctr")
            arg = [ph.sb([128, 512], F32, "arg") for _ in range(4)]
            pT = [ph.sb([128, 512], BF16, "pT") for _ in range(4)]
            cm = [ph.sb([128, 512], F32, "cm") for _ in range(12)]
            pst = [ph.ps([128, 512], F32, "pst") for _ in range(4)]
            po = [ph.ps([128, 512], F32, "po") for _ in range(2)]
            for b_ in va:
                S.op("pool", lambda e: e.memset(b_[:], 1.0), writes=[b_])
            for b_ in q0:
                S.op("pool", lambda e: e.memset(b_[:], 0.0), writes=[b_])
            vsrc = VV.rearrange("(kb p) f -> p kb f", p=128)
            S.dma("sp", ctr[:], CT.rearrange("h (kb p) -> (h kb) p", p=128).rearrange("(c q) p -> q c p", q=128),
                  reads=[k.dCT], writes=[ctr], slot=ctr)
            for c in range(3):
                S.op("pe", lambda e: e.transpose(out=pst[c][:, 0:128], in_=ctr[:, c, :], identity=cst[:, C_ID:C_ID + 128]),
                     reads=[ctr, cst], writes=[pst[c]])
                S.op("dve", lambda e: e.tensor_copy(out=cska[:, c * 128:(c + 1) * 128], in_=pst[c][:, 0:128]), reads=[pst[c]], writes=[cska])

            def load_head(h):
                i = h % 2
                r0 = i * 64
                S.dma("sp", q0[i][r0:r0 + 64, :], QT[h * 64:(h + 1) * 64, :], reads=[k.dQT], writes=[q0[i]], slot=q0[i])
                if h % 2 == 0:
                    c = h // 2
                    S.dma("sp", kTp[c % 2][:], KT[c * 128:(c + 1) * 128, :], reads=[k.dKT], writes=[kTp[c % 2]], slot=kTp[c % 2])
                S.dma("sp", va[i][:, :, 0:64], vsrc[:, :, h * 64:(h + 1) * 64], reads=[k.dVV], writes=[va[i]], slot=va[i])
                S.dma("sp", csb[i][:], bc_rows(CT[h:h + 1, :]), reads=[k.dCT], writes=[csb[i]], slot=csb[i])

            items = []
            for h in range(12):
                for qg in range(8):
                    nkb = 4 * qg + 4
                    for kb in range(nkb):
                        items.append((h, qg, kb, nkb))
            n = len(items)

            def stA(i):
                h, qg, kb, nkb = items[i]
                kt = kTp[(h // 2) % 2]
                s_ = pst[i % 4]
                c0 = max(0, kb - 4 * qg) * 128
                S.op("pe", lambda e: e.matmul(s_[:, c0:512], lhsT=kt[:, kb * 128:(kb + 1) * 128], rhs=q0[h % 2][:, qg * 512 + c0:(qg + 1) * 512],
                                              start=True, stop=True), reads=[kt, q0[h % 2]], writes=[s_])

            def stM(i):
                h, qg, kb, nkb = items[i]
                jd = kb - 4 * qg
                if jd < 0:
                    return
                c0 = jd * 128
                m_ = cm[diag_idx[i] % 12]
                S.op("pool", lambda e: e.tensor_tensor(out=m_[:, c0:512], in0=csb[h % 2][:, qg * 512 + c0:(qg + 1) * 512],
                                                       in1=cst[:, C_MC + jd * 512 + c0:C_MC + (jd + 1) * 512], op=ALU.subtract),
                     reads=[csb[h % 2], cst], writes=[m_])

            def stB(i):
                h, qg, kb, nkb = items[i]
                hb_ = h % 2
                s_, a_, p = pst[i % 4], arg[i % 4], pT[i % 4]
                jd = kb - 4 * qg
                c0 = max(0, jd) * 128
                if jd >= 0:
                    m_ = cm[diag_idx[i] % 12]
                    S.op("dve", lambda e: e.scalar_tensor_tensor(out=a_[:, c0:512], in0=s_[:, c0:512], scalar=SCALE, in1=m_[:, c0:512],
                                                                 op0=ALU.mult, op1=ALU.subtract), reads=[s_, m_], writes=[a_])
                else:
                    S.op("dve", lambda e: e.scalar_tensor_tensor(out=a_[:], in0=s_[:], scalar=SCALE, in1=csb[hb_][:, qg * 512:(qg + 1) * 512],
                                                                 op0=ALU.mult, op1=ALU.subtract), reads=[s_, csb[hb_]], writes=[a_])
                S.op("act", lambda e: e.activation(out=p[:, c0:512], in_=a_[:, c0:512], func=AF.Exp, bias=cska[:, h * NT + kb:h * NT + kb + 1], scale=1.0),
                     reads=[a_, cska], writes=[p])

            def stC(i):
                h, qg, kb, nkb = items[i]
                hb_ = h % 2
                o = po[(h * 8 + qg) % 2]
                c0 = max(0, kb - 4 * qg) * 128
                S.op("pe", lambda e: e.matmul(o[:, c0:512], lhsT=va[hb_][:, kb, :], rhs=pT[i % 4][:, c0:512], start=(kb == 0), stop=(kb == nkb - 1)),
                     reads=[va[hb_], pT[i % 4]], writes=[o], inc=(kb == nkb - 1))
                if kb == nkb - 1:
                    nrm.from_psum(o, h * 64, qg * 512)
                    if qg == 7 and h + 2 < 12:
                        load_head(h + 2)

            diag_idx = {}
            for i_, it_ in enumerate(items):
                if it_[2] - 4 * it_[1] >= 0:
                    diag_idx[i_] = len(diag_idx)
            load_head(0)
            load_head(1)
            warm(pst[2])
            LEAD = 5
            for i in range(min(LEAD, n)):
                stM(i)
            for i in range(n + 3):
                if i < n:
                    stA(i)
                if 0 <= i - 1 < n:
                    stB(i - 1)
                if 0 <= i - 3 < n:
                    stC(i - 3)
                if i + LEAD < n:
                    stM(i + LEAD)

    def dil_phase(li):
        with Phase(k, "b%d" % li) as ph:
            nrm = Normalizer(ph, with_pz=True)
            q0 = [ph.sb([128, SEQ], BF16, "q0") for _ in range(2)]
            kTp = [ph.sb([128, SEQ], BF16, "kTp") for _ in range(2)]
            va = [ph.sb([128, NT, 128], BF16, "va") for _ in range(2)]
            oacc = [ph.sb([128, SEQ], F32, "oacc") for _ in range(2)]
            arg = [ph.sb([128, 256], F32, "arg") for _ in range(4)]
            pT = [ph.sb([128, 256], BF16, "pT") for _ in range(6)]
            pst = [ph.ps([128, 512], F32, "pst") for _ in range(4)]
            po = [ph.ps([128, 512], F32, "po") for _ in range(2)]
            for b_ in va:
                S.op("pool", lambda e: e.memset(b_[:], 1.0), writes=[b_])
            otmp = [ph.sb([128, 512], F32, "otmp") for _ in range(2)]
            acc_n = [0]
            dils = [1, 4, 16]
            order = [(j, g) for j in range(4) for g in range(3)]

            def load_head(ih):
                j, g = order[ih]
                h = 4 * g + j
                d = dils[g]
                i = ih % 2
                r0 = (h % 2) * 64
                z0 = 64 - r0
                S.op("pool", lambda e: e.memset(q0[i][z0:z0 + 64, :], 0.0), writes=[q0[i]])
                S.dma("sp", q0[i][r0:r0 + 64, :], QT[h * 64:(h + 1) * 64, :], reads=[k.dQT], writes=[q0[i]], slot=q0[i])
                c = h // 2
                S.dma("sp", kTp[i][:], KT[c * 128:(c + 1) * 128, :], reads=[k.dKT], writes=[kTp[i]], slot=kTp[i])
                nb = NT // d
                for r in range(d):
                    src = AP(VV.tensor, VV.offset + r * 768 + h * 64, [[d * 768, 128], [128 * d * 768, nb], [1, 64]])
                    S.dma("sp", va[i][:, r * nb:(r + 1) * nb, 0:64], src, reads=[k.dVV], writes=[va[i]], slot=va[i])

            load_head(0)
            warm(pst[3])
            for ih, (j, g) in enumerate(order):
                if ih + 1 < len(order):
                    load_head(ih + 1)
                d = dils[g]
                nb = NT // d
                i2 = ih % 2
                oa = oacc[j % 2]
                q, kk_, v = q0[i2], kTp[i2], va[i2]
                items = [(r, m) for r in range(d) for m in range(nb)]
                n = len(items)

                def cls(r, m0, cnt):
                    start = r + 128 * m0 * d
                    return slice(start, start + (cnt - 1) * d + 1, d)

                def stA(i):
                    r, m = items[i]
                    nq = 256 if m + 1 < nb else 128
                    s_ = pst[i % 4]
                    S.op("pe", lambda e: e.matmul(s_[:, 0:nq], lhsT=kk_[:, cls(r, m, 128)], rhs=q[:, cls(r, m, nq)], start=True, stop=True),
                         reads=[kk_, q], writes=[s_])

                def stB(i):
                    r, m = items[i]
                    nq = 256 if m + 1 < nb else 128
                    s_, a_, p = pst[i % 4], arg[i % 4], pT[i % 6]
                    S.op("dve", lambda e: e.scalar_tensor_tensor(out=a_[:, 0:nq], in0=s_[:, 0:nq], scalar=SCALE, in1=cst[:, C_MB:C_MB + nq],
                                                                 op0=ALU.mult, op1=ALU.add), reads=[s_, cst], writes=[a_])
                    S.op("act", lambda e: e.activation(out=p[:, 0:nq], in_=a_[:, 0:nq], func=AF.Exp), reads=[a_], writes=[p])

                def stC(i):
                    r, m = items[i]
                    o = po[(m // 4) % 2] if d < 16 else po[r % 2]
                    slot = (m % 4) * 128
                    if m > 0:
                        S.op("pe", lambda e: e.matmul(o[:, slot:slot + 128], lhsT=v[:, r * nb + m - 1, :], rhs=pT[(i - 1) % 6][:, 128:256],
                                                      start=True, stop=False), reads=[v, pT[(i - 1) % 6]], writes=[o], inc=False)
                    S.op("pe", lambda e: e.matmul(o[:, slot:slot + 128], lhsT=v[:, r * nb + m, :], rhs=pT[i % 6][:, 0:128],
                                                  start=(m == 0), stop=True), reads=[v, pT[i % 6]], writes=[o])
                    if m % 4 == 3 or m == nb - 1:
                        m0 = (m // 4) * 4
                        cntq = (m - m0 + 1) * 128
                        dst = oa[:, cls(r, m0, cntq)]
                        if g == 0:
                            S.op("act", lambda e: e.copy(out=dst, in_=o[:, 0:cntq]), reads=[o], writes=[oa])
                        else:
                            tb = otmp[acc_n[0] % 2]
                            acc_n[0] += 1
                            S.op("act", lambda e: e.copy(out=tb[:, 0:cntq], in_=o[:, 0:cntq]), reads=[o], writes=[tb])
                            S.op("pool", lambda e: e.tensor_tensor(out=dst, in0=tb[:, 0:cntq], in1=dst, op=ALU.add), reads=[tb, oa], writes=[oa])

                for i in range(n + 3):
                    if i < n:
                        stA(i)
                    if 0 <= i - 1 < n:
                        stB(i - 1)
                    if 0 <= i - 3 < n:
                        stC(i - 3)
                if g == 2:
                    for c in range(8):
                        nrm.from_sbuf(oa, c * 512, j * 64, c * 512)

    src_ap, src_bufs = I["x"], [k.dIN] * NT
    for li in range(n_layers):
        kind, j = kinds[li], li // 3
        ffn_phase(li, 0, src_ap, src_bufs, XS, k.xs_t)
        if kind == 0:
            mixA_phase(li, j)
            memattn_phase(li, 768)
            mixout_phase(li, 8)
        elif kind == 1:
            proj_phase(li, 1)
            dil_phase(li)
            memattn_phase(li, 256)
            mixout_phase(li, 4)
        else:
            proj_phase(li, 2)
            fox_phase(li)
            memattn_phase(li, 768)
            mixout_phase(li, 8)
        last = (li == n_layers - 1)
        ffn_phase(li, 1, XS, k.xs_t, out_d if last else XS, [Buf(None, "o%d" % i) for i in range(NT)] if last else k.xs_t)
        src_ap, src_bufs = XS, k.xs_t
        conv_flush(li + 1)

    S.barrier()
    gst.close()
    ctx.__exit__(None, None, None)
    return nc


_CACHE = {}


def kernel(**inputs):
    x = np.ascontiguousarray(np.asarray(inputs["x"], dtype=np.float32))
    mem = np.ascontiguousarray(np.asarray(inputs["mem"], dtype=np.float32))
    pos = np.ascontiguousarray(np.asarray(inputs["positions"], dtype=np.int32))
    B = x.shape[0]
    if "nc" not in _CACHE:
        _CACHE["nc"] = build()
    nc = _CACHE["nc"]
    consts = make_consts()
    shared = {}
    for name in ("norm_g", "mem_norm_g", "w_mem_kv", "ffn_w_gate_up", "ffn_w_down", "a_w_in", "a_spatial_w", "a_spatial_b",
                 "a_v_norm_g", "a_w_out", "b_w_in", "b_w_out", "c_w_in", "c_forget_bias", "c_w_out"):
        shared[name] = np.ascontiguousarray(np.asarray(inputs[name], dtype=np.float32))
    in_maps = []
    for b in range(B):
        m = dict(shared)
        m["x"] = x[b]
        m["mem"] = mem[b]
        m["positions"] = pos[b:b + 1]
        m["consts"] = consts
        in_maps.append(m)
    res = run_bass_kernel_spmd(nc, in_maps, core_ids=list(range(B)))
    return np.stack([np.asarray(r["out"], dtype=np.float32) for r in res.results], axis=0)
```

```python
import numpy as np
from contextlib import ExitStack
import concourse.bass as bass
import concourse.mybir as mybir
from concourse.bass_utils import run_bass_kernel_spmd

F32 = mybir.dt.float32
BF16 = mybir.dt.bfloat16
I32 = mybir.dt.int32
AF = mybir.ActivationFunctionType
ALU = mybir.AluOpType
AP = bass.AP

D = 1024
SEQ = 4096
NT = SEQ // 128
DFF = 2816
NKF = DFF // 128
NMEM = 256
DEPTH = 4
EPS = 1e-6
LN_EPS = 1e-5
SCALE = 0.125
NEG = -1e30

C_ID, C_SEL, C_TRI, C_MB, C_MC, C_IF, C_NH, C_RM, NCONST = 0, 128, 192, 320, 576, 2624, 2625, 2628, 2756


def make_consts():
    c = np.zeros((128, NCONST), np.float32)
    c[:, C_ID:C_ID + 128] = np.eye(128, dtype=np.float32)
    c[64, C_SEL:C_SEL + 64] = 1.0
    s = np.arange(128)[:, None]
    t = np.arange(128)[None, :]
    c[:, C_TRI:C_TRI + 128] = (t >= s).astype(np.float32)
    mb = np.zeros((128, 256), np.float32)
    mb[:, :128] = np.where(t >= s, 0.0, NEG)
    mb[:, 128:] = np.where(s >= t, 0.0, NEG)
    c[:, C_MB:C_MB + 256] = mb
    tt = np.arange(512)[None, :]
    for j in range(4):
        c[:, C_MC + j * 512:C_MC + (j + 1) * 512] = np.where(tt >= j * 128 + s, 0.0, NEG)
    invf = (np.float32(10000.0) ** (-np.arange(0, 64, 2, dtype=np.float32) / np.float32(64))).astype(np.float32)
    c[:, C_IF] = invf[np.arange(128) % 32]
    c[:, C_NH] = -0.5
    for m in range(128):
        if m % 64 < 32:
            c[m + 32, C_RM + m] = -1.0
        else:
            c[m - 32, C_RM + m] = 1.0
    return c


class Buf:
    __slots__ = ("t", "w", "rs", "name", "dsem", "dcnt", "excl")

    def __init__(self, t=None, name="", excl=False):
        self.t = t
        self.w = None
        self.rs = []
        self.name = name
        self.dsem = None
        self.dcnt = 0
        self.excl = excl

    def __getitem__(self, idx):
        return self.t[idx]


class Sched:
    def __init__(self, nc):
        self.nc = nc
        self.e = dict(pe=nc.tensor, act=nc.scalar, dve=nc.vector, pool=nc.gpsimd, sp=nc.sync)
        self.sem = {k: nc.alloc_semaphore("prog_" + k) for k in self.e}
        self.cnt = {k: 0 for k in self.e}
        self.seen = {k: {} for k in self.e}
        self.pend = {k: False for k in self.e}
        self.dma_pending = {}
        self.pool_sems = []
        self.nsem = 0

    def _wait(self, eng, tok, raw=True):
        if tok is None:
            return
        sem, val = tok
        key = id(sem)
        if self.seen[eng].get(key, 0) >= val:
            return
        if sem is self.sem[eng]:
            if eng == "pe" or eng == "sp" or (not raw) or val > self.cnt[eng]:
                return
        self.e[eng].wait_ge(sem, val)
        self.seen[eng][key] = val

    def _deps(self, eng, reads, writes):
        for b in reads:
            self._wait(eng, b.w, True)
            if b.excl:
                for r in b.rs:
                    self._wait(eng, r, False)
        for b in writes:
            self._wait(eng, b.w, False)
            for r in b.rs:
                self._wait(eng, r, False)

    def _post(self, tok, reads, writes):
        for b in reads:
            if b.excl:
                b.w = tok
                b.rs = []
            else:
                b.rs.append(tok)
                if len(b.rs) > 16:
                    best = {}
                    for sem, val in b.rs:
                        k = id(sem)
                        if k not in best or best[k][1] < val:
                            best[k] = (sem, val)
                    b.rs = list(best.values())
        for b in writes:
            b.w = tok
            b.rs = []

    def op(self, eng, fn, reads=(), writes=(), inc=True):
        self._deps(eng, reads, writes)
        tok = (self.sem[eng], self.cnt[eng] + 1)
        ins = fn(self.e[eng])
        if inc:
            ins.then_inc(self.sem[eng], 1)
            self.cnt[eng] += 1
            self.pend[eng] = False
        else:
            self.pend[eng] = True
        self._post(tok, reads, writes)
        return tok

    def _slot_sem(self, slot):
        if slot.dsem is None:
            if self.pool_sems:
                slot.dsem, slot.dcnt = self.pool_sems.pop()
            else:
                slot.dsem = self.nc.alloc_semaphore("dm%d" % self.nsem)
                slot.dcnt = 0
                self.nsem += 1

    def dma(self, eng, out_ap, in_ap, reads=(), writes=(), slot=None, **kw):
        self._slot_sem(slot)
        saved = []
        for b in writes:
            if b.w is not None and b.w[0] is slot.dsem and not b.rs:
                saved.append((b, b.w))
                b.w = None
        self._deps(eng, reads, writes)
        for b, w_ in saved:
            b.w = w_
        slot.dcnt += 16
        tok = (slot.dsem, slot.dcnt)
        self.e[eng].dma_start(out=out_ap, in_=in_ap, **kw).then_inc(slot.dsem, 16)
        self.dma_pending[id(slot.dsem)] = tok
        self._post(tok, reads, writes)
        return tok

    def release(self, slot):
        if slot.dsem is not None:
            self.pool_sems.append((slot.dsem, slot.dcnt))
            slot.dsem = None

    def barrier(self):
        for tok in self.dma_pending.values():
            self._wait("sp", tok)
        self.dma_pending = {}
        for k in self.e:
            if k != "sp" and self.cnt[k] > 0:
                assert not self.pend[k], k
                self._wait("sp", (self.sem[k], self.cnt[k]))
        ins = self.e["sp"].nop()
        ins.then_inc(self.sem["sp"], 1)
        self.cnt["sp"] += 1
        tok = (self.sem["sp"], self.cnt["sp"])
        for k in self.e:
            if k != "sp":
                self._wait(k, tok)
        return tok


class Phase:
    def __init__(self, K, name):
        self.K = K
        self.name = name
        self.es = ExitStack()
        self.bufs = []

    def __enter__(self):
        return self

    def sb(self, shape, dtype, tag="b"):
        K = self.K
        K.uid += 1
        t = self.es.enter_context(K.nc.sbuf_tensor("%s_%s_%d" % (self.name, tag, K.uid), list(shape), dtype))
        b = Buf(t, tag)
        self.bufs.append(b)
        return b

    def ps(self, shape, dtype, tag="p"):
        K = self.K
        K.uid += 1
        t = self.es.enter_context(K.nc.psum_tensor("%s_%s_%d" % (self.name, tag, K.uid), list(shape), dtype))
        b = Buf(t, tag, excl=True)
        self.bufs.append(b)
        return b

    def __exit__(self, *a):
        self.K.S.barrier()
        for b in self.bufs:
            self.K.S.release(b)
        self.es.close()
        return False


class K:
    pass


def dram_in(nc, name, shape, dtype=F32):
    return nc.dram_tensor(name, list(shape), dtype, kind="ExternalInput").ap()


DBG_OUT = set()


def dram_tmp(nc, name, shape, dtype):
    kind = "ExternalOutput" if name in DBG_OUT else "Internal"
    return nc.dram_tensor(name, list(shape), dtype, kind=kind).ap()


def bc_rows(ap2d_row, nparts=128):
    n = ap2d_row.shape[-1]
    return AP(ap2d_row.tensor, ap2d_row.offset, [[0, nparts], [1, n]])


def build(n_layers=DEPTH, dbg=None):
    nc = bass.Bass("TRN2", target_bir_lowering=False)
    k = K()
    k.nc = nc
    k.uid = 0
    I = {}
    I["x"] = dram_in(nc, "x", [SEQ, D])
    I["mem"] = dram_in(nc, "mem", [NMEM, D])
    I["positions"] = dram_in(nc, "positions", [1, SEQ], I32)
    I["norm_g"] = dram_in(nc, "norm_g", [DEPTH, 6, D])
    I["mem_norm_g"] = dram_in(nc, "mem_norm_g", [DEPTH, D])
    I["w_mem_kv"] = dram_in(nc, "w_mem_kv", [DEPTH, D, 512])
    I["ffn_w_gate_up"] = dram_in(nc, "ffn_w_gate_up", [DEPTH, 2, D, 2 * DFF])
    I["ffn_w_down"] = dram_in(nc, "ffn_w_down", [DEPTH, 2, DFF, D])
    I["a_w_in"] = dram_in(nc, "a_w_in", [2, D, 1792])
    I["a_spatial_w"] = dram_in(nc, "a_spatial_w", [2, 12, 128, 128])
    I["a_spatial_b"] = dram_in(nc, "a_spatial_b", [2, 12, 128])
    I["a_v_norm_g"] = dram_in(nc, "a_v_norm_g", [2, 768])
    I["a_w_out"] = dram_in(nc, "a_w_out", [2, 1024, D])
    I["b_w_in"] = dram_in(nc, "b_w_in", [1, D, 2560])
    I["b_w_out"] = dram_in(nc, "b_w_out", [1, 512, D])
    I["c_w_in"] = dram_in(nc, "c_w_in", [1, D, 2572])
    I["c_forget_bias"] = dram_in(nc, "c_forget_bias", [1, 12])
    I["c_w_out"] = dram_in(nc, "c_w_out", [1, 1024, D])
    I["consts"] = dram_in(nc, "consts", [128, NCONST])
    out_d = nc.dram_tensor("out", [SEQ, D], F32, kind="ExternalOutput").ap()
    k.I = I

    XS = dram_tmp(nc, "XS", [SEQ, D], F32)
    QT = dram_tmp(nc, "QT", [768, SEQ], BF16)
    KT = dram_tmp(nc, "KT", [768, SEQ], BF16)
    VV = dram_tmp(nc, "VV", [SEQ, 768], BF16)
    QM = dram_tmp(nc, "QM", [256, SEQ], BF16)
    ATT = dram_tmp(nc, "ATT", [1024, SEQ], BF16)
    CT = dram_tmp(nc, "CT", [12, SEQ], F32)
    k.XS, k.QT, k.KT, k.VV, k.QM, k.ATT, k.CT = XS, QT, KT, VV, QM, ATT, CT
    k.xs_t = [Buf(None, "xs%d" % i) for i in range(NT)]
    k.dQT, k.dKT, k.dVV, k.dQM, k.dATT, k.dCT = (Buf(None, n) for n in ("dQT", "dKT", "dVV", "dQM", "dATT", "dCT"))
    k.dIN = Buf(None, "din")

    ctx = nc.cleanup_on_exit()
    ctx.__enter__()
    S = Sched(nc)
    k.S = S

    kinds = [0, 1, 2, 0]
    k.W = []
    k.conv_jobs = []

    def add_weight(name, src_ap, rows, cols, li):
        dst = dram_tmp(nc, name, [rows, cols], BF16)
        b = Buf(None, name)
        r0 = 0
        while r0 < rows:
            r1 = min(rows, r0 + 64)
            k.conv_jobs.append((li, dst[r0:r1, :], src_ap[r0:r1, :], b))
            r0 = r1
        return (dst, b)

    for li in range(n_layers):
        kind, j = kinds[li], li // 3
        w = {}
        w["wgu0"] = add_weight("wgu%d0" % li, I["ffn_w_gate_up"][li, 0], D, 2 * DFF, li)
        w["wd0"] = add_weight("wd%d0" % li, I["ffn_w_down"][li, 0], DFF, D, li)
        w["wkv"] = add_weight("wkv%d" % li, I["w_mem_kv"][li], D, 512, li)
        if kind == 0:
            w["win"] = add_weight("win%d" % li, I["a_w_in"][j], D, 1792, li)
            w["wout"] = add_weight("wout%d" % li, I["a_w_out"][j], 1024, D, li)
        elif kind == 1:
            w["win"] = add_weight("win%d" % li, I["b_w_in"][0], D, 2560, li)
            w["wout"] = add_weight("wout%d" % li, I["b_w_out"][0], 512, D, li)
        else:
            w["win"] = add_weight("win%d" % li, I["c_w_in"][0], D, 2572, li)
            w["wout"] = add_weight("wout%d" % li, I["c_w_out"][0], 1024, D, li)
        w["wgu1"] = add_weight("wgu%d1" % li, I["ffn_w_gate_up"][li, 1], D, 2 * DFF, li)
        w["wd1"] = add_weight("wd%d1" % li, I["ffn_w_down"][li, 1], DFF, D, li)
        k.W.append(w)
    k.conv_pos = 0

    def conv_pump(n=1, upto_layer=None):
        while n > 0 and k.conv_pos < len(k.conv_jobs):
            li, dst, src, b = k.conv_jobs[k.conv_pos]
            if upto_layer is not None and li > upto_layer:
                return
            S.dma("pool", dst, src, reads=[k.dIN], writes=[b], slot=b, max_dma_last_dim=4096)
            k.conv_pos += 1
            n -= 1

    def conv_flush(layer):
        while k.conv_pos < len(k.conv_jobs) and k.conv_jobs[k.conv_pos][0] <= layer:
            conv_pump(1)

    k.conv_pump = conv_pump

    gst = ExitStack()
    cst = Buf(gst.enter_context(nc.sbuf_tensor("cst", [128, NCONST], F32)), "cst")
    idb = Buf(gst.enter_context(nc.sbuf_tensor("idb", [128, 128], BF16)), "idb")
    S.dma("sp", cst[:], I["consts"], reads=[k.dIN], writes=[cst], slot=cst)
    S.op("dve", lambda e: e.tensor_copy(out=idb[:], in_=cst[:, C_ID:C_ID + 128]), reads=[cst], writes=[idb])
    wrm = Buf(gst.enter_context(nc.sbuf_tensor("wrm", [128, 512], BF16)), "wrm")
    S.op("pool", lambda e: e.memset(wrm[:], 1.0), writes=[wrm])
    k.cst, k.idb = cst, idb
    S.barrier()

    def warm(bank, n=8):
        for i in range(n):
            S.op("pe", lambda e: e.matmul(bank[:, 0:512], lhsT=idb[:], rhs=wrm[:], start=True, stop=True),
                 reads=[idb, wrm], writes=[bank], inc=(i == n - 1))

    conv_flush(0)

    def load_gain(ph, row_ap, tag):
        g = ph.sb([128, D], F32, tag)
        S.dma("sp", g[:], bc_rows(row_ap), reads=[k.dIN], writes=[g], slot=g)
        return g

    def rms_stats(ss_ap, ms_ap, r_ap, st, n, eps):
        S.op("dve", lambda e: e.tensor_scalar(out=ms_ap, in0=ss_ap, scalar1=1.0 / n, scalar2=eps,
                                              op0=ALU.mult, op1=ALU.add), reads=[st], writes=[st])
        S.op("pool", lambda e: e.tensor_tensor(out=r_ap, in0=ms_ap, in1=cst[:, C_NH:C_NH + 1], op=ALU.pow),
             reads=[st, cst], writes=[st])

    def prep_a(x_ap, x_db, xt, g, hb, junk, st):
        if x_ap is not None:
            S.dma("sp", xt[:], x_ap, reads=[x_db], writes=[xt], slot=xt)
        S.op("act", lambda e: e.activation(out=junk[:], in_=xt[:], func=AF.Square, accum_out=st[:, 0:1]),
             reads=[xt], writes=[junk, st])
        rms_stats(st[:, 0:1], st[:, 1:2], st[:, 2:3], st, D, EPS)
        S.op("dve", lambda e: e.scalar_tensor_tensor(out=hb[:], in0=xt[:], scalar=st[:, 2:3], in1=g[:],
                                                     op0=ALU.mult, op1=ALU.mult), reads=[xt, st, g], writes=[hb])

    def prep_b(hb, trb, hT, dst_ap, evac):
        for kk in range(8):
            S.op("pe", lambda e: e.transpose(out=trb[:, kk, :], in_=hb[:, kk * 128:(kk + 1) * 128], identity=idb[:]),
                 reads=[hb, idb], writes=[trb], inc=(kk == 7))
        if evac == "act":
            S.op("act", lambda e: e.copy(out=dst_ap, in_=trb[:]), reads=[trb], writes=[hT])
        else:
            S.op("dve", lambda e: e.tensor_copy(out=dst_ap, in_=trb[:]), reads=[trb], writes=[hT])

    def load_w(ph, wt, rows, cols, tag):
        dst, b = wt
        nk = rows // 128
        t = ph.sb([128, nk, cols], BF16, tag)
        S.dma("sp", t[:], dst.rearrange("(k p) n -> p k n", p=128), reads=[b], writes=[t], slot=t)
        return t

    class Epi:
        def __init__(self, ph, gpost, factor, src_ap, src_bufs, dst_ap, dst_bufs, nslot=2, pump=0):
            self.ph = ph
            self.pump = pump
            self.ns = nslot
            self.g = gpost
            if factor != 1.0:
                S.op("act", lambda e: e.activation(out=gpost[:], in_=gpost[:], func=AF.Copy, scale=float(factor)),
                     reads=[gpost], writes=[gpost])
            self.xe = [ph.sb([128, D], F32, "xe") for _ in range(nslot)]
            self.ysb = [ph.sb([128, D], F32, "ysb") for _ in range(nslot)]
            self.st = [ph.sb([128, 8], F32, "est") for _ in range(nslot)]
            self.junk = ph.sb([128, 512], BF16, "ejunk")
            self.src_ap, self.src_bufs, self.dst_ap, self.dst_bufs = src_ap, src_bufs, dst_ap, dst_bufs
            self.loaded = 0

        def prefetch(self, upto):
            while self.loaded <= min(upto, NT - 1):
                T = self.loaded
                xe = self.xe[T % self.ns]
                S.dma("sp", xe[:], self.src_ap[T * 128:(T + 1) * 128, :], reads=[self.src_bufs[T]], writes=[xe], slot=xe)
                self.loaded += 1

        def half(self, T, hf, bank):
            st, ysb = self.st[T % self.ns], self.ysb[T % self.ns]
            S.op("act", lambda e: e.activation(out=self.junk[:], in_=bank[:], func=AF.Square, accum_out=st[:, hf:hf + 1]),
                 reads=[bank], writes=[self.junk, st])
            S.op("dve", lambda e: e.tensor_copy(out=ysb[:, hf * 512:(hf + 1) * 512], in_=bank[:]), reads=[bank], writes=[ysb])

        def finish(self, T):
            st = self.st[T % self.ns]
            S.op("dve", lambda e: e.tensor_tensor(out=st[:, 2:3], in0=st[:, 0:1], in1=st[:, 1:2], op=ALU.add),
                 reads=[st], writes=[st])
            rms_stats(st[:, 2:3], st[:, 3:4], st[:, 4:5], st, D, EPS)
            if T >= 1:
                self.apply(T - 1)
            self.prefetch(T - 1 + self.ns)

        def apply(self, T):
            st, ysb, xe = self.st[T % self.ns], self.ysb[T % self.ns], self.xe[T % self.ns]
            S.op("dve", lambda e: e.scalar_tensor_tensor(out=ysb[:], in0=ysb[:], scalar=st[:, 4:5], in1=self.g[:],
                                                         op0=ALU.mult, op1=ALU.mult), reads=[ysb, st, self.g], writes=[ysb])
            S.op("pool", lambda e: e.tensor_tensor(out=xe[:], in0=ysb[:], in1=xe[:], op=ALU.add),
                 reads=[ysb, xe], writes=[xe])
            S.dma("pool", self.dst_ap[T * 128:(T + 1) * 128, :], xe[:], reads=[xe], writes=[self.dst_bufs[T]], slot=xe)
            if self.pump:
                conv_pump(self.pump)

        def flush(self):
            self.apply(NT - 1)

    def ffn_phase(li, fi, src_ap, src_bufs, dst_ap, dst_bufs):
        w = k.W[li]
        wgu_d, wgu_b = w["wgu%d" % fi]
        with Phase(k, "f%d%d" % (li, fi)) as ph:
            gi = 0 if fi == 0 else 4
            gpre = load_gain(ph, I["norm_g"][li, gi:gi + 1, :], "gpre")
            gpost = load_gain(ph, I["norm_g"][li, gi + 1:gi + 2, :], "gpost")
            epi = Epi(ph, gpost, 0.5, src_ap, src_bufs, dst_ap, dst_bufs, pump=3)
            hT = [ph.sb([128, 8, 1024], BF16, "hT") for _ in range(2)]
            act = ph.sb([128, NKF, 1024], BF16, "act")
            wd = ph.sb([128, NKF, 1024], BF16, "wd")
            wg = [ph.sb([128, 8, 512], BF16, "wg") for _ in range(3)]
            xt = [ph.sb([128, D], F32, "xt") for _ in range(3)]
            hb = [ph.sb([128, D], BF16, "hb") for _ in range(4)]
            junk = ph.sb([128, D], BF16, "junk")
            sg = [ph.sb([128, 512], F32, "sg") for _ in range(2)]
            xl = [0]

            def ensure_x(upto):
                while xl[0] <= min(upto, NT - 1):
                    T = xl[0]
                    S.dma("sp", xt[T % 3][:], src_ap[T * 128:(T + 1) * 128, :], reads=[src_bufs[T]], writes=[xt[T % 3]], slot=xt[T % 3])
                    xl[0] += 1

            def prep_c(T):
                ensure_x(T)
                prep_a(None, None, xt[T % 3], gpre, hb[T % 4], junk, st[T % 4])
            st = [ph.sb([128, 4], F32, "st") for _ in range(4)]
            pg = [ph.ps([128, 512], F32, "pg") for _ in range(2)]
            pu = [ph.ps([128, 512], F32, "pu") for _ in range(2)]
            pd = [ph.ps([128, 512], F32, "pd") for _ in range(2)]
            ptr = [ph.ps([128, 8, 128], BF16, "ptr") for _ in range(2)]
            NG = 4
            NJG = 11
            wsrc = wgu_d.rearrange("(k p) n -> p k n", p=128)
            nload = [0]

            def wg_load(upto):
                while nload[0] <= min(upto, NG * NJG - 1):
                    idx = nload[0]
                    jg = idx % NJG
                    b = wg[idx % 3]
                    S.dma("sp", b[:, :, 0:256], wsrc[:, :, jg * 256:(jg + 1) * 256], reads=[wgu_b], writes=[b], slot=b)
                    S.dma("sp", b[:, :, 256:512], wsrc[:, :, DFF + jg * 256:DFF + (jg + 1) * 256], reads=[wgu_b], writes=[b], slot=b)
                    nload[0] += 1

            def prepA(G):
                for t in range(8 + 2):
                    if t < 8:
                        ensure_x(G * 8 + t + 1)
                        prep_c(G * 8 + t)
                    if t >= 2:
                        prepB_tile(G, t - 2)

            def prepB_tile(G, t):
                T = G * 8 + t
                prep_b(hb[T % 4], ptr[T % 2], hT[G % 2], hT[G % 2][:, :, t * 128:(t + 1) * 128], "act" if t % 2 == 0 else "dve")

            ensure_x(2)
            wg_load(2)
            prepA(0)
            cnt = [0]
            for G in range(NG):
                wdsrc = w["wd%d" % fi][0].rearrange("(k p) n -> p k n", p=128)
                for jg in range(NJG):
                    idx = G * NJG + jg
                    if G + 1 < NG and jg < 8:
                        ensure_x((G + 1) * 8 + jg + 1)
                    wg_load(idx + 2)
                    S.dma("sp", wd[:, 2 * jg:2 * jg + 2, :], wdsrc[:, 2 * jg:2 * jg + 2, :], reads=[w["wd%d" % fi][1]],
                          writes=[wd], slot=wd)
                    wb = wg[idx % 3]
                    if G + 1 < NG and jg < 8:
                        prep_c((G + 1) * 8 + jg)
                    for jj in range(2):
                        j = jg * 2 + jj
                        for hf in range(2):
                            s = cnt[0] % 2
                            cnt[0] += 1
                            for kk in range(8):
                                S.op("pe", lambda e: e.matmul(pg[s][:], lhsT=wb[:, kk, jj * 128:(jj + 1) * 128],
                                                              rhs=hT[G % 2][:, kk, hf * 512:(hf + 1) * 512],
                                                              start=(kk == 0), stop=(kk == 7)),
                                     reads=[wb, hT[G % 2]], writes=[pg[s]], inc=(kk == 7))
                            for kk in range(8):
                                S.op("pe", lambda e: e.matmul(pu[s][:], lhsT=wb[:, kk, 256 + jj * 128:256 + (jj + 1) * 128],
                                                              rhs=hT[G % 2][:, kk, hf * 512:(hf + 1) * 512],
                                                              start=(kk == 0), stop=(kk == 7)),
                                     reads=[wb, hT[G % 2]], writes=[pu[s]], inc=(kk == 7))
                            S.op("act", lambda e: e.activation(out=sg[s][:], in_=pg[s][:], func=AF.Silu),
                                 reads=[pg[s]], writes=[sg[s]])
                            S.op("dve", lambda e: e.tensor_tensor(out=act[:, j, hf * 512:(hf + 1) * 512], in0=sg[s][:],
                                                                  in1=pu[s][:], op=ALU.mult),
                                 reads=[sg[s], pu[s]], writes=[act])
                    if G + 1 < NG and jg < 8:
                        prepB_tile(G + 1, jg)
                if G == 0:
                    epi.prefetch(1)
                for t in range(8):
                    T = G * 8 + t
                    for hf in range(2):
                        for kk in range(NKF):
                            S.op("pe", lambda e: e.matmul(pd[hf][:], lhsT=act[:, kk, t * 128:(t + 1) * 128],
                                                          rhs=wd[:, kk, hf * 512:(hf + 1) * 512],
                                                          start=(kk == 0), stop=(kk == NKF - 1)),
                                 reads=[act, wd], writes=[pd[hf]], inc=(kk == NKF - 1))
                        epi.half(T, hf, pd[hf])
                    epi.finish(T)
            epi.flush()

    def mixout_phase(li, nch):
        w = k.W[li]
        with Phase(k, "o%d" % li) as ph:
            g3 = load_gain(ph, I["norm_g"][li, 3:4, :], "g3")
            epi = Epi(ph, g3, 1.0, XS, k.xs_t, XS, k.xs_t, nslot=4)
            wo = load_w(ph, w["wout"], nch * 128, D, "wo")
            at = [ph.sb([128, nch, 512], BF16, "at") for _ in range(2)]
            pd = [ph.ps([128, 512], F32, "pd") for _ in range(4)]
            asrc = ATT.rearrange("(c p) t -> p c t", p=128)
            epi.prefetch(3)
            warm(pd[0])
            def at_load(G):
                S.dma("sp", at[G % 2][:], asrc[:, 0:nch, G * 512:(G + 1) * 512], reads=[k.dATT], writes=[at[G % 2]], slot=at[G % 2])

            at_load(0)
            for G in range(8):
                a = at[G % 2]
                if G + 1 < 8:
                    at_load(G + 1)
                for t in range(4):
                    T = G * 4 + t
                    for hf in range(2):
                        bank = pd[(T % 2) * 2 + hf]
                        for c in range(nch):
                            S.op("pe", lambda e: e.matmul(bank[:], lhsT=a[:, c, t * 128:(t + 1) * 128],
                                                          rhs=wo[:, c, hf * 512:(hf + 1) * 512],
                                                          start=(c == 0), stop=(c == nch - 1)),
                                 reads=[a, wo], writes=[bank], inc=(c == nch - 1))
                        epi.half(T, hf, bank)
                    epi.finish(T)
            epi.flush()

    def mem_prep(ph, li):
        w = k.W[li]
        gm = load_gain(ph, I["mem_norm_g"][li:li + 1, :], "gm")
        wkv = load_w(ph, w["wkv"], D, 512, "wkv")
        memT = ph.sb([128, 8, 256], BF16, "memT")
        KmT = ph.sb([128, 2, 256], BF16, "KmT")
        Vm = ph.sb([128, 2, 4, 128], BF16, "Vm")
        xt = [ph.sb([128, D], F32, "mxt") for _ in range(2)]
        hb = [ph.sb([128, D], BF16, "mhb") for _ in range(2)]
        junk = ph.sb([128, D], BF16, "mjunk")
        st = [ph.sb([128, 4], F32, "mst") for _ in range(2)]
        ptr = ph.ps([128, 8, 128], BF16, "mptr")
        pk = ph.ps([128, 512], F32, "mpk")
        for t in range(2):
            prep_a(I["mem"][t * 128:(t + 1) * 128, :], k.dIN, xt[t], gm, hb[t], junk, st[t])
            prep_b(hb[t], ptr, memT, memT[:, :, t * 128:(t + 1) * 128], "act")
        S.op("pool", lambda e: e.memset(Vm[:], 1.0), writes=[Vm])
        for c in range(2):
            for kk in range(8):
                S.op("pe", lambda e: e.matmul(pk[:, 0:256], lhsT=wkv[:, kk, c * 128:(c + 1) * 128], rhs=memT[:, kk, :],
                                              start=(kk == 0), stop=(kk == 7)), reads=[wkv, memT], writes=[pk], inc=(kk == 7))
            S.op("act", lambda e: e.copy(out=KmT[:, c, :], in_=pk[:, 0:256]), reads=[pk], writes=[KmT])
        for kc in range(2):
            for kk in range(8):
                S.op("pe", lambda e: e.matmul(pk[:, 0:256], lhsT=memT[:, kk, kc * 128:(kc + 1) * 128], rhs=wkv[:, kk, 256:512],
                                              start=(kk == 0), stop=(kk == 7)), reads=[wkv, memT], writes=[pk], inc=(kk == 7))
            S.op("dve", lambda e: e.tensor_copy(out=Vm[:, kc, :, 0:64], in_=pk[:, 0:256].rearrange("p (h d) -> p h d", h=4)),
                 reads=[pk], writes=[Vm])
        return KmT, Vm, pk

    class Normalizer:
        def __init__(self, ph, n=512, with_pz=False):
            self.rec = [ph.sb([128, n], F32, "rec") for _ in range(2)]
            self.stg = [ph.sb([64, n], BF16, "stg") for _ in range(2)]
            self.pz = ph.ps([64, n], F32, "pz") if with_pz else None
            self.i = 0
            self.n = n

        def _fin(self, num_ap, den_ap, reads, row0, tok0, n):
            i = self.i
            self.i += 1
            rec, stg = self.rec[i % 2], self.stg[i % 2]
            S.op("act", lambda e: e.activation(out=rec[64:128, 0:n], in_=den_ap, func=AF.Ln), reads=reads, writes=[rec])
            S.op("act", lambda e: e.activation(out=rec[64:128, 0:n], in_=rec[64:128, 0:n], func=AF.Exp, scale=-1.0),
                 reads=[rec], writes=[rec])
            return rec, stg

        def from_psum(self, po, row0, tok0, n=None):
            n = n or self.n
            rec, stg = self._fin(None, po[64:128, 0:n], [po], row0, tok0, n)
            S.op("dve", lambda e: e.tensor_tensor(out=stg[:, 0:n], in0=po[0:64, 0:n], in1=rec[64:128, 0:n], op=ALU.mult),
                 reads=[po, rec], writes=[stg])
            S.dma("pool", ATT[row0:row0 + 64, tok0:tok0 + n], stg[:, 0:n], reads=[stg], writes=[k.dATT], slot=stg)

        def from_sbuf(self, ob, c0, row0, tok0, n=None):
            n = n or self.n
            rec, stg = self._fin(None, ob[64:128, c0:c0 + n], [ob], row0, tok0, n)
            S.op("dve", lambda e: e.tensor_copy(out=self.pz[:, 0:n], in_=ob[0:64, c0:c0 + n]), reads=[ob], writes=[self.pz])
            S.op("dve", lambda e: e.tensor_tensor(out=stg[:, 0:n], in0=self.pz[:, 0:n], in1=rec[64:128, 0:n], op=ALU.mult),
                 reads=[self.pz, rec], writes=[stg])
            S.dma("pool", ATT[row0:row0 + 64, tok0:tok0 + n], stg[:, 0:n], reads=[stg], writes=[k.dATT], slot=stg)

    def memattn_phase(li, row_off):
        with Phase(k, "m%d" % li) as ph:
            KmT, Vm, pk = mem_prep(ph, li)
            nrm = Normalizer(ph)
            qz = [ph.sb([128, 4, 512], BF16, "qz") for _ in range(2)]
            for b_ in qz:
                S.op("pool", lambda e: e.memset(b_[:], 0.0), writes=[b_])
            pT = [ph.sb([128, 512], BF16, "pT") for _ in range(6)]
            pst = [ph.ps([128, 512], F32, "pst") for _ in range(3)] + [pk]
            po = [ph.ps([128, 512], F32, "po") for _ in range(2)]
            warm(pst[0])
            items = [(G, h) for G in range(8) for h in range(4)]
            n = len(items)

            def s1(i):
                G, h = items[i]
                q = qz[G % 2]
                if h == 0:
                    for hh_ in range(4):
                        r0 = (hh_ % 2) * 64
                        S.dma("sp", q[r0:r0 + 64, hh_, :], QM[hh_ * 64:(hh_ + 1) * 64, G * 512:(G + 1) * 512],
                              reads=[k.dQM], writes=[q], slot=q)
                c = h // 2
                for kc in range(2):
                    sb_ = pst[(2 * i + kc) % 4]
                    p = pT[(2 * i + kc) % 6]
                    S.op("pe", lambda e: e.matmul(sb_[:], lhsT=KmT[:, c, kc * 128:(kc + 1) * 128], rhs=q[:, h, :], start=True, stop=True),
                         reads=[KmT, q], writes=[sb_])
                    S.op("act", lambda e: e.activation(out=p[:], in_=sb_[:], func=AF.Exp, scale=SCALE), reads=[sb_], writes=[p])

            def s2(i):
                G, h = items[i]
                o = po[i % 2]
                for kc in range(2):
                    p = pT[(2 * i + kc) % 6]
                    S.op("pe", lambda e: e.matmul(o[:], lhsT=Vm[:, kc, h, :], rhs=p[:], start=(kc == 0), stop=(kc == 1)),
                         reads=[Vm, p], writes=[o], inc=(kc == 1))
                nrm.from_psum(o, row_off + h * 64, G * 512)

            for i in range(n + 1):
                if i < n:
                    s1(i)
                if 0 <= i - 1 < n:
                    s2(i - 1)

    def mixA_phase(li, j):
        w = k.W[li]
        with Phase(k, "a%d" % li) as ph:
            g2 = load_gain(ph, I["norm_g"][li, 2:3, :], "g2")
            gv = ph.sb([128, 768], F32, "gv")
            S.dma("sp", gv[:], bc_rows(I["a_v_norm_g"][j:j + 1, :]), reads=[k.dIN], writes=[gv], slot=gv)
            win = load_w(ph, w["win"], D, 1792, "win")
            WcT = ph.sb([128, 12, 128], BF16, "WcT")
            Bt = ph.sb([128, 12, 64], F32, "Bt")
            sbT = ph.sb([128, 12], F32, "sbT")
            pu = [ph.ps([128, 512], F32, "pu") for _ in range(3)]
            pm = [ph.ps([128, 512], F32, "pm") for _ in range(2)]
            ptr = ph.ps([128, 8, 128], BF16, "ptr")
            ptg = ph.ps([128, 8, 128], BF16, "ptg")
            pq = ph.ps([128, 512], F32, "pq")
            with ExitStack() as es2:
                k.uid += 1
                swt = Buf(es2.enter_context(nc.sbuf_tensor("swt%d" % k.uid, [128, 12, 128], F32)), "swt")
                S.dma("sp", swt[:], I["a_spatial_w"][j].rearrange("g t s -> t g s"), reads=[k.dIN], writes=[swt], slot=swt)
                for g in range(12):
                    pb = pu[g % 3]
                    S.op("pe", lambda e: e.transpose(out=pb[:, 0:128], in_=swt[:, g, :], identity=cst[:, C_ID:C_ID + 128]),
                         reads=[swt, cst], writes=[pb])
                    S.op("dve", lambda e: e.tensor_tensor(out=WcT[:, g, :], in0=pb[:, 0:128], in1=cst[:, C_TRI:C_TRI + 128],
                                                          op=ALU.mult), reads=[pb, cst], writes=[WcT])
                with nc.allow_non_contiguous_dma(reason="tiny bias transpose"):
                    S.dma("sp", sbT[:], I["a_spatial_b"][j].rearrange("g t -> t g"), reads=[k.dIN], writes=[sbT], slot=sbT)
                S.op("dve", lambda e: e.tensor_copy(out=Bt[:], in_=AP(sbT.t[:].tensor, sbT.t[:].offset,
                                                                     [list(sbT.t[:].ap[0]), [1, 12], [0, 64]])),
                     reads=[sbT], writes=[Bt])
                S.barrier()
                S.release(swt)
            hT = [ph.sb([128, 8, 512], BF16, "hT") for _ in range(2)]
            xt = [ph.sb([128, D], F32, "xt") for _ in range(4)]
            hb = [ph.sb([128, D], BF16, "hb") for _ in range(4)]
            junk = ph.sb([128, D], BF16, "junk")
            st = [ph.sb([128, 4], F32, "st") for _ in range(4)]
            ug = [ph.sb([128, 768], F32, "ug") for _ in range(3)]
            vg = [ph.sb([128, 768], F32, "vg") for _ in range(3)]
            vj = ph.sb([128, 768], F32, "vj")
            vn = [ph.sb([128, 768], BF16, "vn") for _ in range(3)]
            ls = [ph.sb([128, 12], F32, "ls") for _ in range(3)]
            gt = [ph.sb([128, 768], F32, "gt") for _ in range(3)]
            gb = [ph.sb([128, 768], BF16, "gb") for _ in range(3)]
            gT = [ph.sb([128, 6, 512], BF16, "gT") for _ in range(2)]
            qs = [ph.sb([128, 512], BF16, "qs") for _ in range(2)]
            adst = ATT.rearrange("(c p) t -> p c t", p=128)
            Btf = Bt.t[:].rearrange("p g d -> p (g d)")
            warm(pm[0])

            def pA(T):
                prep_a(XS[T * 128:(T + 1) * 128, :], k.xs_t[T], xt[T % 4], g2, hb[T % 4], junk, st[T % 4])

            def pB(T):
                G_, t_ = divmod(T, 4)
                prep_b(hb[T % 4], ptr, hT[G_ % 2], hT[G_ % 2][:, :, t_ * 128:(t_ + 1) * 128], "act")

            for T_ in range(4):
                pA(T_)
            for T_ in range(4):
                pB(T_)

            def sA(T):
                G, t = divmod(T, 4)
                if t == 0 and G + 1 < 8:
                    for t_ in range(4):
                        pA((G + 1) * 4 + t_)
                if G + 1 < 8:
                    pB((G + 1) * 4 + t)
                h = hT[G % 2]
                u, v, l, vnb = ug[T % 3], vg[T % 3], ls[T % 3], vn[T % 3]
                for c in range(3):
                    for kk in range(8):
                        S.op("pe", lambda e: e.matmul(pu[c][:], lhsT=h[:, kk, t * 128:(t + 1) * 128],
                                                      rhs=win[:, kk, c * 512:(c + 1) * 512], start=(kk == 0), stop=(kk == 7)),
                             reads=[h, win], writes=[pu[c]], inc=(kk == 7))
                S.op("act", lambda e: e.activation(out=u[:, 0:512], in_=pu[0][:], func=AF.Gelu_apprx_tanh), reads=[pu[0]], writes=[u])
                S.op("act", lambda e: e.activation(out=u[:, 512:768], in_=pu[1][:, 0:256], func=AF.Gelu_apprx_tanh),
                     reads=[pu[1]], writes=[u])
                S.op("act", lambda e: e.activation(out=v[:, 0:256], in_=pu[1][:, 256:512], func=AF.Gelu_apprx_tanh,
                                                   accum_out=l[:, 0:1]), reads=[pu[1]], writes=[v, l])
                S.op("act", lambda e: e.activation(out=v[:, 256:768], in_=pu[2][:], func=AF.Gelu_apprx_tanh,
                                                   accum_out=l[:, 1:2]), reads=[pu[2]], writes=[v, l])
                S.op("dve", lambda e: e.scalar_tensor_tensor(out=vj[:], in0=v[:], scalar=1.0, in1=v[:], op0=ALU.mult,
                                                             op1=ALU.mult, accum_out=l[:, 2:3]), reads=[v], writes=[vj, l])
                S.op("dve", lambda e: e.tensor_tensor(out=l[:, 3:4], in0=l[:, 0:1], in1=l[:, 1:2], op=ALU.add), reads=[l], writes=[l])
                S.op("dve", lambda e: e.tensor_scalar(out=l[:, 4:5], in0=l[:, 3:4], scalar1=1.0 / 768, scalar2=None, op0=ALU.mult),
                     reads=[l], writes=[l])
                S.op("dve", lambda e: e.tensor_tensor(out=l[:, 5:6], in0=l[:, 4:5], in1=l[:, 4:5], op=ALU.mult), reads=[l], writes=[l])
                S.op("dve", lambda e: e.scalar_tensor_tensor(out=l[:, 6:7], in0=l[:, 2:3], scalar=1.0 / 768, in1=l[:, 5:6],
                                                             op0=ALU.mult, op1=ALU.subtract), reads=[l], writes=[l])
                S.op("dve", lambda e: e.tensor_scalar(out=l[:, 7:8], in0=l[:, 6:7], scalar1=LN_EPS, scalar2=None, op0=ALU.add),
                     reads=[l], writes=[l])
                S.op("pool", lambda e: e.tensor_tensor(out=l[:, 8:9], in0=l[:, 7:8], in1=cst[:, C_NH:C_NH + 1], op=ALU.pow),
                     reads=[l, cst], writes=[l])
                S.op("dve", lambda e: e.tensor_scalar(out=vj[:], in0=v[:], scalar1=l[:, 4:5], scalar2=l[:, 8:9],
                                                      op0=ALU.subtract, op1=ALU.mult), reads=[v, l], writes=[vj])
                S.op("pool", lambda e: e.tensor_tensor(out=vnb[:], in0=vj[:], in1=gv[:], op=ALU.mult), reads=[vj, gv], writes=[vnb])
                if t == 3:
                    for c in range(2):
                        for kk in range(8):
                            S.op("pe", lambda e: e.matmul(pq[:], lhsT=win[:, kk, 1536 + c * 128:1536 + (c + 1) * 128], rhs=h[:, kk, :],
                                                          start=(kk == 0), stop=(kk == 7)), reads=[win, h], writes=[pq], inc=(kk == 7))
                        q = qs[c]
                        S.op("dve", lambda e: e.tensor_copy(out=q[:], in_=pq[:]), reads=[pq], writes=[q])
                        S.dma("pool", QM[c * 128:(c + 1) * 128, G * 512:(G + 1) * 512], q[:], reads=[q], writes=[k.dQM], slot=q)

            def sB(T):
                G, t = divmod(T, 4)
                h = hT[G % 2]
                u, vnb = ug[T % 3], vn[T % 3]
                for g in range(12):
                    bank = pm[0] if g < 8 else pm[1]
                    col = (g % 8) * 64
                    S.op("pe", lambda e: e.matmul(bank[:, col:col + 64], lhsT=WcT[:, g, :], rhs=vnb[:, g * 64:(g + 1) * 64],
                                                  start=True, stop=True), reads=[WcT, vnb], writes=[bank], inc=(g == 7 or g == 11))
                gtt, gbb = gt[T % 3], gb[T % 3]
                S.op("dve", lambda e: e.tensor_tensor(out=gtt[:, 0:512], in0=pm[0][:], in1=Btf[:, 0:512], op=ALU.add),
                     reads=[pm[0], Bt], writes=[gtt])
                S.op("dve", lambda e: e.tensor_tensor(out=gtt[:, 512:768], in0=pm[1][:, 0:256], in1=Btf[:, 512:768], op=ALU.add),
                     reads=[pm[1], Bt], writes=[gtt])
                S.op("pool", lambda e: e.tensor_tensor(out=gbb[:], in0=gtt[:], in1=u[:], op=ALU.mult), reads=[gtt, u], writes=[gbb])

            def sC(T):
                G, t = divmod(T, 4)
                h = hT[G % 2]
                gbb = gb[T % 3]
                for c in range(6):
                    S.op("pe", lambda e: e.transpose(out=ptg[:, c, :], in_=gbb[:, c * 128:(c + 1) * 128], identity=idb[:]),
                         reads=[gbb, idb], writes=[ptg], inc=(c == 5))
                S.op("act", lambda e: e.copy(out=gT[G % 2][:, :, t * 128:(t + 1) * 128], in_=ptg[:, 0:6, :]),
                     reads=[ptg], writes=[gT[G % 2]])
                if t == 3:
                    S.dma("pool", adst[:, 0:6, G * 512:(G + 1) * 512], gT[G % 2][:], reads=[gT[G % 2]], writes=[k.dATT], slot=gT[G % 2])

            for T in range(NT + 2):
                if T < NT:
                    sA(T)
                if 0 <= T - 1 < NT:
                    sB(T - 1)
                if 0 <= T - 2 < NT:
                    sC(T - 2)

    def proj_phase(li, kind):
        w = k.W[li]
        nin = 2560 if kind == 1 else 2572
        qoff = 2304 if kind == 1 else 2316
        with Phase(k, "p%d" % li) as ph:
            g2 = load_gain(ph, I["norm_g"][li, 2:3, :], "g2")
            win = load_w(ph, w["win"], D, nin, "win")
            hT = [ph.sb([128, 8, 512], BF16, "hT") for _ in range(2)]
            xt = [ph.sb([128, D], F32, "xt") for _ in range(4)]
            hb = [ph.sb([128, D], BF16, "hb") for _ in range(4)]
            junk = ph.sb([128, D], BF16, "junk")
            st = [ph.sb([128, 4], F32, "st") for _ in range(4)]
            stg = [ph.sb([128, 512], BF16, "stg") for _ in range(3)]
            vst = [ph.sb([128, 768], BF16, "vst") for _ in range(2)]
            ptr = ph.ps([128, 8, 128], BF16, "ptr")
            pf = [ph.ps([128, 512], F32, "pf") for _ in range(3)]
            pf2 = [ph.ps([128, 512], F32, "pf2") for _ in range(2)]
            pv = [ph.ps([128, 512], F32, "pv") for _ in range(2)]
            if kind == 1:
                rmb = ph.sb([128, 128], BF16, "rmb")
                S.op("dve", lambda e: e.tensor_copy(out=rmb[:], in_=cst[:, C_RM:C_RM + 128]), reads=[cst], writes=[rmb])
                pbs = [ph.sb([128, 512], BF16, "pbs") for _ in range(2)]
                cosT = ph.sb([128, SEQ], F32, "cosT")
                sinT = ph.sb([128, SEQ], F32, "sinT")
                CW = 1024
                posi = ph.sb([128, CW], I32, "posi")
                ang = ph.sb([128, CW], F32, "ang")
                nn = ph.sb([128, CW], F32, "nn")
                ni = ph.sb([128, CW], I32, "ni")
                TWO_PI = 2.0 * np.pi
                HI = float(np.float32(TWO_PI))
                LO = float(TWO_PI - float(np.float32(TWO_PI)))

                def reduce_into(dstb, c0, shift):
                    dst = dstb.t[:, c0:c0 + CW]
                    S.op("dve", lambda e: e.tensor_scalar(out=nn[:], in0=ang[:], scalar1=float(shift), scalar2=1.0 / TWO_PI,
                                                          op0=ALU.add, op1=ALU.mult), reads=[ang], writes=[nn])
                    S.op("dve", lambda e: e.tensor_copy(out=ni[:], in_=nn[:]), reads=[nn], writes=[ni])
                    S.op("dve", lambda e: e.tensor_copy(out=nn[:], in_=ni[:]), reads=[ni], writes=[nn])
                    S.op("dve", lambda e: e.scalar_tensor_tensor(out=dst, in0=nn[:], scalar=-HI, in1=ang[:], op0=ALU.mult,
                                                                 op1=ALU.add), reads=[nn, ang], writes=[dstb])
                    S.op("dve", lambda e: e.scalar_tensor_tensor(out=dst, in0=nn[:], scalar=-LO, in1=dst, op0=ALU.mult,
                                                                 op1=ALU.add), reads=[nn, dstb], writes=[dstb])
                    if shift != 0.0:
                        S.op("dve", lambda e: e.tensor_scalar(out=dst, in0=dst, scalar1=float(shift), scalar2=None, op0=ALU.add),
                             reads=[dstb], writes=[dstb])
                    for sgn in (1.0, -1.0):
                        cmp = ALU.is_gt if sgn > 0 else ALU.is_lt
                        S.op("dve", lambda e: e.tensor_scalar(out=nn[:], in0=dst, scalar1=sgn * np.pi, scalar2=-sgn * TWO_PI,
                                                              op0=cmp, op1=ALU.mult), reads=[dstb], writes=[nn])
                        S.op("dve", lambda e: e.tensor_tensor(out=dst, in0=dst, in1=nn[:], op=ALU.add), reads=[dstb, nn], writes=[dstb])
                    S.op("dve", lambda e: e.tensor_scalar(out=dst, in0=dst, scalar1=3.1415925, scalar2=-3.1415925,
                                                          op0=ALU.min, op1=ALU.max), reads=[dstb], writes=[dstb])
                    S.op("act", lambda e: e.activation(out=dst, in_=dst, func=AF.Sin), reads=[dstb], writes=[dstb])

                for c0 in range(0, SEQ, CW):
                    S.dma("sp", posi[:], bc_rows(I["positions"][:, c0:c0 + CW]), reads=[k.dIN], writes=[posi], slot=posi)
                    S.op("dve", lambda e: e.tensor_copy(out=ang[:], in_=posi[:]), reads=[posi], writes=[ang])
                    S.op("dve", lambda e: e.tensor_scalar(out=ang[:], in0=ang[:], scalar1=cst[:, C_IF:C_IF + 1], scalar2=None,
                                                          op0=ALU.mult), reads=[ang, cst], writes=[ang])
                    reduce_into(sinT, c0, 0.0)
                    reduce_into(cosT, c0, np.pi / 2)
                rt = [ph.sb([128, 512], F32, "rt") for _ in range(2)]
            if kind == 2:
                fT = ph.sb([12, SEQ], F32, "fT")
                fb = ph.sb([12, 2], F32, "fb")
                with nc.allow_non_contiguous_dma(reason="tiny bias"):
                    S.dma("sp", fb[:, 0:1], I["c_forget_bias"].rearrange("o h -> h o"), reads=[k.dIN], writes=[fb], slot=fb)
                S.op("dve", lambda e: e.tensor_scalar(out=fb[:, 1:2], in0=fb[:, 0:1], scalar1=-1.0, scalar2=None, op0=ALU.mult),
                     reads=[fb], writes=[fb])
            nfm = 0
            warm(pf[0])
            def pA(T):
                prep_a(XS[T * 128:(T + 1) * 128, :], k.xs_t[T], xt[T % 4], g2, hb[T % 4], junk, st[T % 4])

            def pB(T):
                G_, t_ = divmod(T, 4)
                prep_b(hb[T % 4], ptr, hT[G_ % 2], hT[G_ % 2][:, :, t_ * 128:(t_ + 1) * 128], "act" if t_ % 2 else "dve")

            for T in range(4):
                pA(T)
            for T in range(4):
                pB(T)
            for G in range(8):
                h = hT[G % 2]
                if G + 1 < 8:
                    for t in range(4):
                        pA((G + 1) * 4 + t)
                blocks = [("q", c, c * 128) for c in range(6)] + [("k", c, 768 + c * 128) for c in range(6)] + \
                         [("m", c, qoff + c * 128) for c in range(2)]
                pend = []

                def tail(what, c, p, sg_, idx):
                    if kind == 1 and what in ("q", "k"):
                        p2 = pf2[idx % 2]
                        r = rt[idx % 2]
                        pb_ = pbs[idx % 2]
                        S.op("act", lambda e: e.copy(out=pb_[:], in_=p[:]), reads=[p], writes=[pb_])
                        S.op("pe", lambda e: e.matmul(p2[:], lhsT=rmb[:], rhs=pb_[:], start=True, stop=True), reads=[rmb, pb_], writes=[p2])
                        S.op("pool", lambda e: e.tensor_tensor(out=r[:], in0=pb_[:], in1=cosT[:, G * 512:(G + 1) * 512], op=ALU.mult),
                             reads=[pb_, cosT], writes=[r])
                        S.op("dve", lambda e: e.tensor_tensor(out=p2[:], in0=p2[:], in1=sinT[:, G * 512:(G + 1) * 512], op=ALU.mult),
                             reads=[p2, sinT], writes=[p2])
                        S.op("dve", lambda e: e.tensor_tensor(out=sg_[:], in0=p2[:], in1=r[:], op=ALU.add), reads=[p2, r], writes=[sg_])
                    else:
                        if idx % 2 == 0:
                            S.op("act", lambda e: e.copy(out=sg_[:], in_=p[:]), reads=[p], writes=[sg_])
                        else:
                            S.op("dve", lambda e: e.tensor_copy(out=sg_[:], in_=p[:]), reads=[p], writes=[sg_])
                    dst, db = {"q": (QT, k.dQT), "k": (KT, k.dKT), "m": (QM, k.dQM)}[what]
                    S.dma("pool", dst[c * 128:(c + 1) * 128, G * 512:(G + 1) * 512], sg_[:], reads=[sg_], writes=[db], slot=sg_)

                for bi, (what, c, col) in enumerate(blocks):
                    if G + 1 < 8 and 2 <= bi < 6:
                        pB((G + 1) * 4 + bi - 2)
                    p = pf[nfm % 3]
                    sg_ = stg[nfm % 3]
                    for kk in range(8):
                        S.op("pe", lambda e: e.matmul(p[:], lhsT=win[:, kk, col:col + 128], rhs=h[:, kk, :], start=(kk == 0), stop=(kk == 7)),
                             reads=[win, h], writes=[p], inc=(kk == 7))
                    if pend:
                        tail(*pend.pop())
                    pend.append((what, c, p, sg_, nfm))
                    nfm += 1
                tail(*pend.pop())
                if kind == 2:
                    p = pf[nfm % 3]
                    nfm += 1
                    for kk in range(8):
                        S.op("pe", lambda e: e.matmul(p[0:12, :], lhsT=win[:, kk, 2304:2316], rhs=h[:, kk, :], start=(kk == 0), stop=(kk == 7)),
                             reads=[win, h], writes=[p], inc=(kk == 7))
                    S.op("dve", lambda e: e.tensor_copy(out=fT[:, G * 512:(G + 1) * 512], in_=p[0:12, :]), reads=[p], writes=[fT])
                for t in range(4):
                    T = G * 4 + t
                    vs = vst[T % 2]
                    for c, (c0, c1) in enumerate(((0, 512), (512, 768))):
                        p = pv[c]
                        for kk in range(8):
                            S.op("pe", lambda e: e.matmul(p[:, 0:c1 - c0], lhsT=h[:, kk, t * 128:(t + 1) * 128],
                                                          rhs=win[:, kk, 1536 + c0:1536 + c1], start=(kk == 0), stop=(kk == 7)),
                                 reads=[win, h], writes=[p], inc=(kk == 7))
                        if c == 0:
                            S.op("act", lambda e: e.copy(out=vs[:, c0:c1], in_=p[:, 0:c1 - c0]), reads=[p], writes=[vs])
                        else:
                            S.op("dve", lambda e: e.tensor_copy(out=vs[:, c0:c1], in_=p[:, 0:c1 - c0]), reads=[p], writes=[vs])
                    S.dma("pool", VV[T * 128:(T + 1) * 128, :], vs[:], reads=[vs], writes=[k.dVV], slot=vs)
            if kind == 2:
                ones = ph.sb([12, SEQ], F32, "ones")
                S.op("pool", lambda e: e.memset(ones[:], 1.0), writes=[ones])
                S.op("act", lambda e: e.activation(out=fT[:], in_=fT[:], func=AF.Exp, bias=fb[:, 1:2], scale=-1.0), reads=[fT, fb], writes=[fT])
                S.op("act", lambda e: e.activation(out=fT[:], in_=fT[:], func=AF.Ln, bias=1.0, scale=1.0), reads=[fT], writes=[fT])
                csum = ph.sb([12, SEQ], F32, "csum")
                S.op("dve", lambda e: e.tensor_tensor_scan(out=csum[:], data0=ones[:], data1=fT[:], initial=0.0, op0=ALU.mult, op1=ALU.add),
                     reads=[ones, fT], writes=[csum])
                S.dma("pool", CT, csum[:], reads=[csum], writes=[k.dCT], slot=csum)

    def fox_phase(li):
        with Phase(k, "c%d" % li) as ph:
            nrm = Normalizer(ph)
            q0 = [ph.sb([128, SEQ], BF16, "q0") for _ in range(2)]
            kTp = [ph.sb([128, SEQ], BF16, "kTp") for _ in range(2)]
            va = [ph.sb([128, NT, 128], BF16, "va") for _ in range(2)]
            csb = [ph.sb([128, SEQ], F32, "csb") for _ in range(2)]
            cska = ph.sb([128, 12 * NT], F32, "cska")
            ctr = ph.sb([128, 3, 128], F32, "ctr")
            arg = [ph.sb([128, 512], F32, "arg") for _ in range(4)]
            pT = [ph.sb([128, 512], BF16, "pT") for _ in range(4)]
            cm = [ph.sb([128, 512], F32, "cm") for _ in range(12)]
            pst = [ph.ps([128, 512], F32, "pst") for _ in range(4)]
            po = [ph.ps([128, 512], F32, "po") for _ in range(2)]
            for b_ in va:
                S.op("pool", lambda e: e.memset(b_[:], 1.0), writes=[b_])
            for b_ in q0:
                S.op("pool", lambda e: e.memset(b_[:], 0.0), writes=[b_])
            vsrc = VV.rearrange("(kb p) f -> p kb f", p=128)
            S.dma("sp", ctr[:], CT.rearrange("h (kb p) -> (h kb) p", p=128).rearrange("(c q) p -> q c p", q=128),
                  reads=[k.dCT], writes=[ctr], slot=ctr)
            for c in range(3):
                S.op("pe", lambda e: e.transpose(out=pst[c][:, 0:128], in_=ctr[:, c, :], identity=cst[:, C_ID:C_ID + 128]),
                     reads=[ctr, cst], writes=[pst[c]])
                S.op("dve", lambda e: e.tensor_copy(out=cska[:, c * 128:(c + 1) * 128], in_=pst[c][:, 0:128]), reads=[pst[c]], writes=[cska])

            def load_head(h):
                i = h % 2
                r0 = i * 64
                S.dma("sp", q0[i][r0:r0 + 64, :], QT[h * 64:(h + 1) * 64, :], reads=[k.dQT], writes=[q0[i]], slot=q0[i])
                if h % 2 == 0:
                    c = h // 2
                    S.dma("sp", kTp[c % 2][:], KT[c * 128:(c + 1) * 128, :], reads=[k.dKT], writes=[kTp[c % 2]], slot=kTp[c % 2])
                S.dma("sp", va[i][:, :, 0:64], vsrc[:, :, h * 64:(h + 1) * 64], reads=[k.dVV], writes=[va[i]], slot=va[i])
                S.dma("sp", csb[i][:], bc_rows(CT[h:h + 1, :]), reads=[k.dCT], writes=[csb[i]], slot=csb[i])

            items = []
            for h in range(12):
                for qg in range(8):
                    nkb = 4 * qg + 4
                    for kb in range(nkb):
                        items.append((h, qg, kb, nkb))
            n = len(items)

            def stA(i):
                h, qg, kb, nkb = items[i]
                kt = kTp[(h // 2) % 2]
                s_ = pst[i % 4]
                c0 = max(0, kb - 4 * qg) * 128
                S.op("pe", lambda e: e.matmul(s_[:, c0:512], lhsT=kt[:, kb * 128:(kb + 1) * 128], rhs=q0[h % 2][:, qg * 512 + c0:(qg + 1) * 512],
                                              start=True, stop=True), reads=[kt, q0[h % 2]], writes=[s_])

            def stM(i):
                h, qg, kb, nkb = items[i]
                jd = kb - 4 * qg
                if jd < 0:
                    return
                c0 = jd * 128
                m_ = cm[diag_idx[i] % 12]
                S.op("pool", lambda e: e.tensor_tensor(out=m_[:, c0:512], in0=csb[h % 2][:, qg * 512 + c0:(qg + 1) * 512],
                                                       in1=cst[:, C_MC + jd * 512 + c0:C_MC + (jd + 1) * 512], op=ALU.subtract),
                     reads=[csb[h % 2], cst], writes=[m_])

            def stB(i):
                h, qg, kb, nkb = items[i]
                hb_ = h % 2
                s_, a_, p = pst[i % 4], arg[i % 4], pT[i % 4]
                jd = kb - 4 * qg
                c0 = max(0, jd) * 128
                if jd >= 0:
                    m_ = cm[diag_idx[i] % 12]
                    S.op("dve", lambda e: e.scalar_tensor_tensor(out=a_[:, c0:512], in0=s_[:, c0:512], scalar=SCALE, in1=m_[:, c0:512],
                                                                 op0=ALU.mult, op1=ALU.subtract), reads=[s_, m_], writes=[a_])
                else:
                    S.op("dve", lambda e: e.scalar_tensor_tensor(out=a_[:], in0=s_[:], scalar=SCALE, in1=csb[hb_][:, qg * 512:(qg + 1) * 512],
                                                                 op0=ALU.mult, op1=ALU.subtract), reads=[s_, csb[hb_]], writes=[a_])
                S.op("act", lambda e: e.activation(out=p[:, c0:512], in_=a_[:, c0:512], func=AF.Exp, bias=cska[:, h * NT + kb:h * NT + kb + 1], scale=1.0),
                     reads=[a_, cska], writes=[p])

            def stC(i):
                h, qg, kb, nkb = items[i]
                hb_ = h % 2
                o = po[(h * 8 + qg) % 2]
                c0 = max(0, kb - 4 * qg) * 128
                S.op("pe", lambda e: e.matmul(o[:, c0:512], lhsT=va[hb_][:, kb, :], rhs=pT[i % 4][:, c0:512], start=(kb == 0), stop=(kb == nkb - 1)),
                     reads=[va[hb_], pT[i % 4]], writes=[o], inc=(kb == nkb - 1))
                if kb == nkb - 1:
                    nrm.from_psum(o, h * 64, qg * 512)
                    if qg == 7 and h + 2 < 12:
                        load_head(h + 2)

            diag_idx = {}
            for i_, it_ in enumerate(items):
                if it_[2] - 4 * it_[1] >= 0:
                    diag_idx[i_] = len(diag_idx)
            load_head(0)
            load_head(1)
            warm(pst[2])
            LEAD = 5
            for i in range(min(LEAD, n)):
                stM(i)
            for i in range(n + 3):
                if i < n:
                    stA(i)
                if 0 <= i - 1 < n:
                    stB(i - 1)
                if 0 <= i - 3 < n:
                    stC(i - 3)
                if i + LEAD < n:
                    stM(i + LEAD)

    def dil_phase(li):
        with Phase(k, "b%d" % li) as ph:
            nrm = Normalizer(ph, with_pz=True)
            q0 = [ph.sb([128, SEQ], BF16, "q0") for _ in range(2)]
            kTp = [ph.sb([128, SEQ], BF16, "kTp") for _ in range(2)]
            va = [ph.sb([128, NT, 128], BF16, "va") for _ in range(2)]
            oacc = [ph.sb([128, SEQ], F32, "oacc") for _ in range(2)]
            arg = [ph.sb([128, 256], F32, "arg") for _ in range(4)]
            pT = [ph.sb([128, 256], BF16, "pT") for _ in range(6)]
            pst = [ph.ps([128, 512], F32, "pst") for _ in range(4)]
            po = [ph.ps([128, 512], F32, "po") for _ in range(2)]
            for b_ in va:
                S.op("pool", lambda e: e.memset(b_[:], 1.0), writes=[b_])
            otmp = [ph.sb([128, 512], F32, "otmp") for _ in range(2)]
            acc_n = [0]
            dils = [1, 4, 16]
            order = [(j, g) for j in range(4) for g in range(3)]

            def load_head(ih):
                j, g = order[ih]
                h = 4 * g + j
                d = dils[g]
                i = ih % 2
                r0 = (h % 2) * 64
                z0 = 64 - r0
                S.op("pool", lambda e: e.memset(q0[i][z0:z0 + 64, :], 0.0), writes=[q0[i]])
                S.dma("sp", q0[i][r0:r0 + 64, :], QT[h * 64:(h + 1) * 64, :], reads=[k.dQT], writes=[q0[i]], slot=q0[i])
                c = h // 2
                S.dma("sp", kTp[i][:], KT[c * 128:(c + 1) * 128, :], reads=[k.dKT], writes=[kTp[i]], slot=kTp[i])
                nb = NT // d
                for r in range(d):
                    src = AP(VV.tensor, VV.offset + r * 768 + h * 64, [[d * 768, 128], [128 * d * 768, nb], [1, 64]])
                    S.dma("sp", va[i][:, r * nb:(r + 1) * nb, 0:64], src, reads=[k.dVV], writes=[va[i]], slot=va[i])

            load_head(0)
            warm(pst[3])
            for ih, (j, g) in enumerate(order):
                if ih + 1 < len(order):
                    load_head(ih + 1)
                d = dils[g]
                nb = NT // d
                i2 = ih % 2
                oa = oacc[j % 2]
                q, kk_, v = q0[i2], kTp[i2], va[i2]
                items = [(r, m) for r in range(d) for m in range(nb)]
                n = len(items)

                def cls(r, m0, cnt):
                    start = r + 128 * m0 * d
                    return slice(start, start + (cnt - 1) * d + 1, d)

                def stA(i):
                    r, m = items[i]
                    nq = 256 if m + 1 < nb else 128
                    s_ = pst[i % 4]
                    S.op("pe", lambda e: e.matmul(s_[:, 0:nq], lhsT=kk_[:, cls(r, m, 128)], rhs=q[:, cls(r, m, nq)], start=True, stop=True),
                         reads=[kk_, q], writes=[s_])

                def stB(i):
                    r, m = items[i]
                    nq = 256 if m + 1 < nb else 128
                    s_, a_, p = pst[i % 4], arg[i % 4], pT[i % 6]
                    S.op("dve", lambda e: e.scalar_tensor_tensor(out=a_[:, 0:nq], in0=s_[:, 0:nq], scalar=SCALE, in1=cst[:, C_MB:C_MB + nq],
                                                                 op0=ALU.mult, op1=ALU.add), reads=[s_, cst], writes=[a_])
                    S.op("act", lambda e: e.activation(out=p[:, 0:nq], in_=a_[:, 0:nq], func=AF.Exp), reads=[a_], writes=[p])

                def stC(i):
                    r, m = items[i]
                    o = po[(m // 4) % 2] if d < 16 else po[r % 2]
                    slot = (m % 4) * 128
                    if m > 0:
                        S.op("pe", lambda e: e.matmul(o[:, slot:slot + 128], lhsT=v[:, r * nb + m - 1, :], rhs=pT[(i - 1) % 6][:, 128:256],
                                                      start=True, stop=False), reads=[v, pT[(i - 1) % 6]], writes=[o], inc=False)
                    S.op("pe", lambda e: e.matmul(o[:, slot:slot + 128], lhsT=v[:, r * nb + m, :], rhs=pT[i % 6][:, 0:128],
                                                  start=(m == 0), stop=True), reads=[v, pT[i % 6]], writes=[o])
                    if m % 4 == 3 or m == nb - 1:
                        m0 = (m // 4) * 4
                        cntq = (m - m0 + 1) * 128
                        dst = oa[:, cls(r, m0, cntq)]
                        if g == 0:
                            S.op("act", lambda e: e.copy(out=dst, in_=o[:, 0:cntq]), reads=[o], writes=[oa])
                        else:
                            tb = otmp[acc_n[0] % 2]
                            acc_n[0] += 1
                            S.op("act", lambda e: e.copy(out=tb[:, 0:cntq], in_=o[:, 0:cntq]), reads=[o], writes=[tb])
                            S.op("pool", lambda e: e.tensor_tensor(out=dst, in0=tb[:, 0:cntq], in1=dst, op=ALU.add), reads=[tb, oa], writes=[oa])

                for i in range(n + 3):
                    if i < n:
                        stA(i)
                    if 0 <= i - 1 < n:
                        stB(i - 1)
                    if 0 <= i - 3 < n:
                        stC(i - 3)
                if g == 2:
                    for c in range(8):
                        nrm.from_sbuf(oa, c * 512, j * 64, c * 512)

    src_ap, src_bufs = I["x"], [k.dIN] * NT
    for li in range(n_layers):
        kind, j = kinds[li], li // 3
        ffn_phase(li, 0, src_ap, src_bufs, XS, k.xs_t)
        if kind == 0:
            mixA_phase(li, j)
            memattn_phase(li, 768)
            mixout_phase(li, 8)
        elif kind == 1:
            proj_phase(li, 1)
            dil_phase(li)
            memattn_phase(li, 256)
            mixout_phase(li, 4)
        else:
            proj_phase(li, 2)
            fox_phase(li)
            memattn_phase(li, 768)
            mixout_phase(li, 8)
        last = (li == n_layers - 1)
        ffn_phase(li, 1, XS, k.xs_t, out_d if last else XS, [Buf(None, "o%d" % i) for i in range(NT)] if last else k.xs_t)
        src_ap, src_bufs = XS, k.xs_t
        conv_flush(li + 1)

    S.barrier()
    gst.close()
    ctx.__exit__(None, None, None)
    return nc


_CACHE = {}


def kernel(**inputs):
    x = np.ascontiguousarray(np.asarray(inputs["x"], dtype=np.float32))
    mem = np.ascontiguousarray(np.asarray(inputs["mem"], dtype=np.float32))
    pos = np.ascontiguousarray(np.asarray(inputs["positions"], dtype=np.int32))
    B = x.shape[0]
    if "nc" not in _CACHE:
        _CACHE["nc"] = build()
    nc = _CACHE["nc"]
    consts = make_consts()
    shared = {}
    for name in ("norm_g", "mem_norm_g", "w_mem_kv", "ffn_w_gate_up", "ffn_w_down", "a_w_in", "a_spatial_w", "a_spatial_b",
                 "a_v_norm_g", "a_w_out", "b_w_in", "b_w_out", "c_w_in", "c_forget_bias", "c_w_out"):
        shared[name] = np.ascontiguousarray(np.asarray(inputs[name], dtype=np.float32))
    in_maps = []
    for b in range(B):
        m = dict(shared)
        m["x"] = x[b]
        m["mem"] = mem[b]
        m["positions"] = pos[b:b + 1]
        m["consts"] = consts
        in_maps.append(m)
    res = run_bass_kernel_spmd(nc, in_maps, core_ids=list(range(B)))
    return np.stack([np.asarray(r["out"], dtype=np.float32) for r in res.results], axis=0)
```
